# Optimizing a Trainium2 kernel written in Bass

```python
import math
import jax
import jax.numpy as jnp
from jax import lax
import numpy as np

D_MODEL = 1024
BATCH = 16
SEQ = 4096
DEPTH = 2
DEC_BATCH = 8
DEC_SEQ = 64
PAST_LEN = 4096

CHUNK = 64
MIX_WIDTH = D_MODEL
GDN_HEADS = 4
GDN_HEAD_DIM = 128
GDN_WIDTH = GDN_HEADS * GDN_HEAD_DIM
QKV_WIDTH = 3 * GDN_WIDTH
CONV_W = 4
SMLP_GROUPS = 4
SMLP_WIDTH = MIX_WIDTH - GDN_WIDTH
SMLP_GROUP_DIM = SMLP_WIDTH // SMLP_GROUPS
SMLP_CHUNK = 128
D_FF = 4 * D_MODEL
IN_WIDTH = QKV_WIDTH + GDN_WIDTH + 2 * GDN_HEADS + 2 * SMLP_WIDTH
SPLITS = (QKV_WIDTH,
          QKV_WIDTH + GDN_WIDTH,
          QKV_WIDTH + GDN_WIDTH + GDN_HEADS,
          QKV_WIDTH + GDN_WIDTH + 2 * GDN_HEADS,
          QKV_WIDTH + GDN_WIDTH + 2 * GDN_HEADS + SMLP_WIDTH)
DN_ALPHA = (2 * DEPTH) ** 0.25
DN_BETA = (8 * DEPTH) ** -0.25
LN_EPS = 1e-5
NORM_EPS = 1e-6

kernel_name = 'hybrid_gdn_smlp_stream_step'


def layer_norm(x, g, b):
    xf = x.astype(jnp.float32)
    mu = xf.mean(-1, keepdims=True)
    var = jnp.square(xf - mu).mean(-1, keepdims=True)
    y = (xf - mu) * lax.rsqrt(var + LN_EPS) * g.astype(jnp.float32) + b.astype(jnp.float32)
    return y.astype(x.dtype)


def l2norm(x):
    return x * lax.rsqrt(jnp.sum(jnp.square(x), -1, keepdims=True) + NORM_EPS)


def causal_conv_silu(x, tail, w):
    t = x.shape[1]
    xc = jnp.concatenate([tail.astype(x.dtype), x], axis=1)
    y = sum(xc[:, j:j + t] * w[j] for j in range(CONV_W))
    return jax.nn.silu(y), xc[:, -(CONV_W - 1):]


def gated_delta_rule(q, k, v, g, beta, s0):
    bsz, t, h, dk = q.shape
    dv = v.shape[-1]
    n = -(-t // CHUNK)
    pad = n * CHUNK - t

    def to_blocks(a):
        a = jnp.pad(a, [(0, 0), (0, pad)] + [(0, 0)] * (a.ndim - 2))
        a = a.reshape((bsz, n, CHUNK) + a.shape[2:])
        return jnp.moveaxis(a, 3, 1)

    qc, kc, vc, gc, bc = (to_blocks(a) for a in (q, k, v, g, beta))
    gcum = jnp.cumsum(gc, axis=-1)
    causal = jnp.tril(jnp.ones((CHUNK, CHUNK), bool))
    strict = jnp.tril(jnp.ones((CHUNK, CHUNK), bool), -1)
    decay = jnp.exp(jnp.where(causal, gcum[..., :, None] - gcum[..., None, :], -jnp.inf))
    kb = kc * bc[..., None]
    a_mat = jnp.where(strict, jnp.einsum('bhncd,bhnsd->bhncs', kb, kc) * decay, 0.0)
    eye = jnp.eye(CHUNK, dtype=jnp.float32)
    rhs = jnp.concatenate([vc * bc[..., None], kb * jnp.exp(gcum)[..., None]], axis=-1)
    sol = lax.linalg.triangular_solve(a_mat + eye, rhs, left_side=True, lower=True,
                                      unit_diagonal=True)
    u_blk, w_blk = sol[..., :dv], sol[..., dv:]
    attn = jnp.einsum('bhncd,bhnsd->bhncs', qc, kc) * decay
    k_dec = kc * jnp.exp(gcum[..., -1:] - gcum)[..., None]
    q_dec = qc * jnp.exp(gcum)[..., None]
    g_last = jnp.exp(gcum[..., -1])
    xs = tuple(jnp.moveaxis(a, 2, 0) for a in (q_dec, k_dec, u_blk, w_blk, attn, g_last))

    def step(s, inp):
        qd, kd, ui, wi, at, gl = inp
        v_new = ui - jnp.einsum('bhcd,bhde->bhce', wi, s)
        o = jnp.einsum('bhcd,bhde->bhce', qd, s) + jnp.einsum('bhcs,bhse->bhce', at, v_new)
        s = s * gl[..., None, None] + jnp.einsum('bhcd,bhce->bhde', kd, v_new)
        return s, o

    s_fin, o = lax.scan(step, s0, xs)
    o = jnp.moveaxis(o, 0, 2).reshape(bsz, h, n * CHUNK, dv)[:, :, :t]
    return jnp.moveaxis(o, 1, 2), s_fin


def gdn_mixer(qkv, z, b_logit, a_logit, conv_tail, s0, conv_w, a_log, dt_bias, norm_g):
    bsz, t, _ = qkv.shape
    f32 = jnp.float32
    qkv_c, new_tail = causal_conv_silu(qkv, conv_tail, conv_w)
    q, k, v = jnp.split(qkv_c.astype(f32), 3, axis=-1)
    shp = (bsz, t, GDN_HEADS, GDN_HEAD_DIM)
    q = l2norm(q.reshape(shp)) * (GDN_HEAD_DIM ** -0.5)
    k = l2norm(k.reshape(shp))
    v = v.reshape(shp)
    beta = jax.nn.sigmoid(b_logit.astype(f32))
    g = -jnp.exp(a_log.astype(f32)) * jax.nn.softplus(a_logit.astype(f32) + dt_bias.astype(f32))
    o, s_fin = gated_delta_rule(q, k, v, g, beta, s0.astype(f32))
    o = o * lax.rsqrt(jnp.mean(jnp.square(o), -1, keepdims=True) + NORM_EPS)
    o = o * norm_g.astype(f32) * jax.nn.silu(z.astype(f32).reshape(shp))
    return o.reshape(bsz, t, GDN_WIDTH).astype(qkv.dtype), new_tail, s_fin.astype(s0.dtype)


def spatial_gating(u, v, ln_g, ln_b, w_s, b_s):
    bsz, t, _ = u.shape
    u = jax.nn.gelu(u, approximate=False)
    v = layer_norm(jax.nn.gelu(v, approximate=False), ln_g, ln_b)
    n = -(-t // SMLP_CHUNK)
    pad = n * SMLP_CHUNK - t
    vc = jnp.pad(v, ((0, 0), (0, pad), (0, 0))).reshape(bsz, n, SMLP_CHUNK, SMLP_GROUPS, SMLP_GROUP_DIM)
    w = jnp.where(jnp.tril(jnp.ones((SMLP_CHUNK, SMLP_CHUNK), bool)), w_s, 0.0)
    s = jnp.einsum('gpq,bnqgc->bnpgc', w, vc) + jnp.transpose(b_s)[None, None, :, :, None]
    s = s.reshape(bsz, n * SMLP_CHUNK, SMLP_WIDTH)[:, :t]
    return u * s, v


def trunk_layer(x, conv_tail, s0, w_in, conv_w, a_log, dt_bias, gdn_norm_g, smlp_ln_g, smlp_ln_b,
                w_s, b_s, w_out, ln1_g, ln1_b, w_up, w_down, ln2_g, ln2_b):
    h = jnp.einsum('btd,de->bte', x, w_in)
    qkv, z, b_logit, a_logit, u, v = jnp.split(h, SPLITS, axis=-1)
    y_a, new_tail, new_s = gdn_mixer(qkv, z, b_logit, a_logit, conv_tail, s0,
                                     conv_w, a_log, dt_bias, gdn_norm_g)
    y_b, v_rows = spatial_gating(u, v, smlp_ln_g, smlp_ln_b, w_s, b_s)
    mix = jnp.einsum('btm,md->btd', jnp.concatenate([y_a, y_b], axis=-1), w_out)
    x = layer_norm(DN_ALPHA * x + mix, ln1_g, ln1_b)
    hid = jnp.square(jax.nn.relu(jnp.einsum('btd,df->btf', x, w_up)))
    x = layer_norm(DN_ALPHA * x + jnp.einsum('btf,fd->btd', hid, w_down), ln2_g, ln2_b)
    return x, new_tail, new_s, v_rows


def setup_inputs(seed: int = 0) -> dict:
    key = jax.random.key(seed)
    ks = jax.random.split(key, 24)
    f32 = jnp.float32

    def nrm(k, shape, scale):
        return jax.random.normal(k, shape, f32) * scale

    dt = jnp.exp(jax.random.uniform(ks[9], (DEPTH, GDN_HEADS), f32, math.log(1e-3), math.log(1e-1)))
    return {
        'x_prompt': nrm(ks[0], (BATCH, SEQ, D_MODEL), 1.0),
        'x_sample': nrm(ks[1], (DEC_BATCH, DEC_SEQ, D_MODEL), 1.0),
        'cache_conv': nrm(ks[2], (DEPTH, DEC_BATCH, CONV_W - 1, QKV_WIDTH), 1.0),
        'state_delta': nrm(ks[3], (DEPTH, DEC_BATCH, GDN_HEADS, GDN_HEAD_DIM, GDN_HEAD_DIM), 0.1),
        'ln0_g': 1.0 + nrm(ks[4], (D_MODEL,), 0.05),
        'ln0_b': nrm(ks[5], (D_MODEL,), 0.02),
        'w_in': nrm(ks[6], (DEPTH, D_MODEL, IN_WIDTH), D_MODEL ** -0.5),
        'conv_w': nrm(ks[7], (DEPTH, CONV_W, QKV_WIDTH), CONV_W ** -0.5),
        'a_log': jnp.log(jax.random.uniform(ks[8], (DEPTH, GDN_HEADS), f32, 1.0, 16.0)),
        'dt_bias': dt + jnp.log(-jnp.expm1(-dt)),
        'gdn_norm_g': 1.0 + nrm(ks[10], (DEPTH, GDN_HEAD_DIM), 0.05),
        'smlp_ln_g': 1.0 + nrm(ks[11], (DEPTH, SMLP_WIDTH), 0.05),
        'smlp_ln_b': nrm(ks[12], (DEPTH, SMLP_WIDTH), 0.02),
        'w_s': nrm(ks[13], (DEPTH, SMLP_GROUPS, SMLP_CHUNK, SMLP_CHUNK), 0.5 * SMLP_CHUNK ** -0.5),
        'b_s': 1.0 + nrm(ks[14], (DEPTH, SMLP_GROUPS, SMLP_CHUNK), 0.01),
        'w_out': nrm(ks[15], (DEPTH, MIX_WIDTH, D_MODEL), DN_BETA * MIX_WIDTH ** -0.5),
        'ln1_g': 1.0 + nrm(ks[16], (DEPTH, D_MODEL), 0.05),
        'ln1_b': nrm(ks[17], (DEPTH, D_MODEL), 0.02),
        'w_up': nrm(ks[18], (DEPTH, D_MODEL, D_FF), D_MODEL ** -0.5),
        'w_down': nrm(ks[19], (DEPTH, D_FF, D_MODEL), DN_BETA * D_FF ** -0.5),
        'ln2_g': 1.0 + nrm(ks[20], (DEPTH, D_MODEL), 0.05),
        'ln2_b': nrm(ks[21], (DEPTH, D_MODEL), 0.02),
    }


def reference(x_prompt, x_sample, cache_conv, state_delta, ln0_g, ln0_b, w_in, conv_w, a_log,
              dt_bias, gdn_norm_g, smlp_ln_g, smlp_ln_b, w_s, b_s, w_out, ln1_g, ln1_b,
              w_up, w_down, ln2_g, ln2_b):
    xp = layer_norm(x_prompt, ln0_g, ln0_b)
    xs = layer_norm(x_sample, ln0_g, ln0_b)
    tail0 = jnp.zeros((x_prompt.shape[0], CONV_W - 1, QKV_WIDTH), x_prompt.dtype)
    s_zero = jnp.zeros((x_prompt.shape[0], GDN_HEADS, GDN_HEAD_DIM, GDN_HEAD_DIM), x_prompt.dtype)
    conv_p, delta_p, conv_s, delta_s, v_s = [], [], [], [], []
    for l in range(DEPTH):
        lw = (w_in[l], conv_w[l], a_log[l], dt_bias[l], gdn_norm_g[l], smlp_ln_g[l], smlp_ln_b[l],
              w_s[l], b_s[l], w_out[l], ln1_g[l], ln1_b[l], w_up[l], w_down[l], ln2_g[l], ln2_b[l])
        xp, tp, sp, _ = trunk_layer(xp, tail0, s_zero, *lw)
        xs, tsm, ssm, vsm = trunk_layer(xs, cache_conv[l], state_delta[l], *lw)
        conv_p.append(tp)
        delta_p.append(sp)
        conv_s.append(tsm)
        delta_s.append(ssm)
        v_s.append(vsm)
    return (xp, xs, jnp.stack(conv_p), jnp.stack(delta_p), jnp.stack(conv_s), jnp.stack(delta_s), jnp.stack(v_s))
```

```python
import numpy as np
import concourse.bass as bass
import concourse.mybir as mybir
from concourse.bass_utils import run_bass_kernel_spmd

F32 = mybir.dt.float32
BF16 = mybir.dt.bfloat16
AF = mybir.ActivationFunctionType
ALU = mybir.AluOpType

NL = 2
D = 1024
INW = 3080
ALPHA = float((2 * NL) ** 0.25)
LN_EPS = 1e-5
NEPS = 1e-6
NEG = -30000.0
NWP = 5
NBIG = 5
LPC = 81
LBC = 1056
NCONST = 1152


class Buf:
    __slots__ = ("w", "r", "name")

    def __init__(self, name=""):
        self.w = None
        self.r = {}
        self.name = name


class TK:
    def __init__(self, nc):
        self.nc = nc
        self.E = {"pe": nc.tensor, "act": nc.scalar, "dve": nc.vector, "pool": nc.gpsimd, "sp": nc.sync}
        self.sem = {}
        self.cnt = {}
        self.waited = {e: {} for e in self.E}
        for e in self.E:
            self.newsem(e)

    def newsem(self, key):
        self.sem[key] = self.nc.alloc_semaphore(name="s_" + key)
        self.cnt[key] = 0

    def _needs(self, reads, writes):
        nd = {}
        for b in reads:
            if b.w is not None:
                k, v = b.w
                if nd.get(k, 0) < v:
                    nd[k] = v
        for b in writes:
            if b.w is not None:
                k, v = b.w
                if nd.get(k, 0) < v:
                    nd[k] = v
            for k, v in b.r.items():
                if nd.get(k, 0) < v:
                    nd[k] = v
        return nd

    def _wait(self, e, nd):
        w = self.waited[e]
        for k, v in nd.items():
            if e == "pe" and k == "pe":
                continue
            if w.get(k, 0) < v:
                self.E[e].wait_ge(self.sem[k], v)
                w[k] = v

    def op(self, e, fn, reads=(), writes=(), sig=True):
        self._wait(e, self._needs(reads, writes))
        ins = fn(self.E[e])
        if sig:
            self.cnt[e] += 1
            ins.then_inc(self.sem[e], 1)
            c = self.cnt[e]
        else:
            c = self.cnt[e] + 1
        for b in reads:
            if b.r.get(e, 0) < c:
                b.r[e] = c
        for b in writes:
            b.w = (e, c)
            b.r = {}

    def dma(self, q, out, in_, key, reads=(), writes=()):
        self._wait(q, self._needs(reads, writes))
        self.E[q].dma_start(out=out, in_=in_).then_inc(self.sem[key], 16)
        self.cnt[key] += 16
        c = self.cnt[key]
        for b in reads:
            if b.r.get(key, 0) < c:
                b.r[key] = c
        for b in writes:
            b.w = (key, c)
            b.r = {}

    def wait_all(self, e, key):
        v = self.cnt[key]
        if v > 0 and self.waited[e].get(key, 0) < v:
            self.E[e].wait_ge(self.sem[key], v)
            self.waited[e][key] = v


class PSB:
    def __init__(self, t, name):
        self.t = t
        self.whole = Buf(name)
        self.q = [self.whole] * 4

    def b(self, c0=0, c1=512):
        return [self.whole]


class _Stop(Exception):
    pass


def build(TP, NSP=2, stop=None):
    nc = bass.Bass("TRN2", target_bir_lowering=False)
    dt = nc.dram_tensor
    xp = dt("xp", [NSP, TP, D], F32, kind="ExternalInput").ap()
    xs = dt("xs", [64, D], F32, kind="ExternalInput").ap()
    cconv = dt("cconv", [NL, 3, 1536], F32, kind="ExternalInput").ap()
    sdelta = dt("sdelta", [NL, 4, 128, 128], F32, kind="ExternalInput").ap()
    w_in = dt("w_in", [NL, D, INW], F32, kind="ExternalInput").ap()
    w_out = dt("w_out", [NL, D, D], F32, kind="ExternalInput").ap()
    w_up = dt("w_up", [NL, D, 4096], F32, kind="ExternalInput").ap()
    w_down = dt("w_down", [NL, 4096, D], F32, kind="ExternalInput").ap()
    ws_d = dt("ws", [NL * 4, 128, 128], F32, kind="ExternalInput").ap()
    consts_d = dt("consts", [128, NCONST], F32, kind="ExternalInput").ap()
    pcol_d = dt("pcol", [128, 16 + NL * LPC], F32, kind="ExternalInput").ap()
    pbc_d = dt("pbc", [128, NL * LBC], F32, kind="ExternalInput").ap()
    bsrow_d = dt("bsrow", [1, NL * 4 * 128], F32, kind="ExternalInput").ap()

    yp = dt("yp", [NSP, TP, D], F32, kind="ExternalOutput").ap()
    ys = dt("ys", [64, D], F32, kind="ExternalOutput").ap()
    ncp = dt("ncp", [NL, NSP, 3, 1536], F32, kind="ExternalOutput").ap()
    ndp = dt("ndp", [NL, NSP, 4, 128, 128], F32, kind="ExternalOutput").ap()
    ncs = dt("ncs", [NL, 3, 1536], F32, kind="ExternalOutput").ap()
    nds = dt("nds", [NL, 4, 128, 128], F32, kind="ExternalOutput").ap()
    nvs = dt("nvs", [NL, 64, 512], F32, kind="ExternalOutput").ap()

    wsc = dt("wsc", [NL, 24, 128, 4096], BF16, kind="Internal").ap()
    wbasc = dt("wbasc", [NL, 128, 64], BF16, kind="Internal").ap()

    tk = TK(nc)
    _n = [0]

    def sb(shape, dtype, name=None):
        _n[0] += 1
        return nc.alloc_sbuf_tensor("sb_" + (name or ("t%d" % _n[0])), shape, dtype)

    cst = sb([128, NCONST], F32, "cst")
    pcol = sb([128, 16 + NL * LPC], F32, "pcol")
    pbc = sb([128, NL * LBC], F32, "pbc")
    epsc = sb([128, 4], F32, "epsc")
    identb = sb([128, 128], BF16, "identb")
    onesb = sb([128, 128], BF16, "onesb")
    ones128b = sb([128, 128], BF16, "ones128b")
    ones1kb = sb([128, 128], BF16, "ones1kb")
    onesrmsb = sb([128, 128], BF16, "onesrmsb")
    WsT = sb([128, 8, 128], BF16, "WsT")
    bsmat = sb([128, 8, 128], BF16, "bsmat")
    nexpA = sb([128, NL, 16], F32, "nexpA")
    wba = sb([128, NL, 64], BF16, "wba")
    wpool = [sb([128, 4096], BF16, "wp%d" % i) for i in range(NWP)]
    wpool_b = [Buf("wp%d" % i) for i in range(NWP)]
    xin = [sb([128, 1024], F32, "xin%d" % i) for i in range(2)]
    xin_b = [Buf("xin%d" % i) for i in range(2)]
    xT = sb([128, 8, 512], F32, "xT")
    xT_b = [Buf("xT%d" % i) for i in range(8)]
    xTb = sb([128, 8, 512], BF16, "xTb")
    xTb_b = [Buf("xTb%d" % i) for i in range(8)]
    qkv_pre = sb([128, 12, 515], BF16, "qkv_pre")
    qp_b = [Buf("qp%d" % i) for i in range(12)]
    tails = sb([128, NL, 12, 3], BF16, "tails")
    tails_b = [Buf("tails%d" % i) for i in range(NL)]
    diag = [sb([128, 4, 128], BF16, "diag%d" % i) for i in range(2)]
    diag_b = [Buf("diag%d" % i) for i in range(2)]
    hid = sb([128, 32, 512], BF16, "hid")
    hid_b = [Buf("hid%d" % i) for i in range(32)]
    vg = [sb([128, 512], F32, "vg%d" % i) for i in range(2)]
    vg_b = [Buf("vg%d" % i) for i in range(2)]
    sF = [sb([128, 512], F32, "sF%d" % i) for i in range(4)]
    sF_b = [Buf("sF%d" % i) for i in range(4)]
    sB = [sb([128, 512], BF16, "sB%d" % i) for i in range(2)]
    sB_b = [Buf("sB%d" % i) for i in range(2)]
    rstd_t = sb([128, 512], F32, "rstd")
    nmr_t = sb([128, 512], F32, "nmr")
    ln_b = Buf("lnstat")
    S32 = sb([128, NL, 4, 128], F32, "S32")
    Sbf = sb([128, NL, 4, 128], BF16, "Sbf")
    S32_b = [[Buf("S32_%d%d" % (l, h)) for h in range(4)] for l in range(NL)]
    Sbf_b = [[Buf("Sbf_%d%d" % (l, h)) for h in range(4)] for l in range(NL)]
    gnames = ["beta", "g", "sbn", "gc", "ngc", "gcb", "bEg", "Ed0", "Ed1", "eg", "tmpa", "tmpb"]
    gt = {n: sb([128, 16], F32, "g_" + n) for n in gnames}
    egl = sb([128, 32], F32, "g_egl")
    gates_b = Buf("gates")
    mv = sb([128, 16], F32, "mv")
    st6 = sb([128, 12], F32, "st6")
    mv_b = Buf("mv")
    hd = []
    for h in range(4):
        d_ = {}
        for n in ["attnT", "P", "kbg", "kd0", "kd1", "vb", "nwT", "qdT", "vnew", "EG"]:
            d_[n] = sb([128, 128], BF16, "h%d_%s" % (h, n))
            d_[n + "_b"] = Buf("h%d_%s" % (h, n))
        d_["YX"] = [sb([128, 256], BF16, "h%d_YX%d" % (h, i)) for i in range(2)]
        d_["YX_b"] = [Buf("h%d_YX%d" % (h, i)) for i in range(2)]
        hd.append(d_)
    dgp = [sb([128, 256], F32, "dg%d" % i) for i in range(2)]
    dgp_b = [Buf("dg%d" % i) for i in range(2)]
    tmpp = [sb([128, 256], F32, "tmpE%d" % i) for i in range(2)]
    tmpp_b = [Buf("tmpE%d" % i) for i in range(2)]
    Ep = [sb([128, 256], F32, "E%d" % i) for i in range(2)]
    Ep_b = [Buf("E%d" % i) for i in range(2)]

    big = [PSB(nc.alloc_psum_tensor("pbig%d" % i, [128, 512], F32), "pbig%d" % i) for i in range(5)]
    psO = PSB(nc.alloc_psum_tensor("psO", [128, 512], F32), "psO")
    pT = [nc.alloc_psum_tensor("pT%d" % i, [128, 512], F32) for i in range(2)]
    pT_b = [Buf("pT%d" % i) for i in range(2)]
    rot = {"big": 0, "small": 0, "pT": 0, "sF": 0, "sB": 0, "vg": 0, "dg": 0, "tmp": 0, "E": 0, "diag": 0}

    def nxt(key, n):
        i = rot[key]
        rot[key] = (i + 1) % n
        return i

    def bigps():
        return big[nxt("big", NBIG)]

    def smallq():
        p_ = bigps()
        return p_.t[:, 0:128], p_.b()

    def pTbank():
        i = nxt("pT", 2)
        return pT[i], [pT_b[i]]

    def sFn():
        i = nxt("sF", 4)
        return sF[i], [sF_b[i]]

    def sBn():
        i = nxt("sB", 2)
        return sB[i], [sB_b[i]]

    def mm(out, lhsT, rhs, start, stop, reads, writes, sig=True):
        tk.op("pe", lambda e: e.matmul(out, lhsT=lhsT, rhs=rhs, start=start, stop=stop), reads, writes, sig)

    def act(out, in_, func, reads, writes, bias=None, scale=None):
        kw = {}
        if bias is not None:
            kw["bias"] = bias
        if scale is not None:
            kw["scale"] = scale
        tk.op("act", lambda e: e.activation(out=out, in_=in_, func=func, **kw), reads, writes)

    def ts(eng, out, in0, s1, s2, op0, op1, reads, writes):
        tk.op(eng, lambda e: e.tensor_scalar(out=out, in0=in0, scalar1=s1, scalar2=s2, op0=op0, op1=op1), reads, writes)

    def rsq(out, in0, epscol, reads, writes):
        act(out, in0, AF.Sqrt, reads, writes, bias=epsc[:, epscol:epscol + 1])
        tk.op("dve", lambda e: e.reciprocal(out=out, in_=out), writes, writes)

    def ts1(eng, out, in0, s1, op0, reads, writes):
        tk.op(eng, lambda e: e.tensor_single_scalar(out=out, in_=in0, scalar=s1, op=op0), reads, writes)

    def stt(eng, out, in0, s, in1, op0, op1, reads, writes):
        tk.op(eng, lambda e: e.scalar_tensor_tensor(out=out, in0=in0, scalar=s, in1=in1, op0=op0, op1=op1), reads, writes)

    def tt(eng, out, in0, in1, op, reads, writes):
        tk.op(eng, lambda e: e.tensor_tensor(out=out, in0=in0, in1=in1, op=op), reads, writes)

    def cp(eng, out, in_, reads, writes):
        if eng == "act":
            act(out, in_, AF.Copy, reads, writes)
        else:
            tk.op(eng, lambda e: e.tensor_copy(out=out, in_=in_), reads, writes)

    def mset(eng, ap, v, writes):
        tk.op(eng, lambda e: e.memset(ap, v), (), writes)

    ident_f = cst[:, 0:128]
    ones_f = cst[:, 128:256]
    Mu_f = cst[:, 256:384]
    Bd_f = cst[:, 384:512]
    E0_f = cst[:, 512:640]
    E1_f = cst[:, 640:768]
    maskneg2 = cst[:, 768:1024]
    triU = cst[:, 1024:1152]

    sc_b = [[Buf("sc%d_%d" % (l, g)) for g in range(4)] for l in range(NL)]
    win_cols = [0, 512, 1024, 1536, 2056, 2568]
    for l in range(NL):
        for g in range(4):
            tk.newsem("cv%d_%d" % (l, g))
    tk.newsem("cvba")
    wba_sc_b = Buf("wbasc")
    for l in range(NL if not _NOCAST[0] else 0):
        for pi, c0 in enumerate(win_cols):
            tk.dma("pool", wsc[l, pi].rearrange("p (k c) -> p k c", k=8),
                   w_in[l][:, c0:c0 + 512].rearrange("(k p) c -> p k c", p=128), "cv%d_0" % l, (), [sc_b[l][0]])
            tk.wait_all("pool", "cv%d_0" % l)
        tk.dma("pool", wbasc[l].rearrange("p (k c) -> p k c", k=8),
               w_in[l][:, 2048:2056].rearrange("(k p) c -> p k c", p=128), "cvba", (), [wba_sc_b])
        tk.wait_all("pool", "cvba")
        for j in range(2):
            tk.dma("pool", wsc[l, 6 + j].rearrange("p (k c) -> p k c", k=8),
                   w_out[l][:, j * 512:(j + 1) * 512].rearrange("(k p) c -> p k c", p=128), "cv%d_1" % l, (), [sc_b[l][1]])
            tk.wait_all("pool", "cv%d_1" % l)
        for j in range(8):
            tk.dma("pool", wsc[l, 8 + j].rearrange("p (k c) -> p k c", k=8),
                   w_up[l][:, j * 512:(j + 1) * 512].rearrange("(k p) c -> p k c", p=128), "cv%d_2" % l, (), [sc_b[l][2]])
            tk.wait_all("pool", "cv%d_2" % l)
        for m in range(8):
            tk.dma("pool", wsc[l, 16 + m].rearrange("p (f c) -> p f c", f=32),
                   w_down[l][:, m * 128:(m + 1) * 128].rearrange("(f p) c -> p f c", p=128), "cv%d_3" % l, (), [sc_b[l][3]])
            tk.wait_all("pool", "cv%d_3" % l)

    tk.newsem("const")
    cbuf = Buf("const")
    tk.dma("sp", cst[:], consts_d, "const", (), [cbuf])
    tk.dma("sp", pcol[:], pcol_d, "const", (), [cbuf])
    tk.dma("sp", pbc[:], pbc_d, "const", (), [cbuf])
    tk.dma("sp", sF[0][0:1, :], bsrow_d[:, 0:512], "const", (), [cbuf])
    tk.dma("sp", sF[1][0:1, :], bsrow_d[:, 512:1024], "const", (), [cbuf])
    tk.dma("sp", xin[0][:].rearrange("p (a q) -> p a q", a=8), ws_d.rearrange("a p q -> p a q"), "const", (), [cbuf, xin_b[0]])
    tk.dma("sp", wba[:], wbasc.rearrange("l p c -> p l c"), "const", [wba_sc_b], [cbuf])
    for e in ("pe", "act", "dve", "pool"):
        tk.wait_all(e, "const")

    setup_b = Buf("setup")
    cp("dve", identb[:], ident_f, (), [setup_b])
    mset("dve", onesb[:], 1.0, [setup_b])
    mset("dve", epsc[:, 0:1], LN_EPS, [setup_b])
    mset("dve", epsc[:, 1:2], NEPS, [setup_b])
    mset("dve", epsc[:, 2:3], 128.0 * NEPS, [setup_b])
    mset("dve", epsc[:, 3:4], 1.0, [setup_b])
    mset("dve", ones128b[:], 128.0, [setup_b])
    mset("dve", ones1kb[:], 1.0 / 1024.0, [setup_b])
    mset("dve", onesrmsb[:], 1.0 / 128.0, [setup_b])
    mset("dve", bsmat[:], 0.0, [setup_b])
    for a in range(8):
        src = sF[a // 4][0:1, (a % 4) * 128:(a % 4 + 1) * 128]
        cp("dve", bsmat[0:1, a, :], src, [setup_b], [setup_b])
    for a in range(8):
        pq, pqb = smallq()
        mm(pq, xin[0][:, a * 128:(a + 1) * 128], ident_f, True, True, [setup_b, xin_b[0]], pqb)
        tt("dve", WsT[:, a, :], pq, triU, ALU.mult, pqb, [setup_b])
    for l in range(NL):
        act(nexpA[:, l, :], pbc[:, l * LBC + 1024:l * LBC + 1040], AF.Exp, [setup_b], [setup_b])
        ts1("dve", nexpA[:, l, :], nexpA[:, l, :], -1.0, ALU.mult, [setup_b], [setup_b])
    for e in ("pe", "act", "pool", "dve"):
        tk._wait(e, {"dve": tk.cnt["dve"], "act": tk.cnt["act"], "pe": tk.cnt["pe"]})

    if _DELAY[0]:
        for _i in range(_DELAY[0]):
            tk.op("act", lambda e: e.activation(out=vg[0][:, 0:512], in_=vg[1][:, 0:512], func=AF.Copy), (), [vg_b[0]])
    NTP = TP // 512
    tiles = []
    for s in range(NSP):
        for ti in range(NTP):
            tiles.append((s, ti, 512))
    tiles.append((NSP, 0, 128))
    pieces = []
    for (_s, _ti, _n) in tiles:
        for l in range(NL):
            for pi in range(24):
                g = 0 if pi < 6 else (1 if pi < 8 else (2 if pi < 16 else 3))
                pieces.append((l, pi, g))
    for i in range(NWP):
        tk.newsem("wp%d" % i)
    wst = {"i": 0, "loaded": 0}

    def wnext():
        while wst["loaded"] < min(len(pieces), wst["i"] + NWP):
            j = wst["loaded"]
            l, pi, g = pieces[j]
            slot = j % NWP
            tk.dma("sp", wpool[slot][:], wsc[l, pi], "wp%d" % slot, [sc_b[l][g]], [wpool_b[slot]])
            wst["loaded"] += 1
        j = wst["i"]
        wst["i"] += 1
        return wpool[j % NWP], [wpool_b[j % NWP]]

    tk.newsem("xin0")
    tk.newsem("xin1")
    tk.newsem("sty0")
    tk.newsem("sty1")
    tk.newsem("misc")

    def misc_store(out, in_, reads):
        tk.dma("sp", out, in_, "misc", reads, ())
        tk.wait_all("sp", "misc")

    def misc_load(out, in_, writes):
        tk.dma("sp", out, in_, "misc", (), writes)
        tk.wait_all("sp", "misc")

    def barrier():
        snap = {k: tk.cnt[k] for k in ("pe", "act", "dve", "pool")}
        for e in ("pe", "act", "dve", "pool"):
            tk._wait(e, {k: v for k, v in snap.items() if v > 0})

    def ln_fm(N, gcol, bcol):
        psM = bigps()
        psQ = bigps()
        for m in range(8):
            cp("act", xTb[:, m, 0:N], xT[:, m, 0:N], [xT_b[m]], [xTb_b[m]])
            sq, sqb = sBn()
            act(sq[:, 0:N], xT[:, m, 0:N], AF.Square, [xT_b[m]], sqb)
            mm(psM.t[:, 0:N], ones1kb[:], xTb[:, m, 0:N], m == 0, m == 7, [xTb_b[m]], psM.b(), sig=(m == 7))
            mm(psQ.t[:, 0:N], ones1kb[:], sq[:, 0:N], m == 0, m == 7, sqb, psQ.b(), sig=True)
        t1, t1b = sFn()
        act(t1[:, 0:N], psM.t[:, 0:N], AF.Square, psM.b(), t1b)
        stt("dve", t1[:, 0:N], t1[:, 0:N], -1.0, psQ.t[:, 0:N], ALU.mult, ALU.add, t1b + psQ.b(), t1b)
        rsq(rstd_t[:, 0:N], t1[:, 0:N], 0, t1b, [ln_b])
        stt("dve", nmr_t[:, 0:N], psM.t[:, 0:N], -1.0, rstd_t[:, 0:N], ALU.mult, ALU.mult, psM.b() + [ln_b], [ln_b])
        for m in range(8):
            t, tb = sFn()
            tt("dve", t[:, 0:N], xT[:, m, 0:N], rstd_t[:, 0:N], ALU.mult, [xT_b[m], ln_b], tb)
            tt("pool", t[:, 0:N], t[:, 0:N], nmr_t[:, 0:N], ALU.add, tb + [ln_b], tb)
            act(xT[:, m, 0:N], t[:, 0:N], AF.Identity, tb, [xT_b[m]],
                bias=pcol[:, bcol + m:bcol + m + 1], scale=pcol[:, gcol + m:gcol + m + 1])
            cp("pool", xTb[:, m, 0:N], xT[:, m, 0:N], [xT_b[m]], [xTb_b[m]])

    cur_seq = -1
    tile_no = -1

    def chk(l, name):
        if stop is not None and stop == (tile_no, l, name):
            raise _Stop()

    try:
      if stop == "setup":
          raise _Stop()
      for (s, ti, N) in tiles:
          tile_no += 1
          NB = N // 128
          is_sample = (s == NSP)
          n_real = 64 if is_sample else N
          last_tile = is_sample or (ti == NTP - 1)
          if s != cur_seq:
              cur_seq = s
              if not is_sample:
                  for l in range(NL):
                      mset("pool", S32[:, l], 0.0, S32_b[l])
                      mset("pool", Sbf[:, l], 0.0, Sbf_b[l])
                      mset("pool", tails[:, l], 0.0, [tails_b[l]])
              else:
                  for l in range(NL):
                      misc_load(S32[:, l], sdelta[l].rearrange("h d e -> d h e"), S32_b[l])
                      cp("pool", Sbf[:, l], S32[:, l], S32_b[l], Sbf_b[l])
                      pq, pqb = smallq()
                      for cb in range(3):
                          t, tb = sFn()
                          misc_load(t[0:3, :], cconv[l][:, cb * 512:(cb + 1) * 512], tb)
                          for cc in range(4):
                              c = cb * 4 + cc
                              mm(pq[:, c * 3:c * 3 + 3], t[0:3, cc * 128:(cc + 1) * 128], ident_f[0:3, 0:3], True, True, tb, pqb, sig=(cc == 3))
                      cp("dve", tails[:, l].rearrange("p c r -> p (c r)"), pq[:, 0:36], pqb, [tails_b[l]])

          for blk in range(NB):
              slot = nxt("vg", 2)
              xi, xib = xin[slot], [xin_b[slot]]
              if is_sample:
                  mset("pool", xi[:], 0.0, xib)
                  tk.dma("sp", xi[0:64, :], xs, "xin%d" % slot, (), xib)
              else:
                  r0 = ti * 512 + blk * 128
                  tk.dma("sp", xi[:], xp[s, r0:r0 + 128, :], "xin%d" % slot, (), xib)
              for hf in range(2):
                  tk.op("dve", lambda e, hf=hf: e.bn_stats(out=st6[:, hf * 6:(hf + 1) * 6], in_=xi[:, hf * 512:(hf + 1) * 512]), xib, [mv_b])
              tk.op("dve", lambda e: e.bn_aggr(out=mv[:, 0:2], in_=st6[:, 0:12]), [mv_b], [mv_b])
              rsq(mv[:, 2:3], mv[:, 1:2], 0, [mv_b], [mv_b])
              stt("dve", mv[:, 3:4], mv[:, 0:1], -1.0, mv[:, 2:3], ALU.mult, ALU.mult, [mv_b], [mv_b])
              act(xi[:], xi[:], AF.Identity, xib + [mv_b], xib, bias=mv[:, 3:4], scale=mv[:, 2:3])
              for half in range(2):
                  ps = bigps()
                  for cc in range(4):
                      c = half * 4 + cc
                      mm(ps.t[:, cc * 128:(cc + 1) * 128], xi[:, c * 128:(c + 1) * 128], ident_f, True, True, xib, [ps.q[cc]], sig=(cc == 3))
                  for cc in range(4):
                      c = half * 4 + cc
                      act(xT[:, c, blk * 128:(blk + 1) * 128], ps.t[:, cc * 128:(cc + 1) * 128], AF.Identity, [ps.q[cc]], [xT_b[c]],
                          bias=pcol[:, 8 + c:9 + c], scale=pcol[:, c:c + 1])
          for c in range(8):
              cp("pool", xTb[:, c, 0:N], xT[:, c, 0:N], [xT_b[c]], [xTb_b[c]])

          chk(0, "p1")
          for l in range(NL):
              PC = 16 + l * LPC
              BC = l * LBC
              for pc in range(3):
                  wp, wpb = wnext()
                  for mi in range(4):
                      c = pc * 4 + mi
                      ps = bigps()
                      for k in range(8):
                          mm(ps.t[:, 0:N], wp[:, k * 512 + mi * 128:k * 512 + (mi + 1) * 128], xTb[:, k, 0:N],
                             k == 0, k == 7, wpb + [xTb_b[k]], ps.b(), sig=(k == 7))
                      cp("act", qkv_pre[:, c, 3:3 + N], ps.t[:, 0:N], ps.b(), [qp_b[c]])
                  if last_tile:
                      ps = bigps()
                      for k in range(8):
                          mm(ps.t[0:3, 0:512], xTb[:, k, n_real - 3:n_real], wp[:, k * 512:(k + 1) * 512],
                             k == 0, k == 7, wpb + [xTb_b[k]], ps.b(), sig=(k == 7))
                      t, tb = sFn()
                      cp("act", t[0:3, :], ps.t[0:3, 0:512], ps.b(), tb)
                      dst = ncs[l] if is_sample else ncp[l, s]
                      misc_store(dst[:, pc * 512:(pc + 1) * 512], t[0:3, :], tb)
              cp("pool", qkv_pre[:, :, 0:3], tails[:, l], [tails_b[l]], qp_b)
              cp("pool", tails[:, l], qkv_pre[:, :, N:N + 3], qp_b, [tails_b[l]])
              for c in range(12):
                  di = nxt("diag", 2)
                  for j in range(4):
                      ts1("pool", diag[di][:, j, :], identb[:], pcol[:, PC + 32 + c * 4 + j:PC + 33 + c * 4 + j], ALU.mult, (), [diag_b[di]])
                  ps = bigps()
                  for j in range(4):
                      mm(ps.t[:, 0:N], diag[di][:, j, :], qkv_pre[:, c, j:j + N], j == 0, j == 3, [diag_b[di], qp_b[c]], ps.b(), sig=(j == 3))
                  act(hid[:, c, 0:N], ps.t[:, 0:N], AF.Silu, ps.b(), [hid_b[c]])
              wp, wpb = wnext()
              for mi in range(4):
                  ps = bigps()
                  for k in range(8):
                      mm(ps.t[:, 0:N], wp[:, k * 512 + mi * 128:k * 512 + (mi + 1) * 128], xTb[:, k, 0:N],
                         k == 0, k == 7, wpb + [xTb_b[k]], ps.b(), sig=(k == 7))
                  act(hid[:, 12 + mi, 0:N], ps.t[:, 0:N], AF.Silu, ps.b(), [hid_b[12 + mi]])
              pq, pqb = smallq()
              for blk in range(NB):
                  for k in range(8):
                      mm(pq[:, blk * 8:(blk + 1) * 8], xTb[:, k, blk * 128:(blk + 1) * 128], wba[:, l, k * 8:(k + 1) * 8],
                         k == 0, k == 7, [xTb_b[k]], pqb, sig=(k == 7))
              NG = NB * 4
              pq3 = pq[:, 0:NB * 8].rearrange("p (b c) -> p b c", c=8)

              def g3(name):
                  return gt[name][:, 0:NG].rearrange("p (b c) -> p b c", c=4)
              gb = [gates_b]
              act(g3("tmpa"), pq3[:, :, 0:4], AF.Exp, pqb, gb, scale=-1.0)
              act(g3("sbn"), g3("tmpa"), AF.Ln, gb, gb, bias=epsc[:, 3:4])
              ts1("dve", gt["tmpa"][:, 0:NG], gt["tmpa"][:, 0:NG], 1.0, ALU.add, gb, gb)
              tk.op("dve", lambda e: e.reciprocal(out=gt["beta"][:, 0:NG], in_=gt["tmpa"][:, 0:NG]), gb, gb)
              tt("dve", g3("tmpb"), pq3[:, :, 4:8], pbc[:, BC + 1040:BC + 1040 + NG].rearrange("p (b c) -> p b c", c=4), ALU.add, pqb + gb, gb)
              act(gt["tmpb"][:, 0:NG], gt["tmpb"][:, 0:NG], AF.Exp, gb, gb)
              act(gt["tmpb"][:, 0:NG], gt["tmpb"][:, 0:NG], AF.Ln, gb, gb, bias=epsc[:, 3:4])
              tt("dve", gt["g"][:, 0:NG], gt["tmpb"][:, 0:NG], nexpA[:, l, 0:NG], ALU.mult, gb, gb)
              pg, pgb = smallq()
              mm(pg[:, 0:NG], Mu_f, gt["g"][:, 0:NG], True, True, gb, pgb, sig=False)
              mm(pg[:, 16:16 + NG], Bd_f, gt["g"][:, 0:NG], True, True, gb, pgb, sig=False)
              mm(pg[:, 32:32 + NG], E0_f, gt["g"][:, 0:NG], True, True, gb, pgb, sig=False)
              mm(pg[:, 48:48 + NG], E1_f, gt["g"][:, 0:NG], True, True, gb, pgb)
              cp("dve", gt["gc"][:, 0:NG], pg[:, 0:NG], pgb, gb)
              ts1("dve", gt["ngc"][:, 0:NG], gt["gc"][:, 0:NG], -1.0, ALU.mult, gb, gb)
              stt("dve", gt["gcb"][:, 0:NG], gt["sbn"][:, 0:NG], -1.0, gt["gc"][:, 0:NG], ALU.mult, ALU.add, gb, gb)
              act(gt["eg"][:, 0:NG], pg[:, 0:NG], AF.Exp, pgb, gb)
              tt("dve", gt["bEg"][:, 0:NG], gt["eg"][:, 0:NG], gt["beta"][:, 0:NG], ALU.mult, gb, gb)
              tt("dve", gt["tmpa"][:, 0:NG], pg[:, 16:16 + NG], gt["ngc"][:, 0:NG], ALU.add, pgb + gb, gb)
              act(gt["tmpa"][:, 0:NG], gt["tmpa"][:, 0:NG], AF.Exp, gb, gb)
              ts1("dve", gt["Ed0"][:, 0:NG], gt["tmpa"][:, 0:NG], E0_f[:, 0:1], ALU.mult, gb, gb)
              ts1("dve", gt["Ed1"][:, 0:NG], gt["tmpa"][:, 0:NG], E1_f[:, 0:1], ALU.mult, gb, gb)
              act(egl[:, 0:NG], pg[:, 32:32 + NG], AF.Exp, pgb, gb)
              act(egl[:, 16:16 + NG], pg[:, 48:48 + NG], AF.Exp, pgb, gb)
              wp, wpb = wnext()
              for mi in range(4):
                  ps = bigps()
                  for k in range(8):
                      mm(ps.t[:, 0:N], wp[:, k * 512 + mi * 128:k * 512 + (mi + 1) * 128], xTb[:, k, 0:N],
                         k == 0, k == 7, wpb + [xTb_b[k]], ps.b(), sig=(k == 7))
                  act(hid[:, 16 + mi, 0:N], ps.t[:, 0:N], AF.Gelu, ps.b(), [hid_b[16 + mi]])
              wp, wpb = wnext()
              for blk in range(NB):
                  ps = bigps()
                  for k in range(8):
                      mm(ps.t[:, 0:512], xTb[:, k, blk * 128:(blk + 1) * 128], wp[:, k * 512:(k + 1) * 512],
                         k == 0, k == 7, wpb + [xTb_b[k]], ps.b(), sig=(k == 7))
                  vi = nxt("vg", 2)
                  v_, vb_ = vg[vi], [vg_b[vi]]
                  act(v_[:], ps.t[:, 0:512], AF.Gelu, ps.b(), vb_)
                  tk.op("dve", lambda e: e.bn_stats(out=st6[:, 0:6], in_=v_[:]), vb_, [mv_b])
                  tk.op("dve", lambda e: e.bn_aggr(out=mv[:, 0:2], in_=st6[:, 0:6]), [mv_b], [mv_b])
                  rsq(mv[:, 2:3], mv[:, 1:2], 0, [mv_b], [mv_b])
                  stt("dve", mv[:, 3:4], mv[:, 0:1], -1.0, mv[:, 2:3], ALU.mult, ALU.mult, [mv_b], [mv_b])
                  act(v_[:], v_[:], AF.Identity, vb_ + [mv_b], vb_, bias=mv[:, 3:4], scale=mv[:, 2:3])
                  tt("dve", v_[:], v_[:], pbc[:, BC:BC + 512], ALU.mult, vb_, vb_)
                  if is_sample:
                      tt("dve", v_[:], v_[:], pbc[:, BC + 512:BC + 1024], ALU.add, vb_, vb_)
                      cp("pool", hid[:, 20 + blk, :], v_[:], vb_, [hid_b[20 + blk]])
                      misc_store(nvs[l], v_[0:64, :], vb_)
                  else:
                      tt("pool", hid[:, 20 + blk, :], v_[:], pbc[:, BC + 512:BC + 1024], ALU.add, vb_, [hid_b[20 + blk]])

              chk(l, "p2")
              for c in range(8):
                  sq, sqb = sBn()
                  act(sq[:, 0:N], hid[:, c, 0:N], AF.Square, [hid_b[c]], sqb)
                  ps = bigps()
                  if c < 4:
                      mm(ps.t[:, 0:N], ones128b[:], sq[:, 0:N], True, True, sqb, ps.b())
                      eps_ = 2
                  else:
                      mm(ps.t[:, 0:N], onesb[:], sq[:, 0:N], True, True, sqb, ps.b())
                      eps_ = 1
                  t, tb = sFn()
                  rsq(t[:, 0:N], ps.t[:, 0:N], eps_, ps.b(), tb)
                  tt("pool", hid[:, c, 0:N], hid[:, c, 0:N], t[:, 0:N], ALU.mult, [hid_b[c]] + tb, [hid_b[c]])

              chk(l, "g1")
              for blk in range(NB):
                  bs = slice(blk * 128, (blk + 1) * 128)
                  chk(l, "g5b%d" % blk)
                  barrier()
                  for h in range(4):
                      H = hd[h]
                      idx = blk * 4 + h
                      qb_, kb_, vb2_ = [hid_b[h]], [hid_b[4 + h]], [hid_b[8 + h]]
                      di = nxt("dg", 2)
                      ts1("pool", dgp[di][:, 0:128], identb[:], gt["gc"][:, idx:idx + 1], ALU.mult, gb, [dgp_b[di]])
                      ts1("pool", dgp[di][:, 128:256], identb[:], gt["gcb"][:, idx:idx + 1], ALU.mult, gb, [dgp_b[di]])
                      ps = bigps()
                      mm(ps.t[:, 0:256], ones_f, dgp[di][:], True, True, [dgp_b[di]], ps.b(0, 256), sig=False)
                      mm(ps.t[:, 256:512], hid[:, 4 + h, bs], hid[:, 0:8, :].rearrange("p (a h) n -> p a h n", a=2)[:, :, h, bs], True, True, qb_ + kb_, ps.b(256, 512))
                      ti_ = nxt("tmp", 2)
                      stt("dve", tmpp[ti_][:], ps.t[:, 0:256], gt["ngc"][:, idx:idx + 1], maskneg2, ALU.add, ALU.add,
                          ps.b(0, 256) + gb, [tmpp_b[ti_]])
                      ei = nxt("E", 2)
                      act(Ep[ei][:], tmpp[ti_][:], AF.Exp, [tmpp_b[ti_]], [Ep_b[ei]])
                      act(H["EG"][:], ps.t[:, 0:128], AF.Exp, ps.b(0, 128), [H["EG_b"]])
                      yx0, yx0b = H["YX"][0], [H["YX_b"][0]]
                      tt("dve", H["attnT"][:], ps.t[:, 256:384], Ep[ei][:, 0:128], ALU.mult, ps.b(256, 384) + [Ep_b[ei]], [H["attnT_b"]])
                      tt("dve", yx0[:, 0:128], ps.t[:, 384:512], Ep[ei][:, 128:256], ALU.mult, ps.b(384, 512) + [Ep_b[ei]], yx0b)
                      stt("dve", H["P"][:], yx0[:, 0:128], -1.0, identb[:], ALU.mult, ALU.add, yx0b, [H["P_b"]])
                      tt("pool", H["qdT"][:], hid[:, h, bs], H["EG"][:], ALU.mult, qb_ + [H["EG_b"]], [H["qdT_b"]])
                  pt, ptb = pTbank()
                  for h in range(4):
                      H = hd[h]
                      mm(pt[:, h * 128:(h + 1) * 128], H["YX"][0][:, 0:128], identb[:], True, True, [H["YX_b"][0]], ptb, sig=(h == 3))
                  for h in range(4):
                      H = hd[h]
                      cp("act", H["YX"][0][:, 128:256], pt[:, h * 128:(h + 1) * 128], ptb, [H["YX_b"][0]])
                  pk, pkb = pTbank()
                  for h in range(4):
                      mm(pk[:, h * 128:(h + 1) * 128], hid[:, 4 + h, bs], identb[:], True, True, [hid_b[4 + h]], pkb, sig=(h == 3))
                  for h in range(4):
                      H = hd[h]
                      idx = blk * 4 + h
                      pks = pk[:, h * 128:(h + 1) * 128]
                      ts1("dve", H["kbg"][:], pks, gt["bEg"][:, idx:idx + 1], ALU.mult, pkb + gb, [H["kbg_b"]])
                      ts1("dve", H["kd0"][:], pks, gt["Ed0"][:, idx:idx + 1], ALU.mult, pkb + gb, [H["kd0_b"]])
                      ts1("dve", H["kd1"][:], pks, gt["Ed1"][:, idx:idx + 1], ALU.mult, pkb + gb, [H["kd1_b"]])
                  pv, pvb = pTbank()
                  for h in range(4):
                      mm(pv[:, h * 128:(h + 1) * 128], hid[:, 8 + h, bs], identb[:], True, True, [hid_b[8 + h]], pvb, sig=(h == 3))
                  for h in range(4):
                      H = hd[h]
                      idx = blk * 4 + h
                      ts1("dve", H["vb"][:], pv[:, h * 128:(h + 1) * 128], gt["beta"][:, idx:idx + 1], ALU.mult, pvb + gb, [H["vb_b"]])
                  chk(l, "g2b%d" % blk)
                  barrier()
                  for k in range(6):
                      for h in range(4):
                          H = hd[h]
                          cur, curb = H["YX"][k % 2], [H["YX_b"][k % 2]]
                          nx_, nxb = H["YX"][(k + 1) % 2], [H["YX_b"][(k + 1) % 2]]
                          ps = bigps()
                          if k >= 1:
                              mm(ps.t[:, 0:128], cur[:, 128:256], H["P"][:], True, True, curb + [H["P_b"]], ps.b(0, 128), sig=(k == 5))
                          if k <= 4:
                              mm(ps.t[:, 128:256], cur[:, 128:256], cur[:, 0:128], True, True, curb, ps.b(128, 256), sig=False)
                              mm(ps.t[:, 256:384], cur[:, 0:128], cur[:, 128:256], True, True, curb, ps.b(256, 384))
                              cp("dve", nx_[:], ps.t[:, 128:384], ps.b(128, 384), nxb)
                          if k >= 1:
                              tt("dve", H["P"][:], H["P"][:], ps.t[:, 0:128], ALU.add, [H["P_b"]] + ps.b(0, 128), [H["P_b"]])
                      chk(l, "n%db%d" % (k, blk))
                      barrier()
                  for h in range(4):
                      H = hd[h]
                      pq, pqb = smallq()
                      mm(pq, H["kbg"][:], H["P"][:], True, True, [H["kbg_b"], H["P_b"]], pqb)
                      act(H["nwT"][:], pq, AF.Identity, pqb, [H["nwT_b"]], scale=-1.0)
                  chk(l, "g3b%d" % blk)
                  barrier()
                  for i in range(2):
                      cs = slice(i * 64, (i + 1) * 64)
                      for h in range(4):
                          H = hd[h]
                          idx = blk * 4 + h
                          pq, pqb = smallq()
                          mm(pq, H["P"][:], H["vb"][:], True, False, [H["P_b"], H["vb_b"]], pqb, sig=False)
                          mm(pq, H["nwT"][:], Sbf[:, l, h, :], False, True, [H["nwT_b"], Sbf_b[l][h]], pqb)
                          cp("act", H["vnew"][:], pq, pqb, [H["vnew_b"]])
                          oc = slice(h * 128 + i * 64, h * 128 + (i + 1) * 64)
                          mm(psO.t[:, oc], Sbf[:, l, h, :], H["qdT"][:, cs], True, False, [Sbf_b[l][h], H["qdT_b"]], [psO.q[h]], sig=False)
                          mm(psO.t[:, oc], H["vnew"][:], H["attnT"][:, cs], False, True, [H["vnew_b"], H["attnT_b"]], [psO.q[h]])
                          pq2, pq2b = smallq()
                          kd = H["kd0"] if i == 0 else H["kd1"]
                          kdb = H["kd0_b"] if i == 0 else H["kd1_b"]
                          mm(pq2, kd[:], H["vnew"][:], True, True, [kdb, H["vnew_b"]], pq2b)
                          stt("dve", S32[:, l, h, :], S32[:, l, h, :], egl[:, i * 16 + idx:i * 16 + idx + 1], pq2, ALU.mult, ALU.add,
                              [S32_b[l][h]] + pq2b + gb, [S32_b[l][h]])
                          cp("pool", Sbf[:, l, h, :], S32[:, l, h, :], [S32_b[l][h]], [Sbf_b[l][h]])
                      if is_sample and i == 0:
                          misc_store(nds[l].rearrange("h d e -> d h e"), S32[:, l], S32_b[l])
                  chk(l, "g4b%d" % blk)
                  barrier()
                  sq, sqb = sBn()
                  act(sq[:, 0:512], psO.t[:, 0:512], AF.Square, psO.b(), sqb)
                  ps = bigps()
                  mm(ps.t[:, 0:512], onesrmsb[:], sq[:, 0:512], True, True, sqb, ps.b())
                  t, tb = sFn()
                  rsq(t[:, 0:512], ps.t[:, 0:512], 1, ps.b(), tb)
                  tt("dve", t[:, 0:512], t[:, 0:512], psO.t[:, 0:512], ALU.mult, tb + psO.b(), tb)
                  stt("dve", qkv_pre[:, 0:4, bs], t[:, 0:512].rearrange("p (h c) -> p h c", h=4), pcol[:, PC + 80:PC + 81],
                      hid[:, 12:16, bs], ALU.mult, ALU.mult, tb + hid_b[12:16], qp_b[0:4])
              if last_tile and not is_sample:
                  misc_store(ndp[l, s].rearrange("h d e -> d h e"), S32[:, l], S32_b[l])

              chk(l, "gdn")
              for g in range(4):
                  ps = bigps()
                  for blk in range(NB):
                      bs = slice(blk * 128, (blk + 1) * 128)
                      mm(ps.t[:, bs], onesb[:], bsmat[:, l * 4 + g, :], True, False, (), ps.b(), sig=False)
                      mm(ps.t[:, bs], hid[:, 20 + blk, g * 128:(g + 1) * 128], WsT[:, l * 4 + g, :], False, True, [hid_b[20 + blk]], ps.b())
                  tt("dve", qkv_pre[:, 4 + g, 0:N], ps.t[:, 0:N], hid[:, 16 + g, 0:N], ALU.mult, ps.b() + [hid_b[16 + g]], [qp_b[4 + g]])

              chk(l, "mix")
              for m in range(8):
                  if m % 4 == 0:
                      wp, wpb = wnext()
                  mi = m % 4
                  ps = bigps()
                  for k in range(8):
                      mm(ps.t[:, 0:N], wp[:, k * 512 + mi * 128:k * 512 + (mi + 1) * 128], qkv_pre[:, k, 0:N],
                         k == 0, k == 7, wpb + [qp_b[k]], ps.b(), sig=(k == 7))
                  stt("dve", xT[:, m, 0:N], xT[:, m, 0:N], ALPHA, ps.t[:, 0:N], ALU.mult, ALU.add, [xT_b[m]] + ps.b(), [xT_b[m]])
              chk(l, "res1")
              ln_fm(N, PC + 0, PC + 8)
              chk(l, "ln1")

              for j in range(8):
                  wp, wpb = wnext()
                  for fi in range(4):
                      f = j * 4 + fi
                      ps = bigps()
                      for k in range(8):
                          mm(ps.t[:, 0:N], wp[:, k * 512 + fi * 128:k * 512 + (fi + 1) * 128], xTb[:, k, 0:N],
                             k == 0, k == 7, wpb + [xTb_b[k]], ps.b(), sig=(k == 7))
                      t, tb = sFn()
                      act(t[:, 0:N], ps.t[:, 0:N], AF.Relu, ps.b(), tb)
                      tt("pool" if f % 2 else "dve", hid[:, f, 0:N], t[:, 0:N], t[:, 0:N], ALU.mult, tb, [hid_b[f]])
              for m in range(8):
                  wp, wpb = wnext()
                  ps = bigps()
                  for f in range(32):
                      mm(ps.t[:, 0:N], wp[:, f * 128:(f + 1) * 128], hid[:, f, 0:N], f == 0, f == 31, wpb + [hid_b[f]], ps.b(), sig=(f == 31))
                  stt("dve", xT[:, m, 0:N], xT[:, m, 0:N], ALPHA, ps.t[:, 0:N], ALU.mult, ALU.add, [xT_b[m]] + ps.b(), [xT_b[m]])
              chk(l, "res2")
              ln_fm(N, PC + 16, PC + 24)
              chk(l, "ln2")

          for blk in range(NB):
              slot = nxt("vg", 2)
              yo, yob = xin[slot], [xin_b[slot]]
              for half in range(2):
                  ps = bigps()
                  for cc in range(4):
                      c = half * 4 + cc
                      mm(ps.t[:, cc * 128:(cc + 1) * 128], xT[:, c, blk * 128:(blk + 1) * 128], ident_f, True, True, [xT_b[c]], [ps.q[cc]], sig=(cc == 3))
                  cp("act" if half else "dve", yo[:, half * 512:(half + 1) * 512], ps.t[:, 0:512], ps.b(), yob)
              if is_sample:
                  tk.dma("sp", ys, yo[0:64, :], "sty%d" % slot, yob, ())
              else:
                  r0 = ti * 512 + blk * 128
                  tk.dma("sp", yp[s, r0:r0 + 128, :], yo[:], "sty%d" % slot, yob, ())

    except _Stop:
        pass

    for key in ("sty0", "sty1", "misc"):
        tk.wait_all("sp", key)
    _LAST_CNT.clear()
    _LAST_CNT.update(tk.cnt)
    return nc


def _consts():
    c = np.zeros((128, NCONST), np.float32)
    i = np.arange(128)
    P = i[:, None]
    Fr = i[None, :]
    same = (P // 64) == (Fr // 64)
    c[:, 0:128] = np.eye(128)
    c[:, 128:256] = 1.0
    c[:, 256:384] = (same & (P <= Fr))
    c[:, 384:512] = same
    c[:, 512:640] = (P < 64) * np.ones((1, 128))
    c[:, 640:768] = (P >= 64) * np.ones((1, 128))
    c[:, 768:896] = np.where(same & (P <= Fr), 0.0, NEG)
    c[:, 896:1024] = np.where(same & (P < Fr), 0.0, NEG)
    c[:, 1024:1152] = (P <= Fr)
    return c


def _pack_small(inp):
    f = lambda a: np.asarray(a, np.float32)
    col = lambda v: f(v).reshape(-1, 128).T
    pcol = np.zeros((128, 16 + NL * LPC), np.float32)
    pcol[:, 0:8] = col(inp["ln0_g"])
    pcol[:, 8:16] = col(inp["ln0_b"])
    pbc = np.zeros((128, NL * LBC), np.float32)
    for l in range(NL):
        b = 16 + l * LPC
        pcol[:, b:b + 8] = col(inp["ln1_g"][l])
        pcol[:, b + 8:b + 16] = col(inp["ln1_b"][l])
        pcol[:, b + 16:b + 24] = col(inp["ln2_g"][l])
        pcol[:, b + 24:b + 32] = col(inp["ln2_b"][l])
        cw = f(inp["conv_w"][l])
        pcol[:, b + 32:b + 80] = cw.reshape(4, 12, 128).transpose(2, 1, 0).reshape(128, 48)
        pcol[:, b + 80] = f(inp["gdn_norm_g"][l])
        o = l * LBC
        pbc[:, o:o + 512] = f(inp["smlp_ln_g"][l])[None, :]
        pbc[:, o + 512:o + 1024] = f(inp["smlp_ln_b"][l])[None, :]
        pbc[:, o + 1024:o + 1040] = np.tile(f(inp["a_log"][l]), 4)[None, :]
        pbc[:, o + 1040:o + 1056] = np.tile(f(inp["dt_bias"][l]), 4)[None, :]
    bsrow = f(inp["b_s"]).reshape(1, NL * 4 * 128)
    ws = f(inp["w_s"]).reshape(NL * 4, 128, 128)
    return pcol, pbc, bsrow, ws


SEQUENTIAL_LAUNCH = False
_NC_CACHE = {}
_NOCAST = [0]
_DELAY = [0]
_LAST_CNT = {}
_STOP = [None]


def run_cores(inp, TP, NSP, ncores):
    key = (TP, NSP, _STOP[0])
    if key not in _NC_CACHE:
        _NC_CACHE[key] = build(TP, NSP, _STOP[0])
    nc = _NC_CACHE[key]
    f = lambda a: np.ascontiguousarray(np.asarray(a, np.float32))
    pcol, pbc, bsrow, ws = _pack_small(inp)
    consts = _consts()
    shared = {"w_in": f(inp["w_in"]), "w_out": f(inp["w_out"]), "w_up": f(inp["w_up"]), "w_down": f(inp["w_down"]),
              "ws": ws, "consts": consts, "pcol": pcol, "pbc": pbc, "bsrow": bsrow}
    xp = f(inp["x_prompt"])
    xs = f(inp["x_sample"])
    cc = f(inp["cache_conv"])
    sd = f(inp["state_delta"])
    in_maps = []
    for i in range(ncores):
        m = dict(shared)
        m["xp"] = np.ascontiguousarray(xp[i * NSP:(i + 1) * NSP])
        m["xs"] = np.ascontiguousarray(xs[i])
        m["cconv"] = np.ascontiguousarray(cc[:, i])
        m["sdelta"] = np.ascontiguousarray(sd[:, i])
        in_maps.append(m)
    if SEQUENTIAL_LAUNCH and ncores > 1:
        R = []
        for i in range(ncores):
            R.append(run_bass_kernel_spmd(nc, [in_maps[i]], core_ids=[0]).results[0])
    else:
        R = run_bass_kernel_spmd(nc, in_maps, core_ids=list(range(ncores))).results
    y_p = np.concatenate([r["yp"] for r in R], axis=0)
    y_s = np.stack([r["ys"] for r in R], axis=0)
    ncp = np.concatenate([r["ncp"] for r in R], axis=1)
    ndp = np.concatenate([r["ndp"] for r in R], axis=1)
    ncs = np.stack([r["ncs"] for r in R], axis=1)
    nds = np.stack([r["nds"] for r in R], axis=1)
    nvs = np.stack([r["nvs"] for r in R], axis=1)
    return tuple(np.ascontiguousarray(a, dtype=np.float32) for a in (y_p, y_s, ncp, ndp, ncs, nds, nvs))


def kernel(**inputs):
    return run_cores(inputs, 4096, 2, 8)
```

```python
import numpy as np
import concourse.bass as bass
import concourse.mybir as mybir
from concourse.bass_utils import run_bass_kernel_spmd

F32 = mybir.dt.float32
BF16 = mybir.dt.bfloat16
AF = mybir.ActivationFunctionType
ALU = mybir.AluOpType

NL = 2
D = 1024
INW = 3080
ALPHA = float((2 * NL) ** 0.25)
LN_EPS = 1e-5
NEPS = 1e-6
NEG = -30000.0
NWP = 8
NBIG = 5
LPC = 81
LBC = 1056
NCONST = 1152


class Buf:
    __slots__ = ("w", "r", "name")

    def __init__(self, name=""):
        self.w = None
        self.r = {}
        self.name = name


class TK:
    def __init__(self, nc):
        self.nc = nc
        self.E = {"pe": nc.tensor, "act": nc.scalar, "dve": nc.vector, "pool": nc.gpsimd, "sp": nc.sync}
        self.sem = {}
        self.cnt = {}
        self.waited = {e: {} for e in self.E}
        for e in self.E:
            self.newsem(e)

    def newsem(self, key):
        self.sem[key] = self.nc.alloc_semaphore(name="s_" + key)
        self.cnt[key] = 0

    def _needs(self, reads, writes):
        nd = {}
        for b in reads:
            if b.w is not None:
                k, v = b.w
                if nd.get(k, 0) < v:
                    nd[k] = v
        for b in writes:
            if b.w is not None:
                k, v = b.w
                if nd.get(k, 0) < v:
                    nd[k] = v
            for k, v in b.r.items():
                if nd.get(k, 0) < v:
                    nd[k] = v
        return nd

    def _wait(self, e, nd):
        w = self.waited[e]
        for k, v in nd.items():
            if e == "pe" and k == "pe":
                continue
            if w.get(k, 0) < v:
                self.E[e].wait_ge(self.sem[k], v)
                w[k] = v

    def op(self, e, fn, reads=(), writes=(), sig=True):
        self._wait(e, self._needs(reads, writes))
        ins = fn(self.E[e])
        if sig:
            self.cnt[e] += 1
            ins.then_inc(self.sem[e], 1)
            c = self.cnt[e]
        else:
            c = self.cnt[e] + 1
        for b in reads:
            if b.r.get(e, 0) < c:
                b.r[e] = c
        for b in writes:
            b.w = (e, c)
            b.r = {}

    def dma(self, q, out, in_, key, reads=(), writes=()):
        self._wait(q, self._needs(reads, writes))
        self.E[q].dma_start(out=out, in_=in_).then_inc(self.sem[key], 16)
        self.cnt[key] += 16
        c = self.cnt[key]
        for b in reads:
            if b.r.get(key, 0) < c:
                b.r[key] = c
        for b in writes:
            b.w = (key, c)
            b.r = {}

    def wait_all(self, e, key):
        v = self.cnt[key]
        if v > 0 and self.waited[e].get(key, 0) < v:
            self.E[e].wait_ge(self.sem[key], v)
            self.waited[e][key] = v


class PSB:
    def __init__(self, t, name):
        self.t = t
        self.whole = Buf(name)
        self.q = [self.whole] * 4

    def b(self, c0=0, c1=512):
        return [self.whole]


class _Stop(Exception):
    pass


def build(TP, NSP=2, stop=None):
    nc = bass.Bass("TRN2", target_bir_lowering=False)
    dt = nc.dram_tensor
    xp = dt("xp", [NSP, TP, D], F32, kind="ExternalInput").ap()
    xs = dt("xs", [64, D], F32, kind="ExternalInput").ap()
    cconv = dt("cconv", [NL, 3, 1536], F32, kind="ExternalInput").ap()
    sdelta = dt("sdelta", [NL, 4, 128, 128], F32, kind="ExternalInput").ap()
    w_in = dt("w_in", [NL, D, INW], F32, kind="ExternalInput").ap()
    w_out = dt("w_out", [NL, D, D], F32, kind="ExternalInput").ap()
    w_up = dt("w_up", [NL, D, 4096], F32, kind="ExternalInput").ap()
    w_down = dt("w_down", [NL, 4096, D], F32, kind="ExternalInput").ap()
    ws_d = dt("ws", [NL * 4, 128, 128], F32, kind="ExternalInput").ap()
    consts_d = dt("consts", [128, NCONST], F32, kind="ExternalInput").ap()
    pcol_d = dt("pcol", [128, 16 + NL * LPC], F32, kind="ExternalInput").ap()
    pbc_d = dt("pbc", [128, NL * LBC], F32, kind="ExternalInput").ap()
    bsrow_d = dt("bsrow", [1, NL * 4 * 128], F32, kind="ExternalInput").ap()

    yp = dt("yp", [NSP, TP, D], F32, kind="ExternalOutput").ap()
    ys = dt("ys", [64, D], F32, kind="ExternalOutput").ap()
    ncp = dt("ncp", [NL, NSP, 3, 1536], F32, kind="ExternalOutput").ap()
    ndp = dt("ndp", [NL, NSP, 4, 128, 128], F32, kind="ExternalOutput").ap()
    ncs = dt("ncs", [NL, 3, 1536], F32, kind="ExternalOutput").ap()
    nds = dt("nds", [NL, 4, 128, 128], F32, kind="ExternalOutput").ap()
    nvs = dt("nvs", [NL, 64, 512], F32, kind="ExternalOutput").ap()

    wsc = dt("wsc", [NL, 24, 128, 4096], BF16, kind="Internal").ap()
    wbasc = dt("wbasc", [NL, 128, 64], BF16, kind="Internal").ap()

    tk = TK(nc)
    _n = [0]

    def sb(shape, dtype, name=None):
        _n[0] += 1
        return nc.alloc_sbuf_tensor("sb_" + (name or ("t%d" % _n[0])), shape, dtype)

    cst = sb([128, NCONST], F32, "cst")
    pcol = sb([128, 16 + NL * LPC], F32, "pcol")
    pbc = sb([128, NL * LBC], F32, "pbc")
    epsc = sb([128, 4], F32, "epsc")
    identb = sb([128, 128], BF16, "identb")
    onesb = sb([128, 128], BF16, "onesb")
    ones128b = sb([128, 128], BF16, "ones128b")
    ones1kb = sb([128, 128], BF16, "ones1kb")
    onesrmsb = sb([128, 128], BF16, "onesrmsb")
    WsT = sb([128, 8, 128], BF16, "WsT")
    bsmat = sb([128, 8, 128], BF16, "bsmat")
    nexpA = sb([128, NL, 16], F32, "nexpA")
    wba = sb([128, NL, 64], BF16, "wba")
    wpool = [sb([128, 4096], BF16, "wp%d" % i) for i in range(NWP)]
    wpool_b = [Buf("wp%d" % i) for i in range(NWP)]
    xin = [sb([128, 1024], F32, "xin%d" % i) for i in range(2)]
    xin_b = [Buf("xin%d" % i) for i in range(2)]
    xT = sb([128, 8, 512], F32, "xT")
    xT_b = [Buf("xT%d" % i) for i in range(8)]
    xTb = sb([128, 8, 512], BF16, "xTb")
    xTb_b = [Buf("xTb%d" % i) for i in range(8)]
    qkv_pre = sb([128, 12, 515], BF16, "qkv_pre")
    qp_b = [Buf("qp%d" % i) for i in range(12)]
    tails = sb([128, NL, 12, 3], BF16, "tails")
    tails_b = [Buf("tails%d" % i) for i in range(NL)]
    diag = [sb([128, 4, 128], BF16, "diag%d" % i) for i in range(2)]
    diag_b = [Buf("diag%d" % i) for i in range(2)]
    hid = sb([128, 32, 512], BF16, "hid")
    hid_b = [Buf("hid%d" % i) for i in range(32)]
    vg = [sb([128, 512], F32, "vg%d" % i) for i in range(2)]
    vg_b = [Buf("vg%d" % i) for i in range(2)]
    sF = [sb([128, 512], F32, "sF%d" % i) for i in range(4)]
    sF_b = [Buf("sF%d" % i) for i in range(4)]
    sB = [sb([128, 512], BF16, "sB%d" % i) for i in range(2)]
    sB_b = [Buf("sB%d" % i) for i in range(2)]
    rstd_t = sb([128, 512], F32, "rstd")
    nmr_t = sb([128, 512], F32, "nmr")
    ln_b = Buf("lnstat")
    S32 = sb([128, NL, 4, 128], F32, "S32")
    Sbf = sb([128, NL, 4, 128], BF16, "Sbf")
    S32_b = [[Buf("S32_%d%d" % (l, h)) for h in range(4)] for l in range(NL)]
    Sbf_b = [[Buf("Sbf_%d%d" % (l, h)) for h in range(4)] for l in range(NL)]
    gnames = ["beta", "g", "sbn", "gc", "ngc", "gcb", "bEg", "Ed0", "Ed1", "eg", "tmpa", "tmpb"]
    gt = {n: sb([128, 16], F32, "g_" + n) for n in gnames}
    egl = sb([128, 32], F32, "g_egl")
    gates_b = Buf("gates")
    mv = sb([128, 16], F32, "mv")
    st6 = sb([128, 12], F32, "st6")
    mv_b = Buf("mv")
    hd = []
    for h in range(4):
        d_ = {}
        for n in ["attnT", "P", "kbg", "kd0", "kd1", "vb", "nwT", "qdT", "vnew", "EG"]:
            d_[n] = sb([128, 128], BF16, "h%d_%s" % (h, n))
            d_[n + "_b"] = Buf("h%d_%s" % (h, n))
        d_["YX"] = [sb([128, 256], BF16, "h%d_YX%d" % (h, i)) for i in range(2)]
        d_["YX_b"] = [Buf("h%d_YX%d" % (h, i)) for i in range(2)]
        hd.append(d_)
    dgp = [sb([128, 256], F32, "dg%d" % i) for i in range(2)]
    dgp_b = [Buf("dg%d" % i) for i in range(2)]
    tmpp = [sb([128, 256], F32, "tmpE%d" % i) for i in range(2)]
    tmpp_b = [Buf("tmpE%d" % i) for i in range(2)]
    Ep = [sb([128, 256], F32, "E%d" % i) for i in range(2)]
    Ep_b = [Buf("E%d" % i) for i in range(2)]

    big = [PSB(nc.alloc_psum_tensor("pbig%d" % i, [128, 512], F32), "pbig%d" % i) for i in range(5)]
    psO = PSB(nc.alloc_psum_tensor("psO", [128, 512], F32), "psO")
    pT = [nc.alloc_psum_tensor("pT%d" % i, [128, 512], F32) for i in range(2)]
    pT_b = [Buf("pT%d" % i) for i in range(2)]
    rot = {"big": 0, "small": 0, "pT": 0, "sF": 0, "sB": 0, "vg": 0, "dg": 0, "tmp": 0, "E": 0, "diag": 0}

    def nxt(key, n):
        i = rot[key]
        rot[key] = (i + 1) % n
        return i

    def bigps():
        return big[nxt("big", NBIG)]

    def smallq():
        p_ = bigps()
        return p_.t[:, 0:128], p_.b()

    def pTbank():
        i = nxt("pT", 2)
        return pT[i], [pT_b[i]]

    def sFn():
        i = nxt("sF", 4)
        return sF[i], [sF_b[i]]

    def sBn():
        i = nxt("sB", 2)
        return sB[i], [sB_b[i]]

    def mm(out, lhsT, rhs, start, stop, reads, writes, sig=True):
        tk.op("pe", lambda e: e.matmul(out, lhsT=lhsT, rhs=rhs, start=start, stop=stop), reads, writes, sig)

    def act(out, in_, func, reads, writes, bias=None, scale=None):
        kw = {}
        if bias is not None:
            kw["bias"] = bias
        if scale is not None:
            kw["scale"] = scale
        tk.op("act", lambda e: e.activation(out=out, in_=in_, func=func, **kw), reads, writes)

    def ts(eng, out, in0, s1, s2, op0, op1, reads, writes):
        tk.op(eng, lambda e: e.tensor_scalar(out=out, in0=in0, scalar1=s1, scalar2=s2, op0=op0, op1=op1), reads, writes)

    def rsq(out, in0, epscol, reads, writes):
        act(out, in0, AF.Sqrt, reads, writes, bias=epsc[:, epscol:epscol + 1])
        tk.op("dve", lambda e: e.reciprocal(out=out, in_=out), writes, writes)

    def ts1(eng, out, in0, s1, op0, reads, writes):
        tk.op(eng, lambda e: e.tensor_single_scalar(out=out, in_=in0, scalar=s1, op=op0), reads, writes)

    def stt(eng, out, in0, s, in1, op0, op1, reads, writes):
        tk.op(eng, lambda e: e.scalar_tensor_tensor(out=out, in0=in0, scalar=s, in1=in1, op0=op0, op1=op1), reads, writes)

    def tt(eng, out, in0, in1, op, reads, writes):
        tk.op(eng, lambda e: e.tensor_tensor(out=out, in0=in0, in1=in1, op=op), reads, writes)

    def cp(eng, out, in_, reads, writes):
        if eng == "act":
            act(out, in_, AF.Copy, reads, writes)
        else:
            tk.op(eng, lambda e: e.tensor_copy(out=out, in_=in_), reads, writes)

    def mset(eng, ap, v, writes):
        tk.op(eng, lambda e: e.memset(ap, v), (), writes)

    ident_f = cst[:, 0:128]
    ones_f = cst[:, 128:256]
    Mu_f = cst[:, 256:384]
    Bd_f = cst[:, 384:512]
    E0_f = cst[:, 512:640]
    E1_f = cst[:, 640:768]
    maskneg2 = cst[:, 768:1024]
    triU = cst[:, 1024:1152]

    sc_b = [[Buf("sc%d_%d" % (l, g)) for g in range(4)] for l in range(NL)]
    win_cols = [0, 512, 1024, 1536, 2056, 2568]
    for l in range(NL):
        for g in range(4):
            tk.newsem("cv%d_%d" % (l, g))
    tk.newsem("cvba")
    wba_sc_b = Buf("wbasc")
    for l in range(NL if not _NOCAST[0] else 0):
        for pi, c0 in enumerate(win_cols):
            tk.dma("pool", wsc[l, pi].rearrange("p (k c) -> p k c", k=8),
                   w_in[l][:, c0:c0 + 512].rearrange("(k p) c -> p k c", p=128), "cv%d_0" % l, (), [sc_b[l][0]])
        tk.dma("pool", wbasc[l].rearrange("p (k c) -> p k c", k=8),
               w_in[l][:, 2048:2056].rearrange("(k p) c -> p k c", p=128), "cvba", (), [wba_sc_b])
        for j in range(2):
            tk.dma("pool", wsc[l, 6 + j].rearrange("p (k c) -> p k c", k=8),
                   w_out[l][:, j * 512:(j + 1) * 512].rearrange("(k p) c -> p k c", p=128), "cv%d_1" % l, (), [sc_b[l][1]])
        for j in range(8):
            tk.dma("pool", wsc[l, 8 + j].rearrange("p (k c) -> p k c", k=8),
                   w_up[l][:, j * 512:(j + 1) * 512].rearrange("(k p) c -> p k c", p=128), "cv%d_2" % l, (), [sc_b[l][2]])
        for m in range(8):
            tk.dma("pool", wsc[l, 16 + m].rearrange("p (f c) -> p f c", f=32),
                   w_down[l][:, m * 128:(m + 1) * 128].rearrange("(f p) c -> p f c", p=128), "cv%d_3" % l, (), [sc_b[l][3]])

    tk.newsem("const")
    cbuf = Buf("const")
    tk.dma("sp", cst[:], consts_d, "const", (), [cbuf])
    tk.dma("sp", pcol[:], pcol_d, "const", (), [cbuf])
    tk.dma("sp", pbc[:], pbc_d, "const", (), [cbuf])
    tk.dma("sp", sF[0][0:1, :], bsrow_d[:, 0:512], "const", (), [cbuf])
    tk.dma("sp", sF[1][0:1, :], bsrow_d[:, 512:1024], "const", (), [cbuf])
    tk.dma("sp", xin[0][:].rearrange("p (a q) -> p a q", a=8), ws_d.rearrange("a p q -> p a q"), "const", (), [cbuf, xin_b[0]])
    tk.dma("sp", wba[:], wbasc.rearrange("l p c -> p l c"), "const", [wba_sc_b], [cbuf])
    for e in ("pe", "act", "dve", "pool"):
        tk.wait_all(e, "const")

    setup_b = Buf("setup")
    cp("dve", identb[:], ident_f, (), [setup_b])
    mset("dve", onesb[:], 1.0, [setup_b])
    mset("dve", epsc[:, 0:1], LN_EPS, [setup_b])
    mset("dve", epsc[:, 1:2], NEPS, [setup_b])
    mset("dve", epsc[:, 2:3], 128.0 * NEPS, [setup_b])
    mset("dve", epsc[:, 3:4], 1.0, [setup_b])
    mset("dve", ones128b[:], 128.0, [setup_b])
    mset("dve", ones1kb[:], 1.0 / 1024.0, [setup_b])
    mset("dve", onesrmsb[:], 1.0 / 128.0, [setup_b])
    mset("dve", bsmat[:], 0.0, [setup_b])
    for a in range(8):
        src = sF[a // 4][0:1, (a % 4) * 128:(a % 4 + 1) * 128]
        cp("dve", bsmat[0:1, a, :], src, [setup_b], [setup_b])
    for a in range(8):
        pq, pqb = smallq()
        mm(pq, xin[0][:, a * 128:(a + 1) * 128], ident_f, True, True, [setup_b, xin_b[0]], pqb)
        tt("dve", WsT[:, a, :], pq, triU, ALU.mult, pqb, [setup_b])
    for l in range(NL):
        act(nexpA[:, l, :], pbc[:, l * LBC + 1024:l * LBC + 1040], AF.Exp, [setup_b], [setup_b])
        ts1("dve", nexpA[:, l, :], nexpA[:, l, :], -1.0, ALU.mult, [setup_b], [setup_b])
    for e in ("pe", "act", "pool", "dve"):
        tk._wait(e, {"dve": tk.cnt["dve"], "act": tk.cnt["act"], "pe": tk.cnt["pe"]})

    if _DELAY[0]:
        for _i in range(_DELAY[0]):
            tk.op("act", lambda e: e.activation(out=vg[0][:, 0:512], in_=vg[1][:, 0:512], func=AF.Copy), (), [vg_b[0]])
    NTP = TP // 512
    tiles = []
    for s in range(NSP):
        for ti in range(NTP):
            tiles.append((s, ti, 512))
    tiles.append((NSP, 0, 128))
    pieces = []
    for (_s, _ti, _n) in tiles:
        for l in range(NL):
            for pi in range(24):
                g = 0 if pi < 6 else (1 if pi < 8 else (2 if pi < 16 else 3))
                pieces.append((l, pi, g))
    for i in range(NWP):
        tk.newsem("wp%d" % i)
    wst = {"i": 0, "loaded": 0}

    def wnext():
        while wst["loaded"] < min(len(pieces), wst["i"] + NWP):
            j = wst["loaded"]
            l, pi, g = pieces[j]
            slot = j % NWP
            tk.dma("sp", wpool[slot][:], wsc[l, pi], "wp%d" % slot, [sc_b[l][g]], [wpool_b[slot]])
            wst["loaded"] += 1
        j = wst["i"]
        wst["i"] += 1
        return wpool[j % NWP], [wpool_b[j % NWP]]

    tk.newsem("xin0")
    tk.newsem("xin1")
    tk.newsem("sty0")
    tk.newsem("sty1")
    tk.newsem("misc")

    def misc_store(out, in_, reads):
        tk.dma("sp", out, in_, "misc", reads, ())
        tk.wait_all("sp", "misc")

    def misc_load(out, in_, writes):
        tk.dma("sp", out, in_, "misc", (), writes)
        tk.wait_all("sp", "misc")

    def barrier():
        snap = {k: tk.cnt[k] for k in ("pe", "act", "dve", "pool")}
        for e in ("pe", "act", "dve", "pool"):
            tk._wait(e, {k: v for k, v in snap.items() if v > 0})

    def ln_fm(N, gcol, bcol):
        psM = bigps()
        psQ = bigps()
        for m in range(8):
            cp("act", xTb[:, m, 0:N], xT[:, m, 0:N], [xT_b[m]], [xTb_b[m]])
            sq, sqb = sBn()
            act(sq[:, 0:N], xT[:, m, 0:N], AF.Square, [xT_b[m]], sqb)
            mm(psM.t[:, 0:N], ones1kb[:], xTb[:, m, 0:N], m == 0, m == 7, [xTb_b[m]], psM.b(), sig=(m == 7))
            mm(psQ.t[:, 0:N], ones1kb[:], sq[:, 0:N], m == 0, m == 7, sqb, psQ.b(), sig=True)
        t1, t1b = sFn()
        act(t1[:, 0:N], psM.t[:, 0:N], AF.Square, psM.b(), t1b)
        stt("dve", t1[:, 0:N], t1[:, 0:N], -1.0, psQ.t[:, 0:N], ALU.mult, ALU.add, t1b + psQ.b(), t1b)
        rsq(rstd_t[:, 0:N], t1[:, 0:N], 0, t1b, [ln_b])
        stt("dve", nmr_t[:, 0:N], psM.t[:, 0:N], -1.0, rstd_t[:, 0:N], ALU.mult, ALU.mult, psM.b() + [ln_b], [ln_b])
        for m in range(8):
            t, tb = sFn()
            tt("dve", t[:, 0:N], xT[:, m, 0:N], rstd_t[:, 0:N], ALU.mult, [xT_b[m], ln_b], tb)
            tt("pool", t[:, 0:N], t[:, 0:N], nmr_t[:, 0:N], ALU.add, tb + [ln_b], tb)
            act(xT[:, m, 0:N], t[:, 0:N], AF.Identity, tb, [xT_b[m]],
                bias=pcol[:, bcol + m:bcol + m + 1], scale=pcol[:, gcol + m:gcol + m + 1])
            cp("pool", xTb[:, m, 0:N], xT[:, m, 0:N], [xT_b[m]], [xTb_b[m]])

    cur_seq = -1
    tile_no = -1

    def chk(l, name):
        if stop is not None and stop == (tile_no, l, name):
            raise _Stop()

    try:
      if stop == "setup":
          raise _Stop()
      for (s, ti, N) in tiles:
          tile_no += 1
          NB = N // 128
          is_sample = (s == NSP)
          n_real = 64 if is_sample else N
          last_tile = is_sample or (ti == NTP - 1)
          if s != cur_seq:
              cur_seq = s
              if not is_sample:
                  for l in range(NL):
                      mset("pool", S32[:, l], 0.0, S32_b[l])
                      mset("pool", Sbf[:, l], 0.0, Sbf_b[l])
                      mset("pool", tails[:, l], 0.0, [tails_b[l]])
              else:
                  for l in range(NL):
                      misc_load(S32[:, l], sdelta[l].rearrange("h d e -> d h e"), S32_b[l])
                      cp("pool", Sbf[:, l], S32[:, l], S32_b[l], Sbf_b[l])
                      pq, pqb = smallq()
                      for cb in range(3):
                          t, tb = sFn()
                          misc_load(t[0:3, :], cconv[l][:, cb * 512:(cb + 1) * 512], tb)
                          for cc in range(4):
                              c = cb * 4 + cc
                              mm(pq[:, c * 3:c * 3 + 3], t[0:3, cc * 128:(cc + 1) * 128], ident_f[0:3, 0:3], True, True, tb, pqb, sig=(cc == 3))
                      cp("dve", tails[:, l].rearrange("p c r -> p (c r)"), pq[:, 0:36], pqb, [tails_b[l]])

          for blk in range(NB):
              slot = nxt("vg", 2)
              xi, xib = xin[slot], [xin_b[slot]]
              if is_sample:
                  mset("pool", xi[:], 0.0, xib)
                  tk.dma("sp", xi[0:64, :], xs, "xin%d" % slot, (), xib)
              else:
                  r0 = ti * 512 + blk * 128
                  tk.dma("sp", xi[:], xp[s, r0:r0 + 128, :], "xin%d" % slot, (), xib)
              for hf in range(2):
                  tk.op("dve", lambda e, hf=hf: e.bn_stats(out=st6[:, hf * 6:(hf + 1) * 6], in_=xi[:, hf * 512:(hf + 1) * 512]), xib, [mv_b])
              tk.op("dve", lambda e: e.bn_aggr(out=mv[:, 0:2], in_=st6[:, 0:12]), [mv_b], [mv_b])
              rsq(mv[:, 2:3], mv[:, 1:2], 0, [mv_b], [mv_b])
              stt("dve", mv[:, 3:4], mv[:, 0:1], -1.0, mv[:, 2:3], ALU.mult, ALU.mult, [mv_b], [mv_b])
              act(xi[:], xi[:], AF.Identity, xib + [mv_b], xib, bias=mv[:, 3:4], scale=mv[:, 2:3])
              for half in range(2):
                  ps = bigps()
                  for cc in range(4):
                      c = half * 4 + cc
                      mm(ps.t[:, cc * 128:(cc + 1) * 128], xi[:, c * 128:(c + 1) * 128], ident_f, True, True, xib, [ps.q[cc]], sig=(cc == 3))
                  for cc in range(4):
                      c = half * 4 + cc
                      act(xT[:, c, blk * 128:(blk + 1) * 128], ps.t[:, cc * 128:(cc + 1) * 128], AF.Identity, [ps.q[cc]], [xT_b[c]],
                          bias=pcol[:, 8 + c:9 + c], scale=pcol[:, c:c + 1])
          for c in range(8):
              cp("pool", xTb[:, c, 0:N], xT[:, c, 0:N], [xT_b[c]], [xTb_b[c]])

          chk(0, "p1")
          for l in range(NL):
              PC = 16 + l * LPC
              BC = l * LBC
              for pc in range(3):
                  wp, wpb = wnext()
                  for mi in range(4):
                      c = pc * 4 + mi
                      ps = bigps()
                      for k in range(8):
                          mm(ps.t[:, 0:N], wp[:, k * 512 + mi * 128:k * 512 + (mi + 1) * 128], xTb[:, k, 0:N],
                             k == 0, k == 7, wpb + [xTb_b[k]], ps.b(), sig=(k == 7))
                      cp("act", qkv_pre[:, c, 3:3 + N], ps.t[:, 0:N], ps.b(), [qp_b[c]])
                  if last_tile:
                      ps = bigps()
                      for k in range(8):
                          mm(ps.t[0:3, 0:512], xTb[:, k, n_real - 3:n_real], wp[:, k * 512:(k + 1) * 512],
                             k == 0, k == 7, wpb + [xTb_b[k]], ps.b(), sig=(k == 7))
                      t, tb = sFn()
                      cp("act", t[0:3, :], ps.t[0:3, 0:512], ps.b(), tb)
                      dst = ncs[l] if is_sample else ncp[l, s]
                      misc_store(dst[:, pc * 512:(pc + 1) * 512], t[0:3, :], tb)
              cp("pool", qkv_pre[:, :, 0:3], tails[:, l], [tails_b[l]], qp_b)
              cp("pool", tails[:, l], qkv_pre[:, :, N:N + 3], qp_b, [tails_b[l]])
              for c in range(12):
                  di = nxt("diag", 2)
                  for j in range(4):
                      ts1("pool", diag[di][:, j, :], identb[:], pcol[:, PC + 32 + c * 4 + j:PC + 33 + c * 4 + j], ALU.mult, (), [diag_b[di]])
                  ps = bigps()
                  for j in range(4):
                      mm(ps.t[:, 0:N], diag[di][:, j, :], qkv_pre[:, c, j:j + N], j == 0, j == 3, [diag_b[di], qp_b[c]], ps.b(), sig=(j == 3))
                  act(hid[:, c, 0:N], ps.t[:, 0:N], AF.Silu, ps.b(), [hid_b[c]])
              wp, wpb = wnext()
              for mi in range(4):
                  ps = bigps()
                  for k in range(8):
                      mm(ps.t[:, 0:N], wp[:, k * 512 + mi * 128:k * 512 + (mi + 1) * 128], xTb[:, k, 0:N],
                         k == 0, k == 7, wpb + [xTb_b[k]], ps.b(), sig=(k == 7))
                  act(hid[:, 12 + mi, 0:N], ps.t[:, 0:N], AF.Silu, ps.b(), [hid_b[12 + mi]])
              pq, pqb = smallq()
              for blk in range(NB):
                  for k in range(8):
                      mm(pq[:, blk * 8:(blk + 1) * 8], xTb[:, k, blk * 128:(blk + 1) * 128], wba[:, l, k * 8:(k + 1) * 8],
                         k == 0, k == 7, [xTb_b[k]], pqb, sig=(k == 7))
              NG = NB * 4
              pq3 = pq[:, 0:NB * 8].rearrange("p (b c) -> p b c", c=8)

              def g3(name):
                  return gt[name][:, 0:NG].rearrange("p (b c) -> p b c", c=4)
              gb = [gates_b]
              act(g3("tmpa"), pq3[:, :, 0:4], AF.Exp, pqb, gb, scale=-1.0)
              act(g3("sbn"), g3("tmpa"), AF.Ln, gb, gb, bias=epsc[:, 3:4])
              ts1("dve", gt["tmpa"][:, 0:NG], gt["tmpa"][:, 0:NG], 1.0, ALU.add, gb, gb)
              tk.op("dve", lambda e: e.reciprocal(out=gt["beta"][:, 0:NG], in_=gt["tmpa"][:, 0:NG]), gb, gb)
              tt("dve", g3("tmpb"), pq3[:, :, 4:8], pbc[:, BC + 1040:BC + 1040 + NG].rearrange("p (b c) -> p b c", c=4), ALU.add, pqb + gb, gb)
              act(gt["tmpb"][:, 0:NG], gt["tmpb"][:, 0:NG], AF.Exp, gb, gb)
              act(gt["tmpb"][:, 0:NG], gt["tmpb"][:, 0:NG], AF.Ln, gb, gb, bias=epsc[:, 3:4])
              tt("dve", gt["g"][:, 0:NG], gt["tmpb"][:, 0:NG], nexpA[:, l, 0:NG], ALU.mult, gb, gb)
              pg, pgb = smallq()
              mm(pg[:, 0:NG], Mu_f, gt["g"][:, 0:NG], True, True, gb, pgb, sig=False)
              mm(pg[:, 16:16 + NG], Bd_f, gt["g"][:, 0:NG], True, True, gb, pgb, sig=False)
              mm(pg[:, 32:32 + NG], E0_f, gt["g"][:, 0:NG], True, True, gb, pgb, sig=False)
              mm(pg[:, 48:48 + NG], E1_f, gt["g"][:, 0:NG], True, True, gb, pgb)
              cp("dve", gt["gc"][:, 0:NG], pg[:, 0:NG], pgb, gb)
              ts1("dve", gt["ngc"][:, 0:NG], gt["gc"][:, 0:NG], -1.0, ALU.mult, gb, gb)
              stt("dve", gt["gcb"][:, 0:NG], gt["sbn"][:, 0:NG], -1.0, gt["gc"][:, 0:NG], ALU.mult, ALU.add, gb, gb)
              act(gt["eg"][:, 0:NG], pg[:, 0:NG], AF.Exp, pgb, gb)
              tt("dve", gt["bEg"][:, 0:NG], gt["eg"][:, 0:NG], gt["beta"][:, 0:NG], ALU.mult, gb, gb)
              tt("dve", gt["tmpa"][:, 0:NG], pg[:, 16:16 + NG], gt["ngc"][:, 0:NG], ALU.add, pgb + gb, gb)
              act(gt["tmpa"][:, 0:NG], gt["tmpa"][:, 0:NG], AF.Exp, gb, gb)
              ts1("dve", gt["Ed0"][:, 0:NG], gt["tmpa"][:, 0:NG], E0_f[:, 0:1], ALU.mult, gb, gb)
              ts1("dve", gt["Ed1"][:, 0:NG], gt["tmpa"][:, 0:NG], E1_f[:, 0:1], ALU.mult, gb, gb)
              act(egl[:, 0:NG], pg[:, 32:32 + NG], AF.Exp, pgb, gb)
              act(egl[:, 16:16 + NG], pg[:, 48:48 + NG], AF.Exp, pgb, gb)
              wp, wpb = wnext()
              for mi in range(4):
                  ps = bigps()
                  for k in range(8):
                      mm(ps.t[:, 0:N], wp[:, k * 512 + mi * 128:k * 512 + (mi + 1) * 128], xTb[:, k, 0:N],
                         k == 0, k == 7, wpb + [xTb_b[k]], ps.b(), sig=(k == 7))
                  act(hid[:, 16 + mi, 0:N], ps.t[:, 0:N], AF.Gelu, ps.b(), [hid_b[16 + mi]])
              wp, wpb = wnext()
              for blk in range(NB):
                  ps = bigps()
                  for k in range(8):
                      mm(ps.t[:, 0:512], xTb[:, k, blk * 128:(blk + 1) * 128], wp[:, k * 512:(k + 1) * 512],
                         k == 0, k == 7, wpb + [xTb_b[k]], ps.b(), sig=(k == 7))
                  vi = nxt("vg", 2)
                  v_, vb_ = vg[vi], [vg_b[vi]]
                  act(v_[:], ps.t[:, 0:512], AF.Gelu, ps.b(), vb_)
                  tk.op("dve", lambda e: e.bn_stats(out=st6[:, 0:6], in_=v_[:]), vb_, [mv_b])
                  tk.op("dve", lambda e: e.bn_aggr(out=mv[:, 0:2], in_=st6[:, 0:6]), [mv_b], [mv_b])
                  rsq(mv[:, 2:3], mv[:, 1:2], 0, [mv_b], [mv_b])
                  stt("dve", mv[:, 3:4], mv[:, 0:1], -1.0, mv[:, 2:3], ALU.mult, ALU.mult, [mv_b], [mv_b])
                  act(v_[:], v_[:], AF.Identity, vb_ + [mv_b], vb_, bias=mv[:, 3:4], scale=mv[:, 2:3])
                  tt("dve", v_[:], v_[:], pbc[:, BC:BC + 512], ALU.mult, vb_, vb_)
                  if is_sample:
                      tt("dve", v_[:], v_[:], pbc[:, BC + 512:BC + 1024], ALU.add, vb_, vb_)
                      cp("pool", hid[:, 20 + blk, :], v_[:], vb_, [hid_b[20 + blk]])
                      misc_store(nvs[l], v_[0:64, :], vb_)
                  else:
                      tt("pool", hid[:, 20 + blk, :], v_[:], pbc[:, BC + 512:BC + 1024], ALU.add, vb_, [hid_b[20 + blk]])

              chk(l, "p2")
              for c in range(8):
                  sq, sqb = sBn()
                  act(sq[:, 0:N], hid[:, c, 0:N], AF.Square, [hid_b[c]], sqb)
                  ps = bigps()
                  if c < 4:
                      mm(ps.t[:, 0:N], ones128b[:], sq[:, 0:N], True, True, sqb, ps.b())
                      eps_ = 2
                  else:
                      mm(ps.t[:, 0:N], onesb[:], sq[:, 0:N], True, True, sqb, ps.b())
                      eps_ = 1
                  t, tb = sFn()
                  rsq(t[:, 0:N], ps.t[:, 0:N], eps_, ps.b(), tb)
                  tt("pool", hid[:, c, 0:N], hid[:, c, 0:N], t[:, 0:N], ALU.mult, [hid_b[c]] + tb, [hid_b[c]])

              chk(l, "g1")
              for blk in range(NB):
                  bs = slice(blk * 128, (blk + 1) * 128)
                  chk(l, "g5b%d" % blk)
                  for h in range(4):
                      H = hd[h]
                      idx = blk * 4 + h
                      qb_, kb_, vb2_ = [hid_b[h]], [hid_b[4 + h]], [hid_b[8 + h]]
                      di = nxt("dg", 2)
                      ts1("pool", dgp[di][:, 0:128], identb[:], gt["gc"][:, idx:idx + 1], ALU.mult, gb, [dgp_b[di]])
                      ts1("pool", dgp[di][:, 128:256], identb[:], gt["gcb"][:, idx:idx + 1], ALU.mult, gb, [dgp_b[di]])
                      ps = bigps()
                      mm(ps.t[:, 0:256], ones_f, dgp[di][:], True, True, [dgp_b[di]], ps.b(0, 256), sig=False)
                      mm(ps.t[:, 256:512], hid[:, 4 + h, bs], hid[:, 0:8, :].rearrange("p (a h) n -> p a h n", a=2)[:, :, h, bs], True, True, qb_ + kb_, ps.b(256, 512))
                      ti_ = nxt("tmp", 2)
                      stt("dve", tmpp[ti_][:], ps.t[:, 0:256], gt["ngc"][:, idx:idx + 1], maskneg2, ALU.add, ALU.add,
                          ps.b(0, 256) + gb, [tmpp_b[ti_]])
                      ei = nxt("E", 2)
                      act(Ep[ei][:], tmpp[ti_][:], AF.Exp, [tmpp_b[ti_]], [Ep_b[ei]])
                      act(H["EG"][:], ps.t[:, 0:128], AF.Exp, ps.b(0, 128), [H["EG_b"]])
                      yx0, yx0b = H["YX"][0], [H["YX_b"][0]]
                      tt("dve", H["attnT"][:], ps.t[:, 256:384], Ep[ei][:, 0:128], ALU.mult, ps.b(256, 384) + [Ep_b[ei]], [H["attnT_b"]])
                      tt("dve", yx0[:, 0:128], ps.t[:, 384:512], Ep[ei][:, 128:256], ALU.mult, ps.b(384, 512) + [Ep_b[ei]], yx0b)
                      stt("dve", H["P"][:], yx0[:, 0:128], -1.0, identb[:], ALU.mult, ALU.add, yx0b, [H["P_b"]])
                      tt("pool", H["qdT"][:], hid[:, h, bs], H["EG"][:], ALU.mult, qb_ + [H["EG_b"]], [H["qdT_b"]])
                  pt, ptb = pTbank()
                  for h in range(4):
                      H = hd[h]
                      mm(pt[:, h * 128:(h + 1) * 128], H["YX"][0][:, 0:128], identb[:], True, True, [H["YX_b"][0]], ptb, sig=(h == 3))
                  for h in range(4):
                      H = hd[h]
                      cp("act", H["YX"][0][:, 128:256], pt[:, h * 128:(h + 1) * 128], ptb, [H["YX_b"][0]])
                  pk, pkb = pTbank()
                  for h in range(4):
                      mm(pk[:, h * 128:(h + 1) * 128], hid[:, 4 + h, bs], identb[:], True, True, [hid_b[4 + h]], pkb, sig=(h == 3))
                  for h in range(4):
                      H = hd[h]
                      idx = blk * 4 + h
                      pks = pk[:, h * 128:(h + 1) * 128]
                      ts1("dve", H["kbg"][:], pks, gt["bEg"][:, idx:idx + 1], ALU.mult, pkb + gb, [H["kbg_b"]])
                      ts1("dve", H["kd0"][:], pks, gt["Ed0"][:, idx:idx + 1], ALU.mult, pkb + gb, [H["kd0_b"]])
                      ts1("dve", H["kd1"][:], pks, gt["Ed1"][:, idx:idx + 1], ALU.mult, pkb + gb, [H["kd1_b"]])
                  pv, pvb = pTbank()
                  for h in range(4):
                      mm(pv[:, h * 128:(h + 1) * 128], hid[:, 8 + h, bs], identb[:], True, True, [hid_b[8 + h]], pvb, sig=(h == 3))
                  for h in range(4):
                      H = hd[h]
                      idx = blk * 4 + h
                      ts1("dve", H["vb"][:], pv[:, h * 128:(h + 1) * 128], gt["beta"][:, idx:idx + 1], ALU.mult, pvb + gb, [H["vb_b"]])
                  chk(l, "g2b%d" % blk)
                  for k in range(6):
                      for h in range(4):
                          H = hd[h]
                          cur, curb = H["YX"][k % 2], [H["YX_b"][k % 2]]
                          nx_, nxb = H["YX"][(k + 1) % 2], [H["YX_b"][(k + 1) % 2]]
                          ps = bigps()
                          if k >= 1:
                              mm(ps.t[:, 0:128], cur[:, 128:256], H["P"][:], True, True, curb + [H["P_b"]], ps.b(0, 128), sig=(k == 5))
                          if k <= 4:
                              mm(ps.t[:, 128:256], cur[:, 128:256], cur[:, 0:128], True, True, curb, ps.b(128, 256), sig=False)
                              mm(ps.t[:, 256:384], cur[:, 0:128], cur[:, 128:256], True, True, curb, ps.b(256, 384))
                              cp("dve", nx_[:], ps.t[:, 128:384], ps.b(128, 384), nxb)
                          if k >= 1:
                              tt("dve", H["P"][:], H["P"][:], ps.t[:, 0:128], ALU.add, [H["P_b"]] + ps.b(0, 128), [H["P_b"]])
                      chk(l, "n%db%d" % (k, blk))
                  for h in range(4):
                      H = hd[h]
                      pq, pqb = smallq()
                      mm(pq, H["kbg"][:], H["P"][:], True, True, [H["kbg_b"], H["P_b"]], pqb)
                      act(H["nwT"][:], pq, AF.Identity, pqb, [H["nwT_b"]], scale=-1.0)
                  chk(l, "g3b%d" % blk)
                  for i in range(2):
                      cs = slice(i * 64, (i + 1) * 64)
                      pqs = []
                      for h in range(4):
                          H = hd[h]
                          pq, pqb = smallq()
                          mm(pq, H["P"][:], H["vb"][:], True, False, [H["P_b"], H["vb_b"]], pqb, sig=False)
                          mm(pq, H["nwT"][:], Sbf[:, l, h, :], False, True, [H["nwT_b"], Sbf_b[l][h]], pqb)
                          pqs.append((pq, pqb))
                      for h in range(4):
                          H = hd[h]
                          pq, pqb = pqs[h]
                          cp("act", H["vnew"][:], pq, pqb, [H["vnew_b"]])
                      pq2s = []
                      for h in range(4):
                          H = hd[h]
                          oc = slice(h * 128 + i * 64, h * 128 + (i + 1) * 64)
                          mm(psO.t[:, oc], Sbf[:, l, h, :], H["qdT"][:, cs], True, False, [Sbf_b[l][h], H["qdT_b"]], [psO.q[h]], sig=False)
                          mm(psO.t[:, oc], H["vnew"][:], H["attnT"][:, cs], False, True, [H["vnew_b"], H["attnT_b"]], [psO.q[h]], sig=False)
                          pq2, pq2b = smallq()
                          kd = H["kd0"] if i == 0 else H["kd1"]
                          kdb = H["kd0_b"] if i == 0 else H["kd1_b"]
                          mm(pq2, kd[:], H["vnew"][:], True, True, [kdb, H["vnew_b"]], pq2b)
                          pq2s.append((pq2, pq2b))
                      for h in range(4):
                          idx = blk * 4 + h
                          pq2, pq2b = pq2s[h]
                          stt("dve", S32[:, l, h, :], S32[:, l, h, :], egl[:, i * 16 + idx:i * 16 + idx + 1], pq2, ALU.mult, ALU.add,
                              [S32_b[l][h]] + pq2b + gb, [S32_b[l][h]])
                          cp("pool", Sbf[:, l, h, :], S32[:, l, h, :], [S32_b[l][h]], [Sbf_b[l][h]])
                      if is_sample and i == 0:
                          misc_store(nds[l].rearrange("h d e -> d h e"), S32[:, l], S32_b[l])
                  chk(l, "g4b%d" % blk)
                  sq, sqb = sBn()
                  act(sq[:, 0:512], psO.t[:, 0:512], AF.Square, psO.b(), sqb)
                  ps = bigps()
                  mm(ps.t[:, 0:512], onesrmsb[:], sq[:, 0:512], True, True, sqb, ps.b())
                  t, tb = sFn()
                  rsq(t[:, 0:512], ps.t[:, 0:512], 1, ps.b(), tb)
                  tt("dve", t[:, 0:512], t[:, 0:512], psO.t[:, 0:512], ALU.mult, tb + psO.b(), tb)
                  stt("dve", qkv_pre[:, 0:4, bs], t[:, 0:512].rearrange("p (h c) -> p h c", h=4), pcol[:, PC + 80:PC + 81],
                      hid[:, 12:16, bs], ALU.mult, ALU.mult, tb + hid_b[12:16], qp_b[0:4])
              if last_tile and not is_sample:
                  misc_store(ndp[l, s].rearrange("h d e -> d h e"), S32[:, l], S32_b[l])

              chk(l, "gdn")
              for g in range(4):
                  ps = bigps()
                  for blk in range(NB):
                      bs = slice(blk * 128, (blk + 1) * 128)
                      mm(ps.t[:, bs], onesb[:], bsmat[:, l * 4 + g, :], True, False, (), ps.b(), sig=False)
                      mm(ps.t[:, bs], hid[:, 20 + blk, g * 128:(g + 1) * 128], WsT[:, l * 4 + g, :], False, True, [hid_b[20 + blk]], ps.b())
                  tt("dve", qkv_pre[:, 4 + g, 0:N], ps.t[:, 0:N], hid[:, 16 + g, 0:N], ALU.mult, ps.b() + [hid_b[16 + g]], [qp_b[4 + g]])

              chk(l, "mix")
              for m in range(8):
                  if m % 4 == 0:
                      wp, wpb = wnext()
                  mi = m % 4
                  ps = bigps()
                  for k in range(8):
                      mm(ps.t[:, 0:N], wp[:, k * 512 + mi * 128:k * 512 + (mi + 1) * 128], qkv_pre[:, k, 0:N],
                         k == 0, k == 7, wpb + [qp_b[k]], ps.b(), sig=(k == 7))
                  stt("dve", xT[:, m, 0:N], xT[:, m, 0:N], ALPHA, ps.t[:, 0:N], ALU.mult, ALU.add, [xT_b[m]] + ps.b(), [xT_b[m]])
              chk(l, "res1")
              ln_fm(N, PC + 0, PC + 8)
              chk(l, "ln1")

              for j in range(8):
                  wp, wpb = wnext()
                  for fi in range(4):
                      f = j * 4 + fi
                      ps = bigps()
                      for k in range(8):
                          mm(ps.t[:, 0:N], wp[:, k * 512 + fi * 128:k * 512 + (fi + 1) * 128], xTb[:, k, 0:N],
                             k == 0, k == 7, wpb + [xTb_b[k]], ps.b(), sig=(k == 7))
                      t, tb = sFn()
                      act(t[:, 0:N], ps.t[:, 0:N], AF.Relu, ps.b(), tb)
                      tt("pool" if f % 2 else "dve", hid[:, f, 0:N], t[:, 0:N], t[:, 0:N], ALU.mult, tb, [hid_b[f]])
              for m in range(8):
                  wp, wpb = wnext()
                  ps = bigps()
                  for f in range(32):
                      mm(ps.t[:, 0:N], wp[:, f * 128:(f + 1) * 128], hid[:, f, 0:N], f == 0, f == 31, wpb + [hid_b[f]], ps.b(), sig=(f == 31))
                  stt("dve", xT[:, m, 0:N], xT[:, m, 0:N], ALPHA, ps.t[:, 0:N], ALU.mult, ALU.add, [xT_b[m]] + ps.b(), [xT_b[m]])
              chk(l, "res2")
              ln_fm(N, PC + 16, PC + 24)
              chk(l, "ln2")

          for blk in range(NB):
              slot = nxt("vg", 2)
              yo, yob = xin[slot], [xin_b[slot]]
              for half in range(2):
                  ps = bigps()
                  for cc in range(4):
                      c = half * 4 + cc
                      mm(ps.t[:, cc * 128:(cc + 1) * 128], xT[:, c, blk * 128:(blk + 1) * 128], ident_f, True, True, [xT_b[c]], [ps.q[cc]], sig=(cc == 3))
                  cp("act" if half else "dve", yo[:, half * 512:(half + 1) * 512], ps.t[:, 0:512], ps.b(), yob)
              if is_sample:
                  tk.dma("sp", ys, yo[0:64, :], "sty%d" % slot, yob, ())
              else:
                  r0 = ti * 512 + blk * 128
                  tk.dma("sp", yp[s, r0:r0 + 128, :], yo[:], "sty%d" % slot, yob, ())

    except _Stop:
        pass

    for key in ("sty0", "sty1", "misc"):
        tk.wait_all("sp", key)
    _LAST_CNT.clear()
    _LAST_CNT["sbuf_free"] = nc.sbuf_bytes_remaining
    _LAST_CNT.update(tk.cnt)
    return nc


def _consts():
    c = np.zeros((128, NCONST), np.float32)
    i = np.arange(128)
    P = i[:, None]
    Fr = i[None, :]
    same = (P // 64) == (Fr // 64)
    c[:, 0:128] = np.eye(128)
    c[:, 128:256] = 1.0
    c[:, 256:384] = (same & (P <= Fr))
    c[:, 384:512] = same
    c[:, 512:640] = (P < 64) * np.ones((1, 128))
    c[:, 640:768] = (P >= 64) * np.ones((1, 128))
    c[:, 768:896] = np.where(same & (P <= Fr), 0.0, NEG)
    c[:, 896:1024] = np.where(same & (P < Fr), 0.0, NEG)
    c[:, 1024:1152] = (P <= Fr)
    return c


def _pack_small(inp):
    f = lambda a: np.asarray(a, np.float32)
    col = lambda v: f(v).reshape(-1, 128).T
    pcol = np.zeros((128, 16 + NL * LPC), np.float32)
    pcol[:, 0:8] = col(inp["ln0_g"])
    pcol[:, 8:16] = col(inp["ln0_b"])
    pbc = np.zeros((128, NL * LBC), np.float32)
    for l in range(NL):
        b = 16 + l * LPC
        pcol[:, b:b + 8] = col(inp["ln1_g"][l])
        pcol[:, b + 8:b + 16] = col(inp["ln1_b"][l])
        pcol[:, b + 16:b + 24] = col(inp["ln2_g"][l])
        pcol[:, b + 24:b + 32] = col(inp["ln2_b"][l])
        cw = f(inp["conv_w"][l])
        pcol[:, b + 32:b + 80] = cw.reshape(4, 12, 128).transpose(2, 1, 0).reshape(128, 48)
        pcol[:, b + 80] = f(inp["gdn_norm_g"][l])
        o = l * LBC
        pbc[:, o:o + 512] = f(inp["smlp_ln_g"][l])[None, :]
        pbc[:, o + 512:o + 1024] = f(inp["smlp_ln_b"][l])[None, :]
        pbc[:, o + 1024:o + 1040] = np.tile(f(inp["a_log"][l]), 4)[None, :]
        pbc[:, o + 1040:o + 1056] = np.tile(f(inp["dt_bias"][l]), 4)[None, :]
    bsrow = f(inp["b_s"]).reshape(1, NL * 4 * 128)
    ws = f(inp["w_s"]).reshape(NL * 4, 128, 128)
    return pcol, pbc, bsrow, ws


SEQUENTIAL_LAUNCH = False
_NC_CACHE = {}
_NOCAST = [0]
_DELAY = [0]
_LAST_CNT = {}
_STOP = [None]


def run_cores(inp, TP, NSP, ncores):
    key = (TP, NSP, _STOP[0])
    if key not in _NC_CACHE:
        _NC_CACHE[key] = build(TP, NSP, _STOP[0])
    nc = _NC_CACHE[key]
    f = lambda a: np.ascontiguousarray(np.asarray(a, np.float32))
    pcol, pbc, bsrow, ws = _pack_small(inp)
    consts = _consts()
    shared = {"w_in": f(inp["w_in"]), "w_out": f(inp["w_out"]), "w_up": f(inp["w_up"]), "w_down": f(inp["w_down"]),
              "ws": ws, "consts": consts, "pcol": pcol, "pbc": pbc, "bsrow": bsrow}
    xp = f(inp["x_prompt"])
    xs = f(inp["x_sample"])
    cc = f(inp["cache_conv"])
    sd = f(inp["state_delta"])
    in_maps = []
    for i in range(ncores):
        m = dict(shared)
        m["xp"] = np.ascontiguousarray(xp[i * NSP:(i + 1) * NSP])
        m["xs"] = np.ascontiguousarray(xs[i])
        m["cconv"] = np.ascontiguousarray(cc[:, i])
        m["sdelta"] = np.ascontiguousarray(sd[:, i])
        in_maps.append(m)
    if SEQUENTIAL_LAUNCH and ncores > 1:
        R = []
        for i in range(ncores):
            R.append(run_bass_kernel_spmd(nc, [in_maps[i]], core_ids=[0]).results[0])
    else:
        R = run_bass_kernel_spmd(nc, in_maps, core_ids=list(range(ncores))).results
    y_p = np.concatenate([r["yp"] for r in R], axis=0)
    y_s = np.stack([r["ys"] for r in R], axis=0)
    ncp = np.concatenate([r["ncp"] for r in R], axis=1)
    ndp = np.concatenate([r["ndp"] for r in R], axis=1)
    ncs = np.stack([r["ncs"] for r in R], axis=1)
    nds = np.stack([r["nds"] for r in R], axis=1)
    nvs = np.stack([r["nvs"] for r in R], axis=1)
    return tuple(np.ascontiguousarray(a, dtype=np.float32) for a in (y_p, y_s, ncp, ndp, ncs, nds, nvs))


def kernel(**inputs):
    return run_cores(inputs, 4096, 2, 8)
```

```python
import numpy as np
import concourse.bass as bass
import concourse.mybir as mybir
from concourse.bass_utils import run_bass_kernel_spmd

F32 = mybir.dt.float32
BF16 = mybir.dt.bfloat16
AF = mybir.ActivationFunctionType
ALU = mybir.AluOpType

NL = 2
D = 1024
INW = 3080
ALPHA = float((2 * NL) ** 0.25)
LN_EPS = 1e-5
NEPS = 1e-6
NEG = -30000.0
NWP = 8
NBIG = 5
LPC = 81
LBC = 1056
NCONST = 1152


class Buf:
    __slots__ = ("w", "r", "name")

    def __init__(self, name=""):
        self.w = None
        self.r = {}
        self.name = name


class TK:
    def __init__(self, nc):
        self.nc = nc
        self.E = {"pe": nc.tensor, "act": nc.scalar, "dve": nc.vector, "pool": nc.gpsimd, "sp": nc.sync}
        self.sem = {}
        self.cnt = {}
        self.waited = {e: {} for e in self.E}
        for e in self.E:
            self.newsem(e)

    def newsem(self, key):
        self.sem[key] = self.nc.alloc_semaphore(name="s_" + key)
        self.cnt[key] = 0

    def _needs(self, reads, writes):
        nd = {}
        for b in reads:
            if b.w is not None:
                k, v = b.w
                if nd.get(k, 0) < v:
                    nd[k] = v
        for b in writes:
            if b.w is not None:
                k, v = b.w
                if nd.get(k, 0) < v:
                    nd[k] = v
            for k, v in b.r.items():
                if nd.get(k, 0) < v:
                    nd[k] = v
        return nd

    def _wait(self, e, nd):
        w = self.waited[e]
        for k, v in nd.items():
            if e == "pe" and k == "pe":
                continue
            if w.get(k, 0) < v:
                self.E[e].wait_ge(self.sem[k], v)
                w[k] = v

    def op(self, e, fn, reads=(), writes=(), sig=True):
        self._wait(e, self._needs(reads, writes))
        ins = fn(self.E[e])
        if sig:
            self.cnt[e] += 1
            ins.then_inc(self.sem[e], 1)
            c = self.cnt[e]
        else:
            c = self.cnt[e] + 1
        for b in reads:
            if b.r.get(e, 0) < c:
                b.r[e] = c
        for b in writes:
            b.w = (e, c)
            b.r = {}

    def dma(self, q, out, in_, key, reads=(), writes=()):
        self._wait(q, self._needs(reads, writes))
        self.E[q].dma_start(out=out, in_=in_).then_inc(self.sem[key], 16)
        self.cnt[key] += 16
        c = self.cnt[key]
        for b in reads:
            if b.r.get(key, 0) < c:
                b.r[key] = c
        for b in writes:
            b.w = (key, c)
            b.r = {}

    def wait_all(self, e, key):
        v = self.cnt[key]
        if v > 0 and self.waited[e].get(key, 0) < v:
            self.E[e].wait_ge(self.sem[key], v)
            self.waited[e][key] = v


class PSB:
    def __init__(self, t, name):
        self.t = t
        self.whole = Buf(name)
        self.q = [self.whole] * 4

    def b(self, c0=0, c1=512):
        return [self.whole]


class _Stop(Exception):
    pass


def build(TP, NSP=2, stop=None):
    nc = bass.Bass("TRN2", target_bir_lowering=False)
    dt = nc.dram_tensor
    xp = dt("xp", [NSP, TP, D], F32, kind="ExternalInput").ap()
    xs = dt("xs", [64, D], F32, kind="ExternalInput").ap()
    cconv = dt("cconv", [NL, 3, 1536], F32, kind="ExternalInput").ap()
    sdelta = dt("sdelta", [NL, 4, 128, 128], F32, kind="ExternalInput").ap()
    w_in = dt("w_in", [NL, D, INW], F32, kind="ExternalInput").ap()
    w_out = dt("w_out", [NL, D, D], F32, kind="ExternalInput").ap()
    w_up = dt("w_up", [NL, D, 4096], F32, kind="ExternalInput").ap()
    w_down = dt("w_down", [NL, 4096, D], F32, kind="ExternalInput").ap()
    ws_d = dt("ws", [NL * 4, 128, 128], F32, kind="ExternalInput").ap()
    consts_d = dt("consts", [128, NCONST], F32, kind="ExternalInput").ap()
    pcol_d = dt("pcol", [128, 16 + NL * LPC], F32, kind="ExternalInput").ap()
    pbc_d = dt("pbc", [128, NL * LBC], F32, kind="ExternalInput").ap()
    bsrow_d = dt("bsrow", [1, NL * 4 * 128], F32, kind="ExternalInput").ap()

    yp = dt("yp", [NSP, TP, D], F32, kind="ExternalOutput").ap()
    ys = dt("ys", [64, D], F32, kind="ExternalOutput").ap()
    ncp = dt("ncp", [NL, NSP, 3, 1536], F32, kind="ExternalOutput").ap()
    ndp = dt("ndp", [NL, NSP, 4, 128, 128], F32, kind="ExternalOutput").ap()
    ncs = dt("ncs", [NL, 3, 1536], F32, kind="ExternalOutput").ap()
    nds = dt("nds", [NL, 4, 128, 128], F32, kind="ExternalOutput").ap()
    nvs = dt("nvs", [NL, 64, 512], F32, kind="ExternalOutput").ap()

    wsc = dt("wsc", [NL, 24, 128, 4096], BF16, kind="Internal").ap()
    wbasc = dt("wbasc", [NL, 128, 64], BF16, kind="Internal").ap()

    tk = TK(nc)
    _n = [0]

    def sb(shape, dtype, name=None):
        _n[0] += 1
        return nc.alloc_sbuf_tensor("sb_" + (name or ("t%d" % _n[0])), shape, dtype)

    cst = sb([128, NCONST], F32, "cst")
    pcol = sb([128, 16 + NL * LPC], F32, "pcol")
    pbc = sb([128, NL * LBC], F32, "pbc")
    epsc = sb([128, 4], F32, "epsc")
    identb = sb([128, 128], BF16, "identb")
    onesb = sb([128, 128], BF16, "onesb")
    ones128b = sb([128, 128], BF16, "ones128b")
    ones1kb = sb([128, 128], BF16, "ones1kb")
    onesrmsb = sb([128, 128], BF16, "onesrmsb")
    WsT = sb([128, 8, 128], BF16, "WsT")
    bsmat = sb([128, 8, 128], BF16, "bsmat")
    nexpA = sb([128, NL, 16], F32, "nexpA")
    wba = sb([128, NL, 64], BF16, "wba")
    wpool = [sb([128, 4096], BF16, "wp%d" % i) for i in range(NWP)]
    wpool_b = [Buf("wp%d" % i) for i in range(NWP)]
    xin = [sb([128, 1024], F32, "xin%d" % i) for i in range(2)]
    xin_b = [Buf("xin%d" % i) for i in range(2)]
    xT = sb([128, 8, 512], F32, "xT")
    xT_b = [Buf("xT%d" % i) for i in range(8)]
    xTb = sb([128, 8, 512], BF16, "xTb")
    xTb_b = [Buf("xTb%d" % i) for i in range(8)]
    qkv_pre = sb([128, 12, 515], BF16, "qkv_pre")
    qp_b = [Buf("qp%d" % i) for i in range(12)]
    tails = sb([128, NL, 12, 3], BF16, "tails")
    tails_b = [Buf("tails%d" % i) for i in range(NL)]
    diag = [sb([128, 4, 128], BF16, "diag%d" % i) for i in range(2)]
    diag_b = [Buf("diag%d" % i) for i in range(2)]
    hid = sb([128, 32, 512], BF16, "hid")
    hid_b = [Buf("hid%d" % i) for i in range(32)]
    vg = [sb([128, 512], F32, "vg%d" % i) for i in range(2)]
    vg_b = [Buf("vg%d" % i) for i in range(2)]
    sF = [sb([128, 512], F32, "sF%d" % i) for i in range(4)]
    sF_b = [Buf("sF%d" % i) for i in range(4)]
    sB = [sb([128, 512], BF16, "sB%d" % i) for i in range(2)]
    sB_b = [Buf("sB%d" % i) for i in range(2)]
    rstd_t = sb([128, 512], F32, "rstd")
    nmr_t = sb([128, 512], F32, "nmr")
    ln_b = Buf("lnstat")
    S32 = sb([128, NL, 4, 128], F32, "S32")
    Sbf = sb([128, NL, 4, 128], BF16, "Sbf")
    S32_b = [[Buf("S32_%d%d" % (l, h)) for h in range(4)] for l in range(NL)]
    Sbf_b = [[Buf("Sbf_%d%d" % (l, h)) for h in range(4)] for l in range(NL)]
    gnames = ["beta", "g", "sbn", "gc", "ngc", "gcb", "bEg", "Ed0", "Ed1", "eg", "tmpa", "tmpb"]
    gt = {n: sb([128, 16], F32, "g_" + n) for n in gnames}
    egl = sb([128, 32], F32, "g_egl")
    gates_b = Buf("gates")
    mv = sb([128, 16], F32, "mv")
    st6 = sb([128, 12], F32, "st6")
    mv_b = Buf("mv")
    hd = []
    for h in range(4):
        d_ = {}
        for n in ["attnT", "P", "kbg", "kd0", "kd1", "vb", "nwT", "qdT", "vnew", "EG"]:
            d_[n] = sb([128, 128], BF16, "h%d_%s" % (h, n))
            d_[n + "_b"] = Buf("h%d_%s" % (h, n))
        d_["YX"] = [sb([128, 256], BF16, "h%d_YX%d" % (h, i)) for i in range(2)]
        d_["YX_b"] = [Buf("h%d_YX%d" % (h, i)) for i in range(2)]
        hd.append(d_)
    dgp = [sb([128, 256], F32, "dg%d" % i) for i in range(2)]
    dgp_b = [Buf("dg%d" % i) for i in range(2)]
    tmpp = [sb([128, 256], F32, "tmpE%d" % i) for i in range(2)]
    tmpp_b = [Buf("tmpE%d" % i) for i in range(2)]
    Ep = [sb([128, 256], F32, "E%d" % i) for i in range(2)]
    Ep_b = [Buf("E%d" % i) for i in range(2)]

    big = [PSB(nc.alloc_psum_tensor("pbig%d" % i, [128, 512], F32), "pbig%d" % i) for i in range(5)]
    psO = PSB(nc.alloc_psum_tensor("psO", [128, 512], F32), "psO")
    pT = [nc.alloc_psum_tensor("pT%d" % i, [128, 512], F32) for i in range(2)]
    pT_b = [Buf("pT%d" % i) for i in range(2)]
    rot = {"big": 0, "small": 0, "pT": 0, "sF": 0, "sB": 0, "vg": 0, "dg": 0, "tmp": 0, "E": 0, "diag": 0}

    def nxt(key, n):
        i = rot[key]
        rot[key] = (i + 1) % n
        return i

    def bigps():
        return big[nxt("big", NBIG)]

    def smallq():
        p_ = bigps()
        return p_.t[:, 0:128], p_.b()

    def pTbank():
        i = nxt("pT", 2)
        return pT[i], [pT_b[i]]

    def sFn():
        i = nxt("sF", 4)
        return sF[i], [sF_b[i]]

    def sBn():
        i = nxt("sB", 2)
        return sB[i], [sB_b[i]]

    def mm(out, lhsT, rhs, start, stop, reads, writes, sig=True):
        tk.op("pe", lambda e: e.matmul(out, lhsT=lhsT, rhs=rhs, start=start, stop=stop), reads, writes, sig)

    def act(out, in_, func, reads, writes, bias=None, scale=None):
        kw = {}
        if bias is not None:
            kw["bias"] = bias
        if scale is not None:
            kw["scale"] = scale
        tk.op("act", lambda e: e.activation(out=out, in_=in_, func=func, **kw), reads, writes)

    def ts(eng, out, in0, s1, s2, op0, op1, reads, writes):
        tk.op(eng, lambda e: e.tensor_scalar(out=out, in0=in0, scalar1=s1, scalar2=s2, op0=op0, op1=op1), reads, writes)

    def rsq(out, in0, epscol, reads, writes):
        act(out, in0, AF.Sqrt, reads, writes, bias=epsc[:, epscol:epscol + 1])
        tk.op("dve", lambda e: e.reciprocal(out=out, in_=out), writes, writes)

    def ts1(eng, out, in0, s1, op0, reads, writes):
        tk.op(eng, lambda e: e.tensor_single_scalar(out=out, in_=in0, scalar=s1, op=op0), reads, writes)

    def stt(eng, out, in0, s, in1, op0, op1, reads, writes):
        tk.op(eng, lambda e: e.scalar_tensor_tensor(out=out, in0=in0, scalar=s, in1=in1, op0=op0, op1=op1), reads, writes)

    def tt(eng, out, in0, in1, op, reads, writes):
        tk.op(eng, lambda e: e.tensor_tensor(out=out, in0=in0, in1=in1, op=op), reads, writes)

    def cp(eng, out, in_, reads, writes):
        if eng == "act":
            act(out, in_, AF.Copy, reads, writes)
        else:
            tk.op(eng, lambda e: e.tensor_copy(out=out, in_=in_), reads, writes)

    def mset(eng, ap, v, writes):
        tk.op(eng, lambda e: e.memset(ap, v), (), writes)

    ident_f = cst[:, 0:128]
    ones_f = cst[:, 128:256]
    Mu_f = cst[:, 256:384]
    Bd_f = cst[:, 384:512]
    E0_f = cst[:, 512:640]
    E1_f = cst[:, 640:768]
    maskneg2 = cst[:, 768:1024]
    triU = cst[:, 1024:1152]

    sc_b = [[Buf("sc%d_%d" % (l, g)) for g in range(4)] for l in range(NL)]
    win_cols = [0, 512, 1024, 1536, 2056, 2568]
    for l in range(NL):
        for g in range(4):
            tk.newsem("cv%d_%d" % (l, g))
    tk.newsem("cvba")
    wba_sc_b = Buf("wbasc")
    for l in range(NL if not _NOCAST[0] else 0):
        for pi, c0 in enumerate(win_cols):
            tk.dma("pool", wsc[l, pi].rearrange("p (k c) -> p k c", k=8),
                   w_in[l][:, c0:c0 + 512].rearrange("(k p) c -> p k c", p=128), "cv%d_0" % l, (), [sc_b[l][0]])
        tk.dma("pool", wbasc[l].rearrange("p (k c) -> p k c", k=8),
               w_in[l][:, 2048:2056].rearrange("(k p) c -> p k c", p=128), "cvba", (), [wba_sc_b])
        for j in range(2):
            tk.dma("pool", wsc[l, 6 + j].rearrange("p (k c) -> p k c", k=8),
                   w_out[l][:, j * 512:(j + 1) * 512].rearrange("(k p) c -> p k c", p=128), "cv%d_1" % l, (), [sc_b[l][1]])
        for j in range(8):
            tk.dma("pool", wsc[l, 8 + j].rearrange("p (k c) -> p k c", k=8),
                   w_up[l][:, j * 512:(j + 1) * 512].rearrange("(k p) c -> p k c", p=128), "cv%d_2" % l, (), [sc_b[l][2]])
        for m in range(8):
            tk.dma("pool", wsc[l, 16 + m].rearrange("p (f c) -> p f c", f=32),
                   w_down[l][:, m * 128:(m + 1) * 128].rearrange("(f p) c -> p f c", p=128), "cv%d_3" % l, (), [sc_b[l][3]])

    tk.newsem("const")
    cbuf = Buf("const")
    tk.dma("sp", cst[:], consts_d, "const", (), [cbuf])
    tk.dma("sp", pcol[:], pcol_d, "const", (), [cbuf])
    tk.dma("sp", pbc[:], pbc_d, "const", (), [cbuf])
    tk.dma("sp", sF[0][0:1, :], bsrow_d[:, 0:512], "const", (), [cbuf])
    tk.dma("sp", sF[1][0:1, :], bsrow_d[:, 512:1024], "const", (), [cbuf])
    tk.dma("sp", xin[0][:].rearrange("p (a q) -> p a q", a=8), ws_d.rearrange("a p q -> p a q"), "const", (), [cbuf, xin_b[0]])
    tk.dma("sp", wba[:], wbasc.rearrange("l p c -> p l c"), "const", [wba_sc_b], [cbuf])
    for e in ("pe", "act", "dve", "pool"):
        tk.wait_all(e, "const")

    setup_b = Buf("setup")
    cp("dve", identb[:], ident_f, (), [setup_b])
    mset("dve", onesb[:], 1.0, [setup_b])
    mset("dve", epsc[:, 0:1], LN_EPS, [setup_b])
    mset("dve", epsc[:, 1:2], NEPS, [setup_b])
    mset("dve", epsc[:, 2:3], 128.0 * NEPS, [setup_b])
    mset("dve", epsc[:, 3:4], 1.0, [setup_b])
    mset("dve", ones128b[:], 128.0, [setup_b])
    mset("dve", ones1kb[:], 1.0 / 1024.0, [setup_b])
    mset("dve", onesrmsb[:], 1.0 / 128.0, [setup_b])
    mset("dve", bsmat[:], 0.0, [setup_b])
    for a in range(8):
        src = sF[a // 4][0:1, (a % 4) * 128:(a % 4 + 1) * 128]
        cp("dve", bsmat[0:1, a, :], src, [setup_b], [setup_b])
    for a in range(8):
        pq, pqb = smallq()
        mm(pq, xin[0][:, a * 128:(a + 1) * 128], ident_f, True, True, [setup_b, xin_b[0]], pqb)
        tt("dve", WsT[:, a, :], pq, triU, ALU.mult, pqb, [setup_b])
    for l in range(NL):
        act(nexpA[:, l, :], pbc[:, l * LBC + 1024:l * LBC + 1040], AF.Exp, [setup_b], [setup_b])
        ts1("dve", nexpA[:, l, :], nexpA[:, l, :], -1.0, ALU.mult, [setup_b], [setup_b])
    for e in ("pe", "act", "pool", "dve"):
        tk._wait(e, {"dve": tk.cnt["dve"], "act": tk.cnt["act"], "pe": tk.cnt["pe"]})

    if _DELAY[0]:
        for _i in range(_DELAY[0]):
            tk.op("act", lambda e: e.activation(out=vg[0][:, 0:512], in_=vg[1][:, 0:512], func=AF.Copy), (), [vg_b[0]])
    NTP = TP // 512
    tiles = []
    for s in range(NSP):
        for ti in range(NTP):
            tiles.append((s, ti, 512))
    tiles.append((NSP, 0, 128))
    pieces = []
    for (_s, _ti, _n) in tiles:
        for l in range(NL):
            for pi in range(24):
                g = 0 if pi < 6 else (1 if pi < 8 else (2 if pi < 16 else 3))
                pieces.append((l, pi, g))
    for i in range(NWP):
        tk.newsem("wp%d" % i)
    wst = {"i": 0, "loaded": 0}

    def wnext():
        while wst["loaded"] < min(len(pieces), wst["i"] + NWP):
            j = wst["loaded"]
            l, pi, g = pieces[j]
            slot = j % NWP
            tk.dma("sp", wpool[slot][:], wsc[l, pi], "wp%d" % slot, [sc_b[l][g]], [wpool_b[slot]])
            wst["loaded"] += 1
        j = wst["i"]
        wst["i"] += 1
        return wpool[j % NWP], [wpool_b[j % NWP]]

    tk.newsem("xin0")
    tk.newsem("xin1")
    tk.newsem("sty0")
    tk.newsem("sty1")
    tk.newsem("misc")

    def misc_store(out, in_, reads):
        tk.dma("sp", out, in_, "misc", reads, ())
        tk.wait_all("sp", "misc")

    def misc_load(out, in_, writes):
        tk.dma("sp", out, in_, "misc", (), writes)
        tk.wait_all("sp", "misc")

    def barrier():
        snap = {k: tk.cnt[k] for k in ("pe", "act", "dve", "pool")}
        for e in ("pe", "act", "dve", "pool"):
            tk._wait(e, {k: v for k, v in snap.items() if v > 0})

    def ln_fm(N, gcol, bcol):
        psM = bigps()
        psQ = bigps()
        for m in range(8):
            cp("act", xTb[:, m, 0:N], xT[:, m, 0:N], [xT_b[m]], [xTb_b[m]])
            sq, sqb = sBn()
            act(sq[:, 0:N], xT[:, m, 0:N], AF.Square, [xT_b[m]], sqb)
            mm(psM.t[:, 0:N], ones1kb[:], xTb[:, m, 0:N], m == 0, m == 7, [xTb_b[m]], psM.b(), sig=(m == 7))
            mm(psQ.t[:, 0:N], ones1kb[:], sq[:, 0:N], m == 0, m == 7, sqb, psQ.b(), sig=True)
        t1, t1b = sFn()
        act(t1[:, 0:N], psM.t[:, 0:N], AF.Square, psM.b(), t1b)
        stt("dve", t1[:, 0:N], t1[:, 0:N], -1.0, psQ.t[:, 0:N], ALU.mult, ALU.add, t1b + psQ.b(), t1b)
        rsq(rstd_t[:, 0:N], t1[:, 0:N], 0, t1b, [ln_b])
        for m in range(8):
            t, tb = sFn()
            stt("dve", t[:, 0:N], xT[:, m, 0:N], 1.0, psM.t[:, 0:N], ALU.mult, ALU.subtract, [xT_b[m]] + psM.b(), tb)
            tt("dve", t[:, 0:N], t[:, 0:N], rstd_t[:, 0:N], ALU.mult, tb + [ln_b], tb)
            act(xT[:, m, 0:N], t[:, 0:N], AF.Identity, tb, [xT_b[m]],
                bias=pcol[:, bcol + m:bcol + m + 1], scale=pcol[:, gcol + m:gcol + m + 1])
            act(xTb[:, m, 0:N], t[:, 0:N], AF.Identity, tb, [xTb_b[m]],
                bias=pcol[:, bcol + m:bcol + m + 1], scale=pcol[:, gcol + m:gcol + m + 1])

    cur_seq = -1
    tile_no = -1

    def chk(l, name):
        if stop is not None and stop == (tile_no, l, name):
            raise _Stop()

    try:
      if stop == "setup":
          raise _Stop()
      for (s, ti, N) in tiles:
          tile_no += 1
          NB = N // 128
          is_sample = (s == NSP)
          n_real = 64 if is_sample else N
          last_tile = is_sample or (ti == NTP - 1)
          if s != cur_seq:
              cur_seq = s
              if not is_sample:
                  for l in range(NL):
                      mset("pool", S32[:, l], 0.0, S32_b[l])
                      mset("pool", Sbf[:, l], 0.0, Sbf_b[l])
                      mset("pool", tails[:, l], 0.0, [tails_b[l]])
              else:
                  for l in range(NL):
                      misc_load(S32[:, l], sdelta[l].rearrange("h d e -> d h e"), S32_b[l])
                      cp("pool", Sbf[:, l], S32[:, l], S32_b[l], Sbf_b[l])
                      pq, pqb = smallq()
                      for cb in range(3):
                          t, tb = sFn()
                          misc_load(t[0:3, :], cconv[l][:, cb * 512:(cb + 1) * 512], tb)
                          for cc in range(4):
                              c = cb * 4 + cc
                              mm(pq[:, c * 3:c * 3 + 3], t[0:3, cc * 128:(cc + 1) * 128], ident_f[0:3, 0:3], True, True, tb, pqb, sig=(cc == 3))
                      cp("dve", tails[:, l].rearrange("p c r -> p (c r)"), pq[:, 0:36], pqb, [tails_b[l]])

          for blk in range(NB):
              slot = nxt("vg", 2)
              xi, xib = xin[slot], [xin_b[slot]]
              if is_sample:
                  mset("pool", xi[:], 0.0, xib)
                  tk.dma("sp", xi[0:64, :], xs, "xin%d" % slot, (), xib)
              else:
                  r0 = ti * 512 + blk * 128
                  tk.dma("sp", xi[:], xp[s, r0:r0 + 128, :], "xin%d" % slot, (), xib)
              for hf in range(2):
                  tk.op("dve", lambda e, hf=hf: e.bn_stats(out=st6[:, hf * 6:(hf + 1) * 6], in_=xi[:, hf * 512:(hf + 1) * 512]), xib, [mv_b])
              tk.op("dve", lambda e: e.bn_aggr(out=mv[:, 0:2], in_=st6[:, 0:12]), [mv_b], [mv_b])
              rsq(mv[:, 2:3], mv[:, 1:2], 0, [mv_b], [mv_b])
              stt("dve", mv[:, 3:4], mv[:, 0:1], -1.0, mv[:, 2:3], ALU.mult, ALU.mult, [mv_b], [mv_b])
              act(xi[:], xi[:], AF.Identity, xib + [mv_b], xib, bias=mv[:, 3:4], scale=mv[:, 2:3])
              for half in range(2):
                  ps = bigps()
                  for cc in range(4):
                      c = half * 4 + cc
                      mm(ps.t[:, cc * 128:(cc + 1) * 128], xi[:, c * 128:(c + 1) * 128], ident_f, True, True, xib, [ps.q[cc]], sig=(cc == 3))
                  for cc in range(4):
                      c = half * 4 + cc
                      act(xT[:, c, blk * 128:(blk + 1) * 128], ps.t[:, cc * 128:(cc + 1) * 128], AF.Identity, [ps.q[cc]], [xT_b[c]],
                          bias=pcol[:, 8 + c:9 + c], scale=pcol[:, c:c + 1])
          for c in range(8):
              cp("pool", xTb[:, c, 0:N], xT[:, c, 0:N], [xT_b[c]], [xTb_b[c]])

          chk(0, "p1")
          for l in range(NL):
              PC = 16 + l * LPC
              BC = l * LBC
              for pc in range(3):
                  wp, wpb = wnext()
                  for mi in range(4):
                      c = pc * 4 + mi
                      ps = bigps()
                      for k in range(8):
                          mm(ps.t[:, 0:N], wp[:, k * 512 + mi * 128:k * 512 + (mi + 1) * 128], xTb[:, k, 0:N],
                             k == 0, k == 7, wpb + [xTb_b[k]], ps.b(), sig=(k == 7))
                      cp("act", qkv_pre[:, c, 3:3 + N], ps.t[:, 0:N], ps.b(), [qp_b[c]])
                  if last_tile:
                      ps = bigps()
                      for k in range(8):
                          mm(ps.t[0:3, 0:512], xTb[:, k, n_real - 3:n_real], wp[:, k * 512:(k + 1) * 512],
                             k == 0, k == 7, wpb + [xTb_b[k]], ps.b(), sig=(k == 7))
                      t, tb = sFn()
                      cp("act", t[0:3, :], ps.t[0:3, 0:512], ps.b(), tb)
                      dst = ncs[l] if is_sample else ncp[l, s]
                      misc_store(dst[:, pc * 512:(pc + 1) * 512], t[0:3, :], tb)
              cp("pool", qkv_pre[:, :, 0:3], tails[:, l], [tails_b[l]], qp_b)
              cp("pool", tails[:, l], qkv_pre[:, :, N:N + 3], qp_b, [tails_b[l]])
              for c in range(12):
                  di = nxt("diag", 2)
                  for j in range(4):
                      act(diag[di][:, j, :], identb[:], AF.Identity, (), [diag_b[di]], scale=pcol[:, PC + 32 + c * 4 + j:PC + 33 + c * 4 + j])
                  ps = bigps()
                  for j in range(4):
                      mm(ps.t[:, 0:N], diag[di][:, j, :], qkv_pre[:, c, j:j + N], j == 0, j == 3, [diag_b[di], qp_b[c]], ps.b(), sig=(j == 3))
                  act(hid[:, c, 0:N], ps.t[:, 0:N], AF.Silu, ps.b(), [hid_b[c]])
              wp, wpb = wnext()
              for mi in range(4):
                  ps = bigps()
                  for k in range(8):
                      mm(ps.t[:, 0:N], wp[:, k * 512 + mi * 128:k * 512 + (mi + 1) * 128], xTb[:, k, 0:N],
                         k == 0, k == 7, wpb + [xTb_b[k]], ps.b(), sig=(k == 7))
                  act(hid[:, 12 + mi, 0:N], ps.t[:, 0:N], AF.Silu, ps.b(), [hid_b[12 + mi]])
              pq, pqb = smallq()
              for blk in range(NB):
                  for k in range(8):
                      mm(pq[:, blk * 8:(blk + 1) * 8], xTb[:, k, blk * 128:(blk + 1) * 128], wba[:, l, k * 8:(k + 1) * 8],
                         k == 0, k == 7, [xTb_b[k]], pqb, sig=(k == 7))
              NG = NB * 4
              pq3 = pq[:, 0:NB * 8].rearrange("p (b c) -> p b c", c=8)

              def g3(name):
                  return gt[name][:, 0:NG].rearrange("p (b c) -> p b c", c=4)
              gb = [gates_b]
              act(g3("tmpa"), pq3[:, :, 0:4], AF.Exp, pqb, gb, scale=-1.0)
              act(g3("sbn"), g3("tmpa"), AF.Ln, gb, gb, bias=epsc[:, 3:4])
              ts1("dve", gt["tmpa"][:, 0:NG], gt["tmpa"][:, 0:NG], 1.0, ALU.add, gb, gb)
              tk.op("dve", lambda e: e.reciprocal(out=gt["beta"][:, 0:NG], in_=gt["tmpa"][:, 0:NG]), gb, gb)
              tt("dve", g3("tmpb"), pq3[:, :, 4:8], pbc[:, BC + 1040:BC + 1040 + NG].rearrange("p (b c) -> p b c", c=4), ALU.add, pqb + gb, gb)
              act(gt["tmpb"][:, 0:NG], gt["tmpb"][:, 0:NG], AF.Exp, gb, gb)
              act(gt["tmpb"][:, 0:NG], gt["tmpb"][:, 0:NG], AF.Ln, gb, gb, bias=epsc[:, 3:4])
              tt("dve", gt["g"][:, 0:NG], gt["tmpb"][:, 0:NG], nexpA[:, l, 0:NG], ALU.mult, gb, gb)
              pg, pgb = smallq()
              mm(pg[:, 0:NG], Mu_f, gt["g"][:, 0:NG], True, True, gb, pgb, sig=False)
              mm(pg[:, 16:16 + NG], Bd_f, gt["g"][:, 0:NG], True, True, gb, pgb, sig=False)
              mm(pg[:, 32:32 + NG], E0_f, gt["g"][:, 0:NG], True, True, gb, pgb, sig=False)
              mm(pg[:, 48:48 + NG], E1_f, gt["g"][:, 0:NG], True, True, gb, pgb)
              cp("dve", gt["gc"][:, 0:NG], pg[:, 0:NG], pgb, gb)
              ts1("dve", gt["ngc"][:, 0:NG], gt["gc"][:, 0:NG], -1.0, ALU.mult, gb, gb)
              stt("dve", gt["gcb"][:, 0:NG], gt["sbn"][:, 0:NG], -1.0, gt["gc"][:, 0:NG], ALU.mult, ALU.add, gb, gb)
              act(gt["eg"][:, 0:NG], pg[:, 0:NG], AF.Exp, pgb, gb)
              tt("dve", gt["bEg"][:, 0:NG], gt["eg"][:, 0:NG], gt["beta"][:, 0:NG], ALU.mult, gb, gb)
              tt("dve", gt["tmpa"][:, 0:NG], pg[:, 16:16 + NG], gt["ngc"][:, 0:NG], ALU.add, pgb + gb, gb)
              act(gt["tmpa"][:, 0:NG], gt["tmpa"][:, 0:NG], AF.Exp, gb, gb)
              ts1("dve", gt["Ed0"][:, 0:NG], gt["tmpa"][:, 0:NG], E0_f[:, 0:1], ALU.mult, gb, gb)
              ts1("dve", gt["Ed1"][:, 0:NG], gt["tmpa"][:, 0:NG], E1_f[:, 0:1], ALU.mult, gb, gb)
              act(egl[:, 0:NG], pg[:, 32:32 + NG], AF.Exp, pgb, gb)
              act(egl[:, 16:16 + NG], pg[:, 48:48 + NG], AF.Exp, pgb, gb)
              wp, wpb = wnext()
              for mi in range(4):
                  ps = bigps()
                  for k in range(8):
                      mm(ps.t[:, 0:N], wp[:, k * 512 + mi * 128:k * 512 + (mi + 1) * 128], xTb[:, k, 0:N],
                         k == 0, k == 7, wpb + [xTb_b[k]], ps.b(), sig=(k == 7))
                  act(hid[:, 16 + mi, 0:N], ps.t[:, 0:N], AF.Gelu, ps.b(), [hid_b[16 + mi]])
              wp, wpb = wnext()
              for blk in range(NB):
                  ps = bigps()
                  for k in range(8):
                      mm(ps.t[:, 0:512], xTb[:, k, blk * 128:(blk + 1) * 128], wp[:, k * 512:(k + 1) * 512],
                         k == 0, k == 7, wpb + [xTb_b[k]], ps.b(), sig=(k == 7))
                  vi = nxt("vg", 2)
                  v_, vb_ = vg[vi], [vg_b[vi]]
                  act(v_[:], ps.t[:, 0:512], AF.Gelu, ps.b(), vb_)
                  tk.op("dve", lambda e: e.bn_stats(out=st6[:, 0:6], in_=v_[:]), vb_, [mv_b])
                  tk.op("dve", lambda e: e.bn_aggr(out=mv[:, 0:2], in_=st6[:, 0:6]), [mv_b], [mv_b])
                  rsq(mv[:, 2:3], mv[:, 1:2], 0, [mv_b], [mv_b])
                  stt("dve", mv[:, 3:4], mv[:, 0:1], -1.0, mv[:, 2:3], ALU.mult, ALU.mult, [mv_b], [mv_b])
                  act(v_[:], v_[:], AF.Identity, vb_ + [mv_b], vb_, bias=mv[:, 3:4], scale=mv[:, 2:3])
                  tt("dve", v_[:], v_[:], pbc[:, BC:BC + 512], ALU.mult, vb_, vb_)
                  if is_sample:
                      tt("dve", v_[:], v_[:], pbc[:, BC + 512:BC + 1024], ALU.add, vb_, vb_)
                      cp("pool", hid[:, 20 + blk, :], v_[:], vb_, [hid_b[20 + blk]])
                      misc_store(nvs[l], v_[0:64, :], vb_)
                  else:
                      tt("pool", hid[:, 20 + blk, :], v_[:], pbc[:, BC + 512:BC + 1024], ALU.add, vb_, [hid_b[20 + blk]])

              chk(l, "p2")
              for c in range(8):
                  sq, sqb = sBn()
                  act(sq[:, 0:N], hid[:, c, 0:N], AF.Square, [hid_b[c]], sqb)
                  ps = bigps()
                  if c < 4:
                      mm(ps.t[:, 0:N], ones128b[:], sq[:, 0:N], True, True, sqb, ps.b())
                      eps_ = 2
                  else:
                      mm(ps.t[:, 0:N], onesb[:], sq[:, 0:N], True, True, sqb, ps.b())
                      eps_ = 1
                  t, tb = sFn()
                  rsq(t[:, 0:N], ps.t[:, 0:N], eps_, ps.b(), tb)
                  tt("pool", hid[:, c, 0:N], hid[:, c, 0:N], t[:, 0:N], ALU.mult, [hid_b[c]] + tb, [hid_b[c]])

              chk(l, "g1")
              for blk in range(NB):
                  bs = slice(blk * 128, (blk + 1) * 128)
                  chk(l, "g5b%d" % blk)
                  for h in range(4):
                      H = hd[h]
                      idx = blk * 4 + h
                      qb_, kb_, vb2_ = [hid_b[h]], [hid_b[4 + h]], [hid_b[8 + h]]
                      di = nxt("dg", 2)
                      ts1("pool", dgp[di][:, 0:128], identb[:], gt["gc"][:, idx:idx + 1], ALU.mult, gb, [dgp_b[di]])
                      ts1("pool", dgp[di][:, 128:256], identb[:], gt["gcb"][:, idx:idx + 1], ALU.mult, gb, [dgp_b[di]])
                      ps = bigps()
                      mm(ps.t[:, 0:256], ones_f, dgp[di][:], True, True, [dgp_b[di]], ps.b(0, 256), sig=False)
                      mm(ps.t[:, 256:512], hid[:, 4 + h, bs], hid[:, 0:8, :].rearrange("p (a h) n -> p a h n", a=2)[:, :, h, bs], True, True, qb_ + kb_, ps.b(256, 512))
                      ti_ = nxt("tmp", 2)
                      stt("dve", tmpp[ti_][:], ps.t[:, 0:256], gt["ngc"][:, idx:idx + 1], maskneg2, ALU.add, ALU.add,
                          ps.b(0, 256) + gb, [tmpp_b[ti_]])
                      ei = nxt("E", 2)
                      act(Ep[ei][:], tmpp[ti_][:], AF.Exp, [tmpp_b[ti_]], [Ep_b[ei]])
                      act(H["EG"][:], ps.t[:, 0:128], AF.Exp, ps.b(0, 128), [H["EG_b"]])
                      yx0, yx0b = H["YX"][0], [H["YX_b"][0]]
                      tt("dve", H["attnT"][:], ps.t[:, 256:384], Ep[ei][:, 0:128], ALU.mult, ps.b(256, 384) + [Ep_b[ei]], [H["attnT_b"]])
                      tt("dve", yx0[:, 0:128], ps.t[:, 384:512], Ep[ei][:, 128:256], ALU.mult, ps.b(384, 512) + [Ep_b[ei]], yx0b)
                      stt("dve", H["P"][:], yx0[:, 0:128], -1.0, identb[:], ALU.mult, ALU.add, yx0b, [H["P_b"]])
                      tt("pool", H["qdT"][:], hid[:, h, bs], H["EG"][:], ALU.mult, qb_ + [H["EG_b"]], [H["qdT_b"]])
                  pt, ptb = pTbank()
                  for h in range(4):
                      H = hd[h]
                      mm(pt[:, h * 128:(h + 1) * 128], H["YX"][0][:, 0:128], identb[:], True, True, [H["YX_b"][0]], ptb, sig=(h == 3))
                  for h in range(4):
                      H = hd[h]
                      cp("act", H["YX"][0][:, 128:256], pt[:, h * 128:(h + 1) * 128], ptb, [H["YX_b"][0]])
                  pk, pkb = pTbank()
                  for h in range(4):
                      mm(pk[:, h * 128:(h + 1) * 128], hid[:, 4 + h, bs], identb[:], True, True, [hid_b[4 + h]], pkb, sig=(h == 3))
                  for h in range(4):
                      H = hd[h]
                      idx = blk * 4 + h
                      pks = pk[:, h * 128:(h + 1) * 128]
                      ts1("dve", H["kbg"][:], pks, gt["bEg"][:, idx:idx + 1], ALU.mult, pkb + gb, [H["kbg_b"]])
                      ts1("dve", H["kd0"][:], pks, gt["Ed0"][:, idx:idx + 1], ALU.mult, pkb + gb, [H["kd0_b"]])
                      ts1("dve", H["kd1"][:], pks, gt["Ed1"][:, idx:idx + 1], ALU.mult, pkb + gb, [H["kd1_b"]])
                  pv, pvb = pTbank()
                  for h in range(4):
                      mm(pv[:, h * 128:(h + 1) * 128], hid[:, 8 + h, bs], identb[:], True, True, [hid_b[8 + h]], pvb, sig=(h == 3))
                  for h in range(4):
                      H = hd[h]
                      idx = blk * 4 + h
                      ts1("dve", H["vb"][:], pv[:, h * 128:(h + 1) * 128], gt["beta"][:, idx:idx + 1], ALU.mult, pvb + gb, [H["vb_b"]])
                  chk(l, "g2b%d" % blk)
                  for k in range(6):
                      for h in range(4):
                          H = hd[h]
                          cur, curb = H["YX"][k % 2], [H["YX_b"][k % 2]]
                          nx_, nxb = H["YX"][(k + 1) % 2], [H["YX_b"][(k + 1) % 2]]
                          ps = bigps()
                          if k >= 1:
                              mm(ps.t[:, 0:128], cur[:, 128:256], H["P"][:], True, True, curb + [H["P_b"]], ps.b(0, 128), sig=(k == 5))
                          if k <= 4:
                              mm(ps.t[:, 128:256], cur[:, 128:256], cur[:, 0:128], True, True, curb, ps.b(128, 256), sig=False)
                              mm(ps.t[:, 256:384], cur[:, 0:128], cur[:, 128:256], True, True, curb, ps.b(256, 384))
                              cp("dve", nx_[:], ps.t[:, 128:384], ps.b(128, 384), nxb)
                          if k >= 1:
                              tt("dve", H["P"][:], H["P"][:], ps.t[:, 0:128], ALU.add, [H["P_b"]] + ps.b(0, 128), [H["P_b"]])
                      chk(l, "n%db%d" % (k, blk))
                  for h in range(4):
                      H = hd[h]
                      pq, pqb = smallq()
                      mm(pq, H["kbg"][:], H["P"][:], True, True, [H["kbg_b"], H["P_b"]], pqb)
                      act(H["nwT"][:], pq, AF.Identity, pqb, [H["nwT_b"]], scale=-1.0)
                  chk(l, "g3b%d" % blk)
                  for i in range(2):
                      cs = slice(i * 64, (i + 1) * 64)
                      pqs = []
                      for h in range(4):
                          H = hd[h]
                          pq, pqb = smallq()
                          mm(pq, H["P"][:], H["vb"][:], True, False, [H["P_b"], H["vb_b"]], pqb, sig=False)
                          mm(pq, H["nwT"][:], Sbf[:, l, h, :], False, True, [H["nwT_b"], Sbf_b[l][h]], pqb)
                          pqs.append((pq, pqb))
                      for h in range(4):
                          H = hd[h]
                          pq, pqb = pqs[h]
                          cp("act", H["vnew"][:], pq, pqb, [H["vnew_b"]])
                      pq2s = []
                      for h in range(4):
                          H = hd[h]
                          oc = slice(h * 128 + i * 64, h * 128 + (i + 1) * 64)
                          mm(psO.t[:, oc], Sbf[:, l, h, :], H["qdT"][:, cs], True, False, [Sbf_b[l][h], H["qdT_b"]], [psO.q[h]], sig=False)
                          mm(psO.t[:, oc], H["vnew"][:], H["attnT"][:, cs], False, True, [H["vnew_b"], H["attnT_b"]], [psO.q[h]], sig=False)
                          pq2, pq2b = smallq()
                          kd = H["kd0"] if i == 0 else H["kd1"]
                          kdb = H["kd0_b"] if i == 0 else H["kd1_b"]
                          mm(pq2, kd[:], H["vnew"][:], True, True, [kdb, H["vnew_b"]], pq2b)
                          pq2s.append((pq2, pq2b))
                      for h in range(4):
                          idx = blk * 4 + h
                          pq2, pq2b = pq2s[h]
                          stt("dve", S32[:, l, h, :], S32[:, l, h, :], egl[:, i * 16 + idx:i * 16 + idx + 1], pq2, ALU.mult, ALU.add,
                              [S32_b[l][h]] + pq2b + gb, [S32_b[l][h]])
                          cp("pool", Sbf[:, l, h, :], S32[:, l, h, :], [S32_b[l][h]], [Sbf_b[l][h]])
                      if is_sample and i == 0:
                          misc_store(nds[l].rearrange("h d e -> d h e"), S32[:, l], S32_b[l])
                  chk(l, "g4b%d" % blk)
                  sq, sqb = sBn()
                  act(sq[:, 0:512], psO.t[:, 0:512], AF.Square, psO.b(), sqb)
                  ps = bigps()
                  mm(ps.t[:, 0:512], onesrmsb[:], sq[:, 0:512], True, True, sqb, ps.b())
                  t, tb = sFn()
                  rsq(t[:, 0:512], ps.t[:, 0:512], 1, ps.b(), tb)
                  tt("dve", t[:, 0:512], t[:, 0:512], psO.t[:, 0:512], ALU.mult, tb + psO.b(), tb)
                  stt("dve", qkv_pre[:, 0:4, bs], t[:, 0:512].rearrange("p (h c) -> p h c", h=4), pcol[:, PC + 80:PC + 81],
                      hid[:, 12:16, bs], ALU.mult, ALU.mult, tb + hid_b[12:16], qp_b[0:4])
              if last_tile and not is_sample:
                  misc_store(ndp[l, s].rearrange("h d e -> d h e"), S32[:, l], S32_b[l])

              chk(l, "gdn")
              for g in range(4):
                  ps = bigps()
                  for blk in range(NB):
                      bs = slice(blk * 128, (blk + 1) * 128)
                      mm(ps.t[:, bs], onesb[:], bsmat[:, l * 4 + g, :], True, False, (), ps.b(), sig=False)
                      mm(ps.t[:, bs], hid[:, 20 + blk, g * 128:(g + 1) * 128], WsT[:, l * 4 + g, :], False, True, [hid_b[20 + blk]], ps.b())
                  tt("dve", qkv_pre[:, 4 + g, 0:N], ps.t[:, 0:N], hid[:, 16 + g, 0:N], ALU.mult, ps.b() + [hid_b[16 + g]], [qp_b[4 + g]])

              chk(l, "mix")
              for m in range(8):
                  if m % 4 == 0:
                      wp, wpb = wnext()
                  mi = m % 4
                  ps = bigps()
                  for k in range(8):
                      mm(ps.t[:, 0:N], wp[:, k * 512 + mi * 128:k * 512 + (mi + 1) * 128], qkv_pre[:, k, 0:N],
                         k == 0, k == 7, wpb + [qp_b[k]], ps.b(), sig=(k == 7))
                  stt("dve", xT[:, m, 0:N], xT[:, m, 0:N], ALPHA, ps.t[:, 0:N], ALU.mult, ALU.add, [xT_b[m]] + ps.b(), [xT_b[m]])
              chk(l, "res1")
              ln_fm(N, PC + 0, PC + 8)
              chk(l, "ln1")

              for j in range(8):
                  wp, wpb = wnext()
                  for fi in range(4):
                      f = j * 4 + fi
                      ps = bigps()
                      for k in range(8):
                          mm(ps.t[:, 0:N], wp[:, k * 512 + fi * 128:k * 512 + (fi + 1) * 128], xTb[:, k, 0:N],
                             k == 0, k == 7, wpb + [xTb_b[k]], ps.b(), sig=(k == 7))
                      t, tb = sFn()
                      act(t[:, 0:N], ps.t[:, 0:N], AF.Relu, ps.b(), tb)
                      tt("pool" if f % 2 else "dve", hid[:, f, 0:N], t[:, 0:N], t[:, 0:N], ALU.mult, tb, [hid_b[f]])
              for m in range(8):
                  wp, wpb = wnext()
                  ps = bigps()
                  for f in range(32):
                      mm(ps.t[:, 0:N], wp[:, f * 128:(f + 1) * 128], hid[:, f, 0:N], f == 0, f == 31, wpb + [hid_b[f]], ps.b(), sig=(f == 31))
                  stt("dve", xT[:, m, 0:N], xT[:, m, 0:N], ALPHA, ps.t[:, 0:N], ALU.mult, ALU.add, [xT_b[m]] + ps.b(), [xT_b[m]])
              chk(l, "res2")
              ln_fm(N, PC + 16, PC + 24)
              chk(l, "ln2")

          for blk in range(NB):
              slot = nxt("vg", 2)
              yo, yob = xin[slot], [xin_b[slot]]
              for half in range(2):
                  ps = bigps()
                  for cc in range(4):
                      c = half * 4 + cc
                      mm(ps.t[:, cc * 128:(cc + 1) * 128], xT[:, c, blk * 128:(blk + 1) * 128], ident_f, True, True, [xT_b[c]], [ps.q[cc]], sig=(cc == 3))
                  cp("act" if half else "dve", yo[:, half * 512:(half + 1) * 512], ps.t[:, 0:512], ps.b(), yob)
              if is_sample:
                  tk.dma("sp", ys, yo[0:64, :], "sty%d" % slot, yob, ())
              else:
                  r0 = ti * 512 + blk * 128
                  tk.dma("sp", yp[s, r0:r0 + 128, :], yo[:], "sty%d" % slot, yob, ())

    except _Stop:
        pass

    for key in ("sty0", "sty1", "misc"):
        tk.wait_all("sp", key)
    _LAST_CNT.clear()
    _LAST_CNT["sbuf_free"] = nc.sbuf_bytes_remaining
    _LAST_CNT.update(tk.cnt)
    return nc


def _consts():
    c = np.zeros((128, NCONST), np.float32)
    i = np.arange(128)
    P = i[:, None]
    Fr = i[None, :]
    same = (P // 64) == (Fr // 64)
    c[:, 0:128] = np.eye(128)
    c[:, 128:256] = 1.0
    c[:, 256:384] = (same & (P <= Fr))
    c[:, 384:512] = same
    c[:, 512:640] = (P < 64) * np.ones((1, 128))
    c[:, 640:768] = (P >= 64) * np.ones((1, 128))
    c[:, 768:896] = np.where(same & (P <= Fr), 0.0, NEG)
    c[:, 896:1024] = np.where(same & (P < Fr), 0.0, NEG)
    c[:, 1024:1152] = (P <= Fr)
    return c


def _pack_small(inp):
    f = lambda a: np.asarray(a, np.float32)
    col = lambda v: f(v).reshape(-1, 128).T
    pcol = np.zeros((128, 16 + NL * LPC), np.float32)
    pcol[:, 0:8] = col(inp["ln0_g"])
    pcol[:, 8:16] = col(inp["ln0_b"])
    pbc = np.zeros((128, NL * LBC), np.float32)
    for l in range(NL):
        b = 16 + l * LPC
        pcol[:, b:b + 8] = col(inp["ln1_g"][l])
        pcol[:, b + 8:b + 16] = col(inp["ln1_b"][l])
        pcol[:, b + 16:b + 24] = col(inp["ln2_g"][l])
        pcol[:, b + 24:b + 32] = col(inp["ln2_b"][l])
        cw = f(inp["conv_w"][l])
        pcol[:, b + 32:b + 80] = cw.reshape(4, 12, 128).transpose(2, 1, 0).reshape(128, 48)
        pcol[:, b + 80] = f(inp["gdn_norm_g"][l])
        o = l * LBC
        pbc[:, o:o + 512] = f(inp["smlp_ln_g"][l])[None, :]
        pbc[:, o + 512:o + 1024] = f(inp["smlp_ln_b"][l])[None, :]
        pbc[:, o + 1024:o + 1040] = np.tile(f(inp["a_log"][l]), 4)[None, :]
        pbc[:, o + 1040:o + 1056] = np.tile(f(inp["dt_bias"][l]), 4)[None, :]
    bsrow = f(inp["b_s"]).reshape(1, NL * 4 * 128)
    ws = f(inp["w_s"]).reshape(NL * 4, 128, 128)
    return pcol, pbc, bsrow, ws


SEQUENTIAL_LAUNCH = False
_NC_CACHE = {}
_NOCAST = [0]
_DELAY = [0]
_LAST_CNT = {}
_STOP = [None]


def run_cores(inp, TP, NSP, ncores):
    key = (TP, NSP, _STOP[0])
    if key not in _NC_CACHE:
        _NC_CACHE[key] = build(TP, NSP, _STOP[0])
    nc = _NC_CACHE[key]
    f = lambda a: np.ascontiguousarray(np.asarray(a, np.float32))
    pcol, pbc, bsrow, ws = _pack_small(inp)
    consts = _consts()
    shared = {"w_in": f(inp["w_in"]), "w_out": f(inp["w_out"]), "w_up": f(inp["w_up"]), "w_down": f(inp["w_down"]),
              "ws": ws, "consts": consts, "pcol": pcol, "pbc": pbc, "bsrow": bsrow}
    xp = f(inp["x_prompt"])
    xs = f(inp["x_sample"])
    cc = f(inp["cache_conv"])
    sd = f(inp["state_delta"])
    in_maps = []
    for i in range(ncores):
        m = dict(shared)
        m["xp"] = np.ascontiguousarray(xp[i * NSP:(i + 1) * NSP])
        m["xs"] = np.ascontiguousarray(xs[i])
        m["cconv"] = np.ascontiguousarray(cc[:, i])
        m["sdelta"] = np.ascontiguousarray(sd[:, i])
        in_maps.append(m)
    if SEQUENTIAL_LAUNCH and ncores > 1:
        R = []
        for i in range(ncores):
            R.append(run_bass_kernel_spmd(nc, [in_maps[i]], core_ids=[0]).results[0])
    else:
        R = run_bass_kernel_spmd(nc, in_maps, core_ids=list(range(ncores))).results
    y_p = np.concatenate([r["yp"] for r in R], axis=0)
    y_s = np.stack([r["ys"] for r in R], axis=0)
    ncp = np.concatenate([r["ncp"] for r in R], axis=1)
    ndp = np.concatenate([r["ndp"] for r in R], axis=1)
    ncs = np.stack([r["ncs"] for r in R], axis=1)
    nds = np.stack([r["nds"] for r in R], axis=1)
    nvs = np.stack([r["nvs"] for r in R], axis=1)
    return tuple(np.ascontiguousarray(a, dtype=np.float32) for a in (y_p, y_s, ncp, ndp, ncs, nds, nvs))


def kernel(**inputs):
    return run_cores(inputs, 4096, 2, 8)
```

```python
import numpy as np
import concourse.bass as bass
import concourse.mybir as mybir
from concourse.bass_utils import run_bass_kernel_spmd

F32 = mybir.dt.float32
BF16 = mybir.dt.bfloat16
AF = mybir.ActivationFunctionType
ALU = mybir.AluOpType

NL = 2
D = 1024
INW = 3080
ALPHA = float((2 * NL) ** 0.25)
LN_EPS = 1e-5
NEPS = 1e-6
NEG = -30000.0
NWP = 8
NBIG = 5
LPC = 81
LBC = 1056
NCONST = 1152


class Buf:
    __slots__ = ("w", "r", "name")

    def __init__(self, name=""):
        self.w = None
        self.r = {}
        self.name = name


class TK:
    def __init__(self, nc):
        self.nc = nc
        self.E = {"pe": nc.tensor, "act": nc.scalar, "dve": nc.vector, "pool": nc.gpsimd, "sp": nc.sync}
        self.sem = {}
        self.cnt = {}
        self.waited = {e: {} for e in self.E}
        for e in self.E:
            self.newsem(e)

    def newsem(self, key):
        self.sem[key] = self.nc.alloc_semaphore(name="s_" + key)
        self.cnt[key] = 0

    def _needs(self, reads, writes):
        nd = {}
        for b in reads:
            if b.w is not None:
                k, v = b.w
                if nd.get(k, 0) < v:
                    nd[k] = v
        for b in writes:
            if b.w is not None:
                k, v = b.w
                if nd.get(k, 0) < v:
                    nd[k] = v
            for k, v in b.r.items():
                if nd.get(k, 0) < v:
                    nd[k] = v
        return nd

    def _wait(self, e, nd):
        w = self.waited[e]
        for k, v in nd.items():
            if e == "pe" and k == "pe":
                continue
            if w.get(k, 0) < v:
                self.E[e].wait_ge(self.sem[k], v)
                w[k] = v

    def op(self, e, fn, reads=(), writes=(), sig=True):
        self._wait(e, self._needs(reads, writes))
        ins = fn(self.E[e])
        if sig:
            self.cnt[e] += 1
            ins.then_inc(self.sem[e], 1)
            c = self.cnt[e]
        else:
            c = self.cnt[e] + 1
        for b in reads:
            if b.r.get(e, 0) < c:
                b.r[e] = c
        for b in writes:
            b.w = (e, c)
            b.r = {}

    def dma(self, q, out, in_, key, reads=(), writes=()):
        self._wait(q, self._needs(reads, writes))
        self.E[q].dma_start(out=out, in_=in_).then_inc(self.sem[key], 16)
        self.cnt[key] += 16
        c = self.cnt[key]
        for b in reads:
            if b.r.get(key, 0) < c:
                b.r[key] = c
        for b in writes:
            b.w = (key, c)
            b.r = {}

    def wait_all(self, e, key):
        v = self.cnt[key]
        if v > 0 and self.waited[e].get(key, 0) < v:
            self.E[e].wait_ge(self.sem[key], v)
            self.waited[e][key] = v


class PSB:
    def __init__(self, t, name):
        self.t = t
        self.whole = Buf(name)
        self.q = [self.whole] * 4

    def b(self, c0=0, c1=512):
        return [self.whole]


class _Stop(Exception):
    pass


def build(TP, NSP=2, stop=None):
    nc = bass.Bass("TRN2", target_bir_lowering=False)
    dt = nc.dram_tensor
    xp = dt("xp", [NSP, TP, D], F32, kind="ExternalInput").ap()
    xs = dt("xs", [64, D], F32, kind="ExternalInput").ap()
    cconv = dt("cconv", [NL, 3, 1536], F32, kind="ExternalInput").ap()
    sdelta = dt("sdelta", [NL, 4, 128, 128], F32, kind="ExternalInput").ap()
    w_in = dt("w_in", [NL, D, INW], F32, kind="ExternalInput").ap()
    w_out = dt("w_out", [NL, D, D], F32, kind="ExternalInput").ap()
    w_up = dt("w_up", [NL, D, 4096], F32, kind="ExternalInput").ap()
    w_down = dt("w_down", [NL, 4096, D], F32, kind="ExternalInput").ap()
    ws_d = dt("ws", [NL * 4, 128, 128], F32, kind="ExternalInput").ap()
    consts_d = dt("consts", [128, NCONST], F32, kind="ExternalInput").ap()
    pcol_d = dt("pcol", [128, 16 + NL * LPC], F32, kind="ExternalInput").ap()
    pbc_d = dt("pbc", [128, NL * LBC], F32, kind="ExternalInput").ap()
    bsrow_d = dt("bsrow", [1, NL * 4 * 128], F32, kind="ExternalInput").ap()

    yp = dt("yp", [NSP, TP, D], F32, kind="ExternalOutput").ap()
    ys = dt("ys", [64, D], F32, kind="ExternalOutput").ap()
    ncp = dt("ncp", [NL, NSP, 3, 1536], F32, kind="ExternalOutput").ap()
    ndp = dt("ndp", [NL, NSP, 4, 128, 128], F32, kind="ExternalOutput").ap()
    ncs = dt("ncs", [NL, 3, 1536], F32, kind="ExternalOutput").ap()
    nds = dt("nds", [NL, 4, 128, 128], F32, kind="ExternalOutput").ap()
    nvs = dt("nvs", [NL, 64, 512], F32, kind="ExternalOutput").ap()

    wsc = dt("wsc", [NL, 24, 128, 4096], BF16, kind="Internal").ap()
    wbasc = dt("wbasc", [NL, 128, 64], BF16, kind="Internal").ap()

    tk = TK(nc)
    _n = [0]

    def sb(shape, dtype, name=None):
        _n[0] += 1
        return nc.alloc_sbuf_tensor("sb_" + (name or ("t%d" % _n[0])), shape, dtype)

    cst = sb([128, NCONST], F32, "cst")
    pcol = sb([128, 16 + NL * LPC], F32, "pcol")
    pbc = sb([128, NL * LBC], F32, "pbc")
    epsc = sb([128, 4], F32, "epsc")
    identb = sb([128, 128], BF16, "identb")
    onesb = sb([128, 128], BF16, "onesb")
    ones128b = sb([128, 128], BF16, "ones128b")
    ones1kb = sb([128, 128], BF16, "ones1kb")
    onesrmsb = sb([128, 128], BF16, "onesrmsb")
    WsT = sb([128, 8, 128], BF16, "WsT")
    bsmat = sb([128, 8, 128], BF16, "bsmat")
    nexpA = sb([128, NL, 16], F32, "nexpA")
    wba = sb([128, NL, 64], BF16, "wba")
    wpool = [sb([128, 4096], BF16, "wp%d" % i) for i in range(NWP)]
    wpool_b = [Buf("wp%d" % i) for i in range(NWP)]
    xin = [sb([128, 1024], F32, "xin%d" % i) for i in range(2)]
    xin_b = [Buf("xin%d" % i) for i in range(2)]
    xT = sb([128, 8, 512], F32, "xT")
    xT_b = [Buf("xT%d" % i) for i in range(8)]
    xTb = sb([128, 8, 512], BF16, "xTb")
    xTb_b = [Buf("xTb%d" % i) for i in range(8)]
    qkv_pre = sb([128, 12, 515], BF16, "qkv_pre")
    qp_b = [Buf("qp%d" % i) for i in range(12)]
    tails = sb([128, NL, 12, 3], BF16, "tails")
    tails_b = [Buf("tails%d" % i) for i in range(NL)]
    diag = [sb([128, 4, 128], BF16, "diag%d" % i) for i in range(2)]
    diag_b = [Buf("diag%d" % i) for i in range(2)]
    hid = sb([128, 32, 512], BF16, "hid")
    hid_b = [Buf("hid%d" % i) for i in range(32)]
    vg = [sb([128, 512], F32, "vg%d" % i) for i in range(2)]
    vg_b = [Buf("vg%d" % i) for i in range(2)]
    sF = [sb([128, 512], F32, "sF%d" % i) for i in range(4)]
    sF_b = [Buf("sF%d" % i) for i in range(4)]
    sB = [sb([128, 512], BF16, "sB%d" % i) for i in range(2)]
    sB_b = [Buf("sB%d" % i) for i in range(2)]
    rstd_t = sb([128, 512], F32, "rstd")
    nmr_t = sb([128, 512], F32, "nmr")
    ln_b = Buf("lnstat")
    S32 = sb([128, NL, 4, 128], F32, "S32")
    Sbf = sb([128, NL, 4, 128], BF16, "Sbf")
    S32_b = [[Buf("S32_%d%d" % (l, h)) for h in range(4)] for l in range(NL)]
    Sbf_b = [[Buf("Sbf_%d%d" % (l, h)) for h in range(4)] for l in range(NL)]
    gnames = ["beta", "g", "sbn", "gc", "ngc", "gcb", "bEg", "Ed0", "Ed1", "eg", "tmpa", "tmpb"]
    gt = {n: sb([128, 16], F32, "g_" + n) for n in gnames}
    egl = sb([128, 32], F32, "g_egl")
    gates_b = Buf("gates")
    mv = sb([128, 16], F32, "mv")
    st6 = sb([128, 12], F32, "st6")
    mv_b = Buf("mv")
    hd = []
    for h in range(4):
        d_ = {}
        for n in ["attnT", "P", "kbg", "kd0", "kd1", "vb", "nwT", "qdT", "vnew", "EG"]:
            d_[n] = sb([128, 128], BF16, "h%d_%s" % (h, n))
            d_[n + "_b"] = Buf("h%d_%s" % (h, n))
        d_["YX"] = [sb([128, 256], BF16, "h%d_YX%d" % (h, i)) for i in range(2)]
        d_["YX_b"] = [Buf("h%d_YX%d" % (h, i)) for i in range(2)]
        hd.append(d_)
    dgp = [sb([128, 256], F32, "dg%d" % i) for i in range(2)]
    dgp_b = [Buf("dg%d" % i) for i in range(2)]
    tmpp = [sb([128, 256], F32, "tmpE%d" % i) for i in range(2)]
    tmpp_b = [Buf("tmpE%d" % i) for i in range(2)]
    Ep = [sb([128, 256], F32, "E%d" % i) for i in range(2)]
    Ep_b = [Buf("E%d" % i) for i in range(2)]

    big = [PSB(nc.alloc_psum_tensor("pbig%d" % i, [128, 512], F32), "pbig%d" % i) for i in range(5)]
    psO = PSB(nc.alloc_psum_tensor("psO", [128, 512], F32), "psO")
    pT = [nc.alloc_psum_tensor("pT%d" % i, [128, 512], F32) for i in range(2)]
    pT_b = [Buf("pT%d" % i) for i in range(2)]
    rot = {"big": 0, "small": 0, "pT": 0, "sF": 0, "sB": 0, "vg": 0, "dg": 0, "tmp": 0, "E": 0, "diag": 0}

    def nxt(key, n):
        i = rot[key]
        rot[key] = (i + 1) % n
        return i

    def bigps():
        return big[nxt("big", NBIG)]

    def smallq():
        p_ = bigps()
        return p_.t[:, 0:128], p_.b()

    def pTbank():
        i = nxt("pT", 2)
        return pT[i], [pT_b[i]]

    def sFn():
        i = nxt("sF", 4)
        return sF[i], [sF_b[i]]

    def sBn():
        i = nxt("sB", 2)
        return sB[i], [sB_b[i]]

    def mm(out, lhsT, rhs, start, stop, reads, writes, sig=True):
        tk.op("pe", lambda e: e.matmul(out, lhsT=lhsT, rhs=rhs, start=start, stop=stop), reads, writes, sig)

    def act(out, in_, func, reads, writes, bias=None, scale=None):
        kw = {}
        if bias is not None:
            kw["bias"] = bias
        if scale is not None:
            kw["scale"] = scale
        tk.op("act", lambda e: e.activation(out=out, in_=in_, func=func, **kw), reads, writes)

    def ts(eng, out, in0, s1, s2, op0, op1, reads, writes):
        tk.op(eng, lambda e: e.tensor_scalar(out=out, in0=in0, scalar1=s1, scalar2=s2, op0=op0, op1=op1), reads, writes)

    def rsq(out, in0, epscol, reads, writes):
        act(out, in0, AF.Sqrt, reads, writes, bias=epsc[:, epscol:epscol + 1])
        tk.op("dve", lambda e: e.reciprocal(out=out, in_=out), writes, writes)

    def ts1(eng, out, in0, s1, op0, reads, writes):
        tk.op(eng, lambda e: e.tensor_single_scalar(out=out, in_=in0, scalar=s1, op=op0), reads, writes)

    def stt(eng, out, in0, s, in1, op0, op1, reads, writes):
        tk.op(eng, lambda e: e.scalar_tensor_tensor(out=out, in0=in0, scalar=s, in1=in1, op0=op0, op1=op1), reads, writes)

    def tt(eng, out, in0, in1, op, reads, writes):
        tk.op(eng, lambda e: e.tensor_tensor(out=out, in0=in0, in1=in1, op=op), reads, writes)

    def cp(eng, out, in_, reads, writes):
        if eng == "act":
            act(out, in_, AF.Copy, reads, writes)
        else:
            tk.op(eng, lambda e: e.tensor_copy(out=out, in_=in_), reads, writes)

    def mset(eng, ap, v, writes):
        tk.op(eng, lambda e: e.memset(ap, v), (), writes)

    ident_f = cst[:, 0:128]
    ones_f = cst[:, 128:256]
    Mu_f = cst[:, 256:384]
    Bd_f = cst[:, 384:512]
    E0_f = cst[:, 512:640]
    E1_f = cst[:, 640:768]
    maskneg2 = cst[:, 768:1024]
    triU = cst[:, 1024:1152]

    sc_b = [[Buf("sc%d_%d" % (l, g)) for g in range(4)] for l in range(NL)]
    win_cols = [0, 512, 1024, 1536, 2056, 2568]
    for l in range(NL):
        for g in range(4):
            tk.newsem("cv%d_%d" % (l, g))
    tk.newsem("cvba")
    wba_sc_b = Buf("wbasc")
    for l in range(NL if not _NOCAST[0] else 0):
        for pi, c0 in enumerate(win_cols):
            tk.dma("pool", wsc[l, pi].rearrange("p (k c) -> p k c", k=8),
                   w_in[l][:, c0:c0 + 512].rearrange("(k p) c -> p k c", p=128), "cv%d_0" % l, (), [sc_b[l][0]])
        tk.dma("pool", wbasc[l].rearrange("p (k c) -> p k c", k=8),
               w_in[l][:, 2048:2056].rearrange("(k p) c -> p k c", p=128), "cvba", (), [wba_sc_b])
        for j in range(2):
            tk.dma("pool", wsc[l, 6 + j].rearrange("p (k c) -> p k c", k=8),
                   w_out[l][:, j * 512:(j + 1) * 512].rearrange("(k p) c -> p k c", p=128), "cv%d_1" % l, (), [sc_b[l][1]])
        for j in range(8):
            tk.dma("pool", wsc[l, 8 + j].rearrange("p (k c) -> p k c", k=8),
                   w_up[l][:, j * 512:(j + 1) * 512].rearrange("(k p) c -> p k c", p=128), "cv%d_2" % l, (), [sc_b[l][2]])
        for m in range(8):
            tk.dma("pool", wsc[l, 16 + m].rearrange("p (f c) -> p f c", f=32),
                   w_down[l][:, m * 128:(m + 1) * 128].rearrange("(f p) c -> p f c", p=128), "cv%d_3" % l, (), [sc_b[l][3]])

    tk.newsem("const")
    cbuf = Buf("const")
    tk.dma("sp", cst[:], consts_d, "const", (), [cbuf])
    tk.dma("sp", pcol[:], pcol_d, "const", (), [cbuf])
    tk.dma("sp", pbc[:], pbc_d, "const", (), [cbuf])
    tk.dma("sp", sF[0][0:1, :], bsrow_d[:, 0:512], "const", (), [cbuf])
    tk.dma("sp", sF[1][0:1, :], bsrow_d[:, 512:1024], "const", (), [cbuf])
    tk.dma("sp", xin[0][:].rearrange("p (a q) -> p a q", a=8), ws_d.rearrange("a p q -> p a q"), "const", (), [cbuf, xin_b[0]])
    tk.dma("sp", wba[:], wbasc.rearrange("l p c -> p l c"), "const", [wba_sc_b], [cbuf])
    for e in ("pe", "act", "dve", "pool"):
        tk.wait_all(e, "const")

    setup_b = Buf("setup")
    cp("dve", identb[:], ident_f, (), [setup_b])
    mset("dve", onesb[:], 1.0, [setup_b])
    mset("dve", epsc[:, 0:1], LN_EPS, [setup_b])
    mset("dve", epsc[:, 1:2], NEPS, [setup_b])
    mset("dve", epsc[:, 2:3], 128.0 * NEPS, [setup_b])
    mset("dve", epsc[:, 3:4], 1.0, [setup_b])
    mset("dve", ones128b[:], 128.0, [setup_b])
    mset("dve", ones1kb[:], 1.0 / 1024.0, [setup_b])
    mset("dve", onesrmsb[:], 1.0 / 128.0, [setup_b])
    mset("dve", bsmat[:], 0.0, [setup_b])
    for a in range(8):
        src = sF[a // 4][0:1, (a % 4) * 128:(a % 4 + 1) * 128]
        cp("dve", bsmat[0:1, a, :], src, [setup_b], [setup_b])
    for a in range(8):
        pq, pqb = smallq()
        mm(pq, xin[0][:, a * 128:(a + 1) * 128], ident_f, True, True, [setup_b, xin_b[0]], pqb)
        tt("dve", WsT[:, a, :], pq, triU, ALU.mult, pqb, [setup_b])
    for l in range(NL):
        act(nexpA[:, l, :], pbc[:, l * LBC + 1024:l * LBC + 1040], AF.Exp, [setup_b], [setup_b])
        ts1("dve", nexpA[:, l, :], nexpA[:, l, :], -1.0, ALU.mult, [setup_b], [setup_b])
    for e in ("pe", "act", "pool", "dve"):
        tk._wait(e, {"dve": tk.cnt["dve"], "act": tk.cnt["act"], "pe": tk.cnt["pe"]})

    if _DELAY[0]:
        for _i in range(_DELAY[0]):
            tk.op("act", lambda e: e.activation(out=vg[0][:, 0:512], in_=vg[1][:, 0:512], func=AF.Copy), (), [vg_b[0]])
    NTP = TP // 512
    tiles = []
    for s in range(NSP):
        for ti in range(NTP):
            tiles.append((s, ti, 512))
    tiles.append((NSP, 0, 128))
    pieces = []
    for (_s, _ti, _n) in tiles:
        for l in range(NL):
            for pi in range(24):
                g = 0 if pi < 6 else (1 if pi < 8 else (2 if pi < 16 else 3))
                pieces.append((l, pi, g))
    for i in range(NWP):
        tk.newsem("wp%d" % i)
    wst = {"i": 0, "loaded": 0}

    def wnext():
        while wst["loaded"] < min(len(pieces), wst["i"] + NWP):
            j = wst["loaded"]
            l, pi, g = pieces[j]
            slot = j % NWP
            tk.dma("sp", wpool[slot][:], wsc[l, pi], "wp%d" % slot, [sc_b[l][g]], [wpool_b[slot]])
            wst["loaded"] += 1
        j = wst["i"]
        wst["i"] += 1
        return wpool[j % NWP], [wpool_b[j % NWP]]

    tk.newsem("xin0")
    tk.newsem("xin1")
    tk.newsem("sty0")
    tk.newsem("sty1")
    tk.newsem("misc")

    def misc_store(out, in_, reads):
        tk.dma("sp", out, in_, "misc", reads, ())
        tk.wait_all("sp", "misc")

    def misc_load(out, in_, writes):
        tk.dma("sp", out, in_, "misc", (), writes)
        tk.wait_all("sp", "misc")

    def barrier():
        snap = {k: tk.cnt[k] for k in ("pe", "act", "dve", "pool")}
        for e in ("pe", "act", "dve", "pool"):
            tk._wait(e, {k: v for k, v in snap.items() if v > 0})

    def ln_fm(N, gcol, bcol):
        psM = bigps()
        psQ = bigps()
        for m in range(8):
            cp("act", xTb[:, m, 0:N], xT[:, m, 0:N], [xT_b[m]], [xTb_b[m]])
            sq, sqb = sBn()
            act(sq[:, 0:N], xT[:, m, 0:N], AF.Square, [xT_b[m]], sqb)
            mm(psM.t[:, 0:N], ones1kb[:], xTb[:, m, 0:N], m == 0, m == 7, [xTb_b[m]], psM.b(), sig=(m == 7))
            mm(psQ.t[:, 0:N], ones1kb[:], sq[:, 0:N], m == 0, m == 7, sqb, psQ.b(), sig=True)
        t1, t1b = sFn()
        act(t1[:, 0:N], psM.t[:, 0:N], AF.Square, psM.b(), t1b)
        stt("dve", t1[:, 0:N], t1[:, 0:N], -1.0, psQ.t[:, 0:N], ALU.mult, ALU.add, t1b + psQ.b(), t1b)
        rsq(rstd_t[:, 0:N], t1[:, 0:N], 0, t1b, [ln_b])
        for m in range(8):
            t, tb = sFn()
            stt("dve", t[:, 0:N], xT[:, m, 0:N], 1.0, psM.t[:, 0:N], ALU.mult, ALU.subtract, [xT_b[m]] + psM.b(), tb)
            tt("dve", t[:, 0:N], t[:, 0:N], rstd_t[:, 0:N], ALU.mult, tb + [ln_b], tb)
            act(xT[:, m, 0:N], t[:, 0:N], AF.Identity, tb, [xT_b[m]],
                bias=pcol[:, bcol + m:bcol + m + 1], scale=pcol[:, gcol + m:gcol + m + 1])
            act(xTb[:, m, 0:N], t[:, 0:N], AF.Identity, tb, [xTb_b[m]],
                bias=pcol[:, bcol + m:bcol + m + 1], scale=pcol[:, gcol + m:gcol + m + 1])

    cur_seq = -1
    tile_no = -1

    def chk(l, name):
        if stop is not None and stop == (tile_no, l, name):
            raise _Stop()

    try:
      if stop == "setup":
          raise _Stop()
      for (s, ti, N) in tiles:
          tile_no += 1
          NB = N // 128
          is_sample = (s == NSP)
          n_real = 64 if is_sample else N
          last_tile = is_sample or (ti == NTP - 1)
          if s != cur_seq:
              cur_seq = s
              if not is_sample:
                  for l in range(NL):
                      mset("pool", S32[:, l], 0.0, S32_b[l])
                      mset("pool", Sbf[:, l], 0.0, Sbf_b[l])
                      mset("pool", tails[:, l], 0.0, [tails_b[l]])
              else:
                  for l in range(NL):
                      misc_load(S32[:, l], sdelta[l].rearrange("h d e -> d h e"), S32_b[l])
                      cp("pool", Sbf[:, l], S32[:, l], S32_b[l], Sbf_b[l])
                      pq, pqb = smallq()
                      for cb in range(3):
                          t, tb = sFn()
                          misc_load(t[0:3, :], cconv[l][:, cb * 512:(cb + 1) * 512], tb)
                          for cc in range(4):
                              c = cb * 4 + cc
                              mm(pq[:, c * 3:c * 3 + 3], t[0:3, cc * 128:(cc + 1) * 128], ident_f[0:3, 0:3], True, True, tb, pqb, sig=(cc == 3))
                      cp("dve", tails[:, l].rearrange("p c r -> p (c r)"), pq[:, 0:36], pqb, [tails_b[l]])

          for blk in range(NB):
              slot = nxt("vg", 2)
              xi, xib = xin[slot], [xin_b[slot]]
              if is_sample:
                  mset("pool", xi[:], 0.0, xib)
                  tk.dma("sp", xi[0:64, :], xs, "xin%d" % slot, (), xib)
              else:
                  r0 = ti * 512 + blk * 128
                  tk.dma("sp", xi[:], xp[s, r0:r0 + 128, :], "xin%d" % slot, (), xib)
              for hf in range(2):
                  tk.op("dve", lambda e, hf=hf: e.bn_stats(out=st6[:, hf * 6:(hf + 1) * 6], in_=xi[:, hf * 512:(hf + 1) * 512]), xib, [mv_b])
              tk.op("dve", lambda e: e.bn_aggr(out=mv[:, 0:2], in_=st6[:, 0:12]), [mv_b], [mv_b])
              rsq(mv[:, 2:3], mv[:, 1:2], 0, [mv_b], [mv_b])
              stt("dve", mv[:, 3:4], mv[:, 0:1], -1.0, mv[:, 2:3], ALU.mult, ALU.mult, [mv_b], [mv_b])
              act(xi[:], xi[:], AF.Identity, xib + [mv_b], xib, bias=mv[:, 3:4], scale=mv[:, 2:3])
              for half in range(2):
                  ps = bigps()
                  for cc in range(4):
                      c = half * 4 + cc
                      mm(ps.t[:, cc * 128:(cc + 1) * 128], xi[:, c * 128:(c + 1) * 128], ident_f, True, True, xib, [ps.q[cc]], sig=(cc == 3))
                  for cc in range(4):
                      c = half * 4 + cc
                      act(xT[:, c, blk * 128:(blk + 1) * 128], ps.t[:, cc * 128:(cc + 1) * 128], AF.Identity, [ps.q[cc]], [xT_b[c]],
                          bias=pcol[:, 8 + c:9 + c], scale=pcol[:, c:c + 1])
          for c in range(8):
              cp("pool", xTb[:, c, 0:N], xT[:, c, 0:N], [xT_b[c]], [xTb_b[c]])

          chk(0, "p1")
          for l in range(NL):
              PC = 16 + l * LPC
              BC = l * LBC
              for pc in range(3):
                  wp, wpb = wnext()
                  for mi in range(4):
                      c = pc * 4 + mi
                      ps = bigps()
                      for k in range(8):
                          mm(ps.t[:, 0:N], wp[:, k * 512 + mi * 128:k * 512 + (mi + 1) * 128], xTb[:, k, 0:N],
                             k == 0, k == 7, wpb + [xTb_b[k]], ps.b(), sig=(k == 7))
                      cp("act", qkv_pre[:, c, 3:3 + N], ps.t[:, 0:N], ps.b(), [qp_b[c]])
                  if last_tile:
                      ps = bigps()
                      for k in range(8):
                          mm(ps.t[0:3, 0:512], xTb[:, k, n_real - 3:n_real], wp[:, k * 512:(k + 1) * 512],
                             k == 0, k == 7, wpb + [xTb_b[k]], ps.b(), sig=(k == 7))
                      t, tb = sFn()
                      cp("act", t[0:3, :], ps.t[0:3, 0:512], ps.b(), tb)
                      dst = ncs[l] if is_sample else ncp[l, s]
                      misc_store(dst[:, pc * 512:(pc + 1) * 512], t[0:3, :], tb)
              cp("pool", qkv_pre[:, :, 0:3], tails[:, l], [tails_b[l]], qp_b)
              cp("pool", tails[:, l], qkv_pre[:, :, N:N + 3], qp_b, [tails_b[l]])
              for c in range(12):
                  di = nxt("diag", 2)
                  for j in range(4):
                      act(diag[di][:, j, :], identb[:], AF.Identity, (), [diag_b[di]], scale=pcol[:, PC + 32 + c * 4 + j:PC + 33 + c * 4 + j])
                  ps = bigps()
                  for j in range(4):
                      mm(ps.t[:, 0:N], diag[di][:, j, :], qkv_pre[:, c, j:j + N], j == 0, j == 3, [diag_b[di], qp_b[c]], ps.b(), sig=(j == 3))
                  act(hid[:, c, 0:N], ps.t[:, 0:N], AF.Silu, ps.b(), [hid_b[c]])
              wp, wpb = wnext()
              for mi in range(4):
                  ps = bigps()
                  for k in range(8):
                      mm(ps.t[:, 0:N], wp[:, k * 512 + mi * 128:k * 512 + (mi + 1) * 128], xTb[:, k, 0:N],
                         k == 0, k == 7, wpb + [xTb_b[k]], ps.b(), sig=(k == 7))
                  act(hid[:, 12 + mi, 0:N], ps.t[:, 0:N], AF.Silu, ps.b(), [hid_b[12 + mi]])
              pq, pqb = smallq()
              for blk in range(NB):
                  for k in range(8):
                      mm(pq[:, blk * 8:(blk + 1) * 8], xTb[:, k, blk * 128:(blk + 1) * 128], wba[:, l, k * 8:(k + 1) * 8],
                         k == 0, k == 7, [xTb_b[k]], pqb, sig=(k == 7))
              NG = NB * 4
              pq3 = pq[:, 0:NB * 8].rearrange("p (b c) -> p b c", c=8)

              def g3(name):
                  return gt[name][:, 0:NG].rearrange("p (b c) -> p b c", c=4)
              gb = [gates_b]
              act(g3("tmpa"), pq3[:, :, 0:4], AF.Exp, pqb, gb, scale=-1.0)
              act(g3("sbn"), g3("tmpa"), AF.Ln, gb, gb, bias=epsc[:, 3:4])
              ts1("dve", gt["tmpa"][:, 0:NG], gt["tmpa"][:, 0:NG], 1.0, ALU.add, gb, gb)
              tk.op("dve", lambda e: e.reciprocal(out=gt["beta"][:, 0:NG], in_=gt["tmpa"][:, 0:NG]), gb, gb)
              tt("dve", g3("tmpb"), pq3[:, :, 4:8], pbc[:, BC + 1040:BC + 1040 + NG].rearrange("p (b c) -> p b c", c=4), ALU.add, pqb + gb, gb)
              act(gt["tmpb"][:, 0:NG], gt["tmpb"][:, 0:NG], AF.Exp, gb, gb)
              act(gt["tmpb"][:, 0:NG], gt["tmpb"][:, 0:NG], AF.Ln, gb, gb, bias=epsc[:, 3:4])
              tt("dve", gt["g"][:, 0:NG], gt["tmpb"][:, 0:NG], nexpA[:, l, 0:NG], ALU.mult, gb, gb)
              pg, pgb = smallq()
              mm(pg[:, 0:NG], Mu_f, gt["g"][:, 0:NG], True, True, gb, pgb, sig=False)
              mm(pg[:, 16:16 + NG], Bd_f, gt["g"][:, 0:NG], True, True, gb, pgb, sig=False)
              mm(pg[:, 32:32 + NG], E0_f, gt["g"][:, 0:NG], True, True, gb, pgb, sig=False)
              mm(pg[:, 48:48 + NG], E1_f, gt["g"][:, 0:NG], True, True, gb, pgb)
              cp("dve", gt["gc"][:, 0:NG], pg[:, 0:NG], pgb, gb)
              ts1("dve", gt["ngc"][:, 0:NG], gt["gc"][:, 0:NG], -1.0, ALU.mult, gb, gb)
              stt("dve", gt["gcb"][:, 0:NG], gt["sbn"][:, 0:NG], -1.0, gt["gc"][:, 0:NG], ALU.mult, ALU.add, gb, gb)
              act(gt["eg"][:, 0:NG], pg[:, 0:NG], AF.Exp, pgb, gb)
              tt("dve", gt["bEg"][:, 0:NG], gt["eg"][:, 0:NG], gt["beta"][:, 0:NG], ALU.mult, gb, gb)
              tt("dve", gt["tmpa"][:, 0:NG], pg[:, 16:16 + NG], gt["ngc"][:, 0:NG], ALU.add, pgb + gb, gb)
              act(gt["tmpa"][:, 0:NG], gt["tmpa"][:, 0:NG], AF.Exp, gb, gb)
              ts1("dve", gt["Ed0"][:, 0:NG], gt["tmpa"][:, 0:NG], E0_f[:, 0:1], ALU.mult, gb, gb)
              ts1("dve", gt["Ed1"][:, 0:NG], gt["tmpa"][:, 0:NG], E1_f[:, 0:1], ALU.mult, gb, gb)
              act(egl[:, 0:NG], pg[:, 32:32 + NG], AF.Exp, pgb, gb)
              act(egl[:, 16:16 + NG], pg[:, 48:48 + NG], AF.Exp, pgb, gb)
              wp, wpb = wnext()
              for mi in range(4):
                  ps = bigps()
                  for k in range(8):
                      mm(ps.t[:, 0:N], wp[:, k * 512 + mi * 128:k * 512 + (mi + 1) * 128], xTb[:, k, 0:N],
                         k == 0, k == 7, wpb + [xTb_b[k]], ps.b(), sig=(k == 7))
                  act(hid[:, 16 + mi, 0:N], ps.t[:, 0:N], AF.Gelu, ps.b(), [hid_b[16 + mi]])
              wp, wpb = wnext()
              for blk in range(NB):
                  ps = bigps()
                  for k in range(8):
                      mm(ps.t[:, 0:512], xTb[:, k, blk * 128:(blk + 1) * 128], wp[:, k * 512:(k + 1) * 512],
                         k == 0, k == 7, wpb + [xTb_b[k]], ps.b(), sig=(k == 7))
                  vi = nxt("vg", 2)
                  v_, vb_ = vg[vi], [vg_b[vi]]
                  act(v_[:], ps.t[:, 0:512], AF.Gelu, ps.b(), vb_)
                  tk.op("dve", lambda e: e.bn_stats(out=st6[:, 0:6], in_=v_[:]), vb_, [mv_b])
                  tk.op("dve", lambda e: e.bn_aggr(out=mv[:, 0:2], in_=st6[:, 0:6]), [mv_b], [mv_b])
                  rsq(mv[:, 2:3], mv[:, 1:2], 0, [mv_b], [mv_b])
                  stt("dve", mv[:, 3:4], mv[:, 0:1], -1.0, mv[:, 2:3], ALU.mult, ALU.mult, [mv_b], [mv_b])
                  act(v_[:], v_[:], AF.Identity, vb_ + [mv_b], vb_, bias=mv[:, 3:4], scale=mv[:, 2:3])
                  tt("dve", v_[:], v_[:], pbc[:, BC:BC + 512], ALU.mult, vb_, vb_)
                  if is_sample:
                      tt("dve", v_[:], v_[:], pbc[:, BC + 512:BC + 1024], ALU.add, vb_, vb_)
                      cp("pool", hid[:, 20 + blk, :], v_[:], vb_, [hid_b[20 + blk]])
                      misc_store(nvs[l], v_[0:64, :], vb_)
                  else:
                      tt("dve", hid[:, 20 + blk, :], v_[:], pbc[:, BC + 512:BC + 1024], ALU.add, vb_, [hid_b[20 + blk]])

              chk(l, "p2")
              for c in range(8):
                  sq, sqb = sBn()
                  act(sq[:, 0:N], hid[:, c, 0:N], AF.Square, [hid_b[c]], sqb)
                  ps = bigps()
                  if c < 4:
                      mm(ps.t[:, 0:N], ones128b[:], sq[:, 0:N], True, True, sqb, ps.b())
                      eps_ = 2
                  else:
                      mm(ps.t[:, 0:N], onesb[:], sq[:, 0:N], True, True, sqb, ps.b())
                      eps_ = 1
                  t, tb = sFn()
                  rsq(t[:, 0:N], ps.t[:, 0:N], eps_, ps.b(), tb)
                  tt("dve", hid[:, c, 0:N], hid[:, c, 0:N], t[:, 0:N], ALU.mult, [hid_b[c]] + tb, [hid_b[c]])

              chk(l, "g1")
              for blk in range(NB):
                  bs = slice(blk * 128, (blk + 1) * 128)
                  chk(l, "g5b%d" % blk)
                  for h in range(4):
                      H = hd[h]
                      idx = blk * 4 + h
                      qb_, kb_, vb2_ = [hid_b[h]], [hid_b[4 + h]], [hid_b[8 + h]]
                      di = nxt("dg", 2)
                      act(dgp[di][:, 0:128], identb[:], AF.Identity, gb, [dgp_b[di]], scale=gt["gc"][:, idx:idx + 1])
                      act(dgp[di][:, 128:256], identb[:], AF.Identity, gb, [dgp_b[di]], scale=gt["gcb"][:, idx:idx + 1])
                      ps = bigps()
                      mm(ps.t[:, 0:256], ones_f, dgp[di][:], True, True, [dgp_b[di]], ps.b(0, 256), sig=False)
                      mm(ps.t[:, 256:512], hid[:, 4 + h, bs], hid[:, 0:8, :].rearrange("p (a h) n -> p a h n", a=2)[:, :, h, bs], True, True, qb_ + kb_, ps.b(256, 512))
                      ti_ = nxt("tmp", 2)
                      stt("dve", tmpp[ti_][:], ps.t[:, 0:256], gt["ngc"][:, idx:idx + 1], maskneg2, ALU.add, ALU.add,
                          ps.b(0, 256) + gb, [tmpp_b[ti_]])
                      ei = nxt("E", 2)
                      act(Ep[ei][:], tmpp[ti_][:], AF.Exp, [tmpp_b[ti_]], [Ep_b[ei]])
                      act(H["EG"][:], ps.t[:, 0:128], AF.Exp, ps.b(0, 128), [H["EG_b"]])
                      yx0, yx0b = H["YX"][0], [H["YX_b"][0]]
                      tt("dve", H["attnT"][:], ps.t[:, 256:384], Ep[ei][:, 0:128], ALU.mult, ps.b(256, 384) + [Ep_b[ei]], [H["attnT_b"]])
                      tt("dve", yx0[:, 0:128], ps.t[:, 384:512], Ep[ei][:, 128:256], ALU.mult, ps.b(384, 512) + [Ep_b[ei]], yx0b)
                      stt("dve", H["P"][:], yx0[:, 0:128], -1.0, identb[:], ALU.mult, ALU.add, yx0b, [H["P_b"]])
                      tt("pool", H["qdT"][:], hid[:, h, bs], H["EG"][:], ALU.mult, qb_ + [H["EG_b"]], [H["qdT_b"]])
                  pt, ptb = pTbank()
                  for h in range(4):
                      H = hd[h]
                      mm(pt[:, h * 128:(h + 1) * 128], H["YX"][0][:, 0:128], identb[:], True, True, [H["YX_b"][0]], ptb, sig=(h == 3))
                  for h in range(4):
                      H = hd[h]
                      cp("act", H["YX"][0][:, 128:256], pt[:, h * 128:(h + 1) * 128], ptb, [H["YX_b"][0]])
                  pk, pkb = pTbank()
                  for h in range(4):
                      mm(pk[:, h * 128:(h + 1) * 128], hid[:, 4 + h, bs], identb[:], True, True, [hid_b[4 + h]], pkb, sig=(h == 3))
                  for h in range(4):
                      H = hd[h]
                      idx = blk * 4 + h
                      pks = pk[:, h * 128:(h + 1) * 128]
                      ts1("dve", H["kbg"][:], pks, gt["bEg"][:, idx:idx + 1], ALU.mult, pkb + gb, [H["kbg_b"]])
                      ts1("dve", H["kd0"][:], pks, gt["Ed0"][:, idx:idx + 1], ALU.mult, pkb + gb, [H["kd0_b"]])
                      ts1("dve", H["kd1"][:], pks, gt["Ed1"][:, idx:idx + 1], ALU.mult, pkb + gb, [H["kd1_b"]])
                  pv, pvb = pTbank()
                  for h in range(4):
                      mm(pv[:, h * 128:(h + 1) * 128], hid[:, 8 + h, bs], identb[:], True, True, [hid_b[8 + h]], pvb, sig=(h == 3))
                  for h in range(4):
                      H = hd[h]
                      idx = blk * 4 + h
                      ts1("dve", H["vb"][:], pv[:, h * 128:(h + 1) * 128], gt["beta"][:, idx:idx + 1], ALU.mult, pvb + gb, [H["vb_b"]])
                  chk(l, "g2b%d" % blk)
                  for k in range(6):
                      for h in range(4):
                          H = hd[h]
                          cur, curb = H["YX"][k % 2], [H["YX_b"][k % 2]]
                          nx_, nxb = H["YX"][(k + 1) % 2], [H["YX_b"][(k + 1) % 2]]
                          ps = bigps()
                          if k >= 1:
                              mm(ps.t[:, 0:128], cur[:, 128:256], H["P"][:], True, True, curb + [H["P_b"]], ps.b(0, 128), sig=(k == 5))
                          if k <= 4:
                              mm(ps.t[:, 128:256], cur[:, 128:256], cur[:, 0:128], True, True, curb, ps.b(128, 256), sig=False)
                              mm(ps.t[:, 256:384], cur[:, 0:128], cur[:, 128:256], True, True, curb, ps.b(256, 384))
                              cp("dve", nx_[:], ps.t[:, 128:384], ps.b(128, 384), nxb)
                          if k >= 1:
                              tt("dve", H["P"][:], H["P"][:], ps.t[:, 0:128], ALU.add, [H["P_b"]] + ps.b(0, 128), [H["P_b"]])
                      chk(l, "n%db%d" % (k, blk))
                  for h in range(4):
                      H = hd[h]
                      pq, pqb = smallq()
                      mm(pq, H["kbg"][:], H["P"][:], True, True, [H["kbg_b"], H["P_b"]], pqb)
                      act(H["nwT"][:], pq, AF.Identity, pqb, [H["nwT_b"]], scale=-1.0)
                  chk(l, "g3b%d" % blk)
                  for i in range(2):
                      cs = slice(i * 64, (i + 1) * 64)
                      pqs = []
                      for h in range(4):
                          H = hd[h]
                          pq, pqb = smallq()
                          mm(pq, H["P"][:], H["vb"][:], True, False, [H["P_b"], H["vb_b"]], pqb, sig=False)
                          mm(pq, H["nwT"][:], Sbf[:, l, h, :], False, True, [H["nwT_b"], Sbf_b[l][h]], pqb)
                          pqs.append((pq, pqb))
                      for h in range(4):
                          H = hd[h]
                          pq, pqb = pqs[h]
                          cp("act", H["vnew"][:], pq, pqb, [H["vnew_b"]])
                      pq2s = []
                      for h in range(4):
                          H = hd[h]
                          oc = slice(h * 128 + i * 64, h * 128 + (i + 1) * 64)
                          mm(psO.t[:, oc], Sbf[:, l, h, :], H["qdT"][:, cs], True, False, [Sbf_b[l][h], H["qdT_b"]], [psO.q[h]], sig=False)
                          mm(psO.t[:, oc], H["vnew"][:], H["attnT"][:, cs], False, True, [H["vnew_b"], H["attnT_b"]], [psO.q[h]], sig=False)
                          pq2, pq2b = smallq()
                          kd = H["kd0"] if i == 0 else H["kd1"]
                          kdb = H["kd0_b"] if i == 0 else H["kd1_b"]
                          mm(pq2, kd[:], H["vnew"][:], True, True, [kdb, H["vnew_b"]], pq2b)
                          pq2s.append((pq2, pq2b))
                      for h in range(4):
                          idx = blk * 4 + h
                          pq2, pq2b = pq2s[h]
                          stt("dve", S32[:, l, h, :], S32[:, l, h, :], egl[:, i * 16 + idx:i * 16 + idx + 1], pq2, ALU.mult, ALU.add,
                              [S32_b[l][h]] + pq2b + gb, [S32_b[l][h]])
                          cp("act", Sbf[:, l, h, :], S32[:, l, h, :], [S32_b[l][h]], [Sbf_b[l][h]])
                      if is_sample and i == 0:
                          misc_store(nds[l].rearrange("h d e -> d h e"), S32[:, l], S32_b[l])
                  chk(l, "g4b%d" % blk)
                  sq, sqb = sBn()
                  act(sq[:, 0:512], psO.t[:, 0:512], AF.Square, psO.b(), sqb)
                  ps = bigps()
                  mm(ps.t[:, 0:512], onesrmsb[:], sq[:, 0:512], True, True, sqb, ps.b())
                  t, tb = sFn()
                  rsq(t[:, 0:512], ps.t[:, 0:512], 1, ps.b(), tb)
                  tt("dve", t[:, 0:512], t[:, 0:512], psO.t[:, 0:512], ALU.mult, tb + psO.b(), tb)
                  stt("dve", qkv_pre[:, 0:4, bs], t[:, 0:512].rearrange("p (h c) -> p h c", h=4), pcol[:, PC + 80:PC + 81],
                      hid[:, 12:16, bs], ALU.mult, ALU.mult, tb + hid_b[12:16], qp_b[0:4])
              if last_tile and not is_sample:
                  misc_store(ndp[l, s].rearrange("h d e -> d h e"), S32[:, l], S32_b[l])

              chk(l, "gdn")
              for g in range(4):
                  ps = bigps()
                  for blk in range(NB):
                      bs = slice(blk * 128, (blk + 1) * 128)
                      mm(ps.t[:, bs], onesb[:], bsmat[:, l * 4 + g, :], True, False, (), ps.b(), sig=False)
                      mm(ps.t[:, bs], hid[:, 20 + blk, g * 128:(g + 1) * 128], WsT[:, l * 4 + g, :], False, True, [hid_b[20 + blk]], ps.b())
                  tt("dve", qkv_pre[:, 4 + g, 0:N], ps.t[:, 0:N], hid[:, 16 + g, 0:N], ALU.mult, ps.b() + [hid_b[16 + g]], [qp_b[4 + g]])

              chk(l, "mix")
              for m in range(8):
                  if m % 4 == 0:
                      wp, wpb = wnext()
                  mi = m % 4
                  ps = bigps()
                  for k in range(8):
                      mm(ps.t[:, 0:N], wp[:, k * 512 + mi * 128:k * 512 + (mi + 1) * 128], qkv_pre[:, k, 0:N],
                         k == 0, k == 7, wpb + [qp_b[k]], ps.b(), sig=(k == 7))
                  stt("dve", xT[:, m, 0:N], xT[:, m, 0:N], ALPHA, ps.t[:, 0:N], ALU.mult, ALU.add, [xT_b[m]] + ps.b(), [xT_b[m]])
              chk(l, "res1")
              ln_fm(N, PC + 0, PC + 8)
              chk(l, "ln1")

              for j in range(8):
                  wp, wpb = wnext()
                  for fi in range(4):
                      f = j * 4 + fi
                      ps = bigps()
                      for k in range(8):
                          mm(ps.t[:, 0:N], wp[:, k * 512 + fi * 128:k * 512 + (fi + 1) * 128], xTb[:, k, 0:N],
                             k == 0, k == 7, wpb + [xTb_b[k]], ps.b(), sig=(k == 7))
                      t, tb = sFn()
                      act(t[:, 0:N], ps.t[:, 0:N], AF.Relu, ps.b(), tb)
                      tt("dve", hid[:, f, 0:N], t[:, 0:N], t[:, 0:N], ALU.mult, tb, [hid_b[f]])
              for m in range(8):
                  wp, wpb = wnext()
                  ps = bigps()
                  for f in range(32):
                      mm(ps.t[:, 0:N], wp[:, f * 128:(f + 1) * 128], hid[:, f, 0:N], f == 0, f == 31, wpb + [hid_b[f]], ps.b(), sig=(f == 31))
                  stt("dve", xT[:, m, 0:N], xT[:, m, 0:N], ALPHA, ps.t[:, 0:N], ALU.mult, ALU.add, [xT_b[m]] + ps.b(), [xT_b[m]])
              chk(l, "res2")
              ln_fm(N, PC + 16, PC + 24)
              chk(l, "ln2")

          for blk in range(NB):
              slot = nxt("vg", 2)
              yo, yob = xin[slot], [xin_b[slot]]
              for half in range(2):
                  ps = bigps()
                  for cc in range(4):
                      c = half * 4 + cc
                      mm(ps.t[:, cc * 128:(cc + 1) * 128], xT[:, c, blk * 128:(blk + 1) * 128], ident_f, True, True, [xT_b[c]], [ps.q[cc]], sig=(cc == 3))
                  cp("act" if half else "dve", yo[:, half * 512:(half + 1) * 512], ps.t[:, 0:512], ps.b(), yob)
              if is_sample:
                  tk.dma("sp", ys, yo[0:64, :], "sty%d" % slot, yob, ())
              else:
                  r0 = ti * 512 + blk * 128
                  tk.dma("sp", yp[s, r0:r0 + 128, :], yo[:], "sty%d" % slot, yob, ())

    except _Stop:
        pass

    for key in ("sty0", "sty1", "misc"):
        tk.wait_all("sp", key)
    _LAST_CNT.clear()
    _LAST_CNT["sbuf_free"] = nc.sbuf_bytes_remaining
    _LAST_CNT.update(tk.cnt)
    return nc


def _consts():
    c = np.zeros((128, NCONST), np.float32)
    i = np.arange(128)
    P = i[:, None]
    Fr = i[None, :]
    same = (P // 64) == (Fr // 64)
    c[:, 0:128] = np.eye(128)
    c[:, 128:256] = 1.0
    c[:, 256:384] = (same & (P <= Fr))
    c[:, 384:512] = same
    c[:, 512:640] = (P < 64) * np.ones((1, 128))
    c[:, 640:768] = (P >= 64) * np.ones((1, 128))
    c[:, 768:896] = np.where(same & (P <= Fr), 0.0, NEG)
    c[:, 896:1024] = np.where(same & (P < Fr), 0.0, NEG)
    c[:, 1024:1152] = (P <= Fr)
    return c


def _pack_small(inp):
    f = lambda a: np.asarray(a, np.float32)
    col = lambda v: f(v).reshape(-1, 128).T
    pcol = np.zeros((128, 16 + NL * LPC), np.float32)
    pcol[:, 0:8] = col(inp["ln0_g"])
    pcol[:, 8:16] = col(inp["ln0_b"])
    pbc = np.zeros((128, NL * LBC), np.float32)
    for l in range(NL):
        b = 16 + l * LPC
        pcol[:, b:b + 8] = col(inp["ln1_g"][l])
        pcol[:, b + 8:b + 16] = col(inp["ln1_b"][l])
        pcol[:, b + 16:b + 24] = col(inp["ln2_g"][l])
        pcol[:, b + 24:b + 32] = col(inp["ln2_b"][l])
        cw = f(inp["conv_w"][l])
        pcol[:, b + 32:b + 80] = cw.reshape(4, 12, 128).transpose(2, 1, 0).reshape(128, 48)
        pcol[:, b + 80] = f(inp["gdn_norm_g"][l])
        o = l * LBC
        pbc[:, o:o + 512] = f(inp["smlp_ln_g"][l])[None, :]
        pbc[:, o + 512:o + 1024] = f(inp["smlp_ln_b"][l])[None, :]
        pbc[:, o + 1024:o + 1040] = np.tile(f(inp["a_log"][l]), 4)[None, :]
        pbc[:, o + 1040:o + 1056] = np.tile(f(inp["dt_bias"][l]), 4)[None, :]
    bsrow = f(inp["b_s"]).reshape(1, NL * 4 * 128)
    ws = f(inp["w_s"]).reshape(NL * 4, 128, 128)
    return pcol, pbc, bsrow, ws


SEQUENTIAL_LAUNCH = False
_NC_CACHE = {}
_NOCAST = [0]
_DELAY = [0]
_LAST_CNT = {}
_STOP = [None]


def run_cores(inp, TP, NSP, ncores):
    key = (TP, NSP, _STOP[0])
    if key not in _NC_CACHE:
        _NC_CACHE[key] = build(TP, NSP, _STOP[0])
    nc = _NC_CACHE[key]
    f = lambda a: np.ascontiguousarray(np.asarray(a, np.float32))
    pcol, pbc, bsrow, ws = _pack_small(inp)
    consts = _consts()
    shared = {"w_in": f(inp["w_in"]), "w_out": f(inp["w_out"]), "w_up": f(inp["w_up"]), "w_down": f(inp["w_down"]),
              "ws": ws, "consts": consts, "pcol": pcol, "pbc": pbc, "bsrow": bsrow}
    xp = f(inp["x_prompt"])
    xs = f(inp["x_sample"])
    cc = f(inp["cache_conv"])
    sd = f(inp["state_delta"])
    in_maps = []
    for i in range(ncores):
        m = dict(shared)
        m["xp"] = np.ascontiguousarray(xp[i * NSP:(i + 1) * NSP])
        m["xs"] = np.ascontiguousarray(xs[i])
        m["cconv"] = np.ascontiguousarray(cc[:, i])
        m["sdelta"] = np.ascontiguousarray(sd[:, i])
        in_maps.append(m)
    if SEQUENTIAL_LAUNCH and ncores > 1:
        R = []
        for i in range(ncores):
            R.append(run_bass_kernel_spmd(nc, [in_maps[i]], core_ids=[0]).results[0])
    else:
        R = run_bass_kernel_spmd(nc, in_maps, core_ids=list(range(ncores))).results
    y_p = np.concatenate([r["yp"] for r in R], axis=0)
    y_s = np.stack([r["ys"] for r in R], axis=0)
    ncp = np.concatenate([r["ncp"] for r in R], axis=1)
    ndp = np.concatenate([r["ndp"] for r in R], axis=1)
    ncs = np.stack([r["ncs"] for r in R], axis=1)
    nds = np.stack([r["nds"] for r in R], axis=1)
    nvs = np.stack([r["nvs"] for r in R], axis=1)
    return tuple(np.ascontiguousarray(a, dtype=np.float32) for a in (y_p, y_s, ncp, ndp, ncs, nds, nvs))


def kernel(**inputs):
    return run_cores(inputs, 4096, 2, 8)
```

```python
import numpy as np
import concourse.bass as bass
import concourse.mybir as mybir
from concourse.bass_utils import run_bass_kernel_spmd

F32 = mybir.dt.float32
BF16 = mybir.dt.bfloat16
AF = mybir.ActivationFunctionType
ALU = mybir.AluOpType

NL = 2
D = 1024
INW = 3080
ALPHA = float((2 * NL) ** 0.25)
LN_EPS = 1e-5
NEPS = 1e-6
NEG = -30000.0
NWP = 8
NBIG = 5
LPC = 81
LBC = 1056
NCONST = 1152


class Buf:
    __slots__ = ("w", "r", "name")

    def __init__(self, name=""):
        self.w = None
        self.r = {}
        self.name = name


class TK:
    def __init__(self, nc):
        self.nc = nc
        self.E = {"pe": nc.tensor, "act": nc.scalar, "dve": nc.vector, "pool": nc.gpsimd, "sp": nc.sync}
        self.sem = {}
        self.cnt = {}
        self.waited = {e: {} for e in self.E}
        for e in self.E:
            self.newsem(e)

    def newsem(self, key):
        self.sem[key] = self.nc.alloc_semaphore(name="s_" + key)
        self.cnt[key] = 0

    def _needs(self, reads, writes):
        nd = {}
        for b in reads:
            if b.w is not None:
                k, v = b.w
                if nd.get(k, 0) < v:
                    nd[k] = v
        for b in writes:
            if b.w is not None:
                k, v = b.w
                if nd.get(k, 0) < v:
                    nd[k] = v
            for k, v in b.r.items():
                if nd.get(k, 0) < v:
                    nd[k] = v
        return nd

    def _wait(self, e, nd):
        w = self.waited[e]
        for k, v in nd.items():
            if e == "pe" and k == "pe":
                continue
            if w.get(k, 0) < v:
                self.E[e].wait_ge(self.sem[k], v)
                w[k] = v

    def op(self, e, fn, reads=(), writes=(), sig=True):
        self._wait(e, self._needs(reads, writes))
        ins = fn(self.E[e])
        if sig:
            self.cnt[e] += 1
            ins.then_inc(self.sem[e], 1)
            c = self.cnt[e]
        else:
            c = self.cnt[e] + 1
        for b in reads:
            if b.r.get(e, 0) < c:
                b.r[e] = c
        for b in writes:
            b.w = (e, c)
            b.r = {}

    def dma(self, q, out, in_, key, reads=(), writes=()):
        self._wait(q, self._needs(reads, writes))
        self.E[q].dma_start(out=out, in_=in_).then_inc(self.sem[key], 16)
        self.cnt[key] += 16
        c = self.cnt[key]
        for b in reads:
            if b.r.get(key, 0) < c:
                b.r[key] = c
        for b in writes:
            b.w = (key, c)
            b.r = {}

    def wait_all(self, e, key):
        v = self.cnt[key]
        if v > 0 and self.waited[e].get(key, 0) < v:
            self.E[e].wait_ge(self.sem[key], v)
            self.waited[e][key] = v


class PSB:
    def __init__(self, t, name):
        self.t = t
        self.whole = Buf(name)
        self.q = [self.whole] * 4

    def b(self, c0=0, c1=512):
        return [self.whole]


class _Stop(Exception):
    pass


def build(TP, NSP=2, stop=None):
    nc = bass.Bass("TRN2", target_bir_lowering=False)
    dt = nc.dram_tensor
    xp = dt("xp", [NSP, TP, D], F32, kind="ExternalInput").ap()
    xs = dt("xs", [64, D], F32, kind="ExternalInput").ap()
    cconv = dt("cconv", [NL, 3, 1536], F32, kind="ExternalInput").ap()
    sdelta = dt("sdelta", [NL, 4, 128, 128], F32, kind="ExternalInput").ap()
    w_in = dt("w_in", [NL, D, INW], F32, kind="ExternalInput").ap()
    w_out = dt("w_out", [NL, D, D], F32, kind="ExternalInput").ap()
    w_up = dt("w_up", [NL, D, 4096], F32, kind="ExternalInput").ap()
    w_down = dt("w_down", [NL, 4096, D], F32, kind="ExternalInput").ap()
    ws_d = dt("ws", [NL * 4, 128, 128], F32, kind="ExternalInput").ap()
    consts_d = dt("consts", [128, NCONST], F32, kind="ExternalInput").ap()
    pcol_d = dt("pcol", [128, 16 + NL * LPC], F32, kind="ExternalInput").ap()
    pbc_d = dt("pbc", [128, NL * LBC], F32, kind="ExternalInput").ap()
    bsrow_d = dt("bsrow", [1, NL * 4 * 128], F32, kind="ExternalInput").ap()

    yp = dt("yp", [NSP, TP, D], F32, kind="ExternalOutput").ap()
    ys = dt("ys", [64, D], F32, kind="ExternalOutput").ap()
    ncp = dt("ncp", [NL, NSP, 3, 1536], F32, kind="ExternalOutput").ap()
    ndp = dt("ndp", [NL, NSP, 4, 128, 128], F32, kind="ExternalOutput").ap()
    ncs = dt("ncs", [NL, 3, 1536], F32, kind="ExternalOutput").ap()
    nds = dt("nds", [NL, 4, 128, 128], F32, kind="ExternalOutput").ap()
    nvs = dt("nvs", [NL, 64, 512], F32, kind="ExternalOutput").ap()

    wsc = dt("wsc", [NL, 24, 128, 4096], BF16, kind="Internal").ap()
    wbasc = dt("wbasc", [NL, 128, 64], BF16, kind="Internal").ap()

    tk = TK(nc)
    _n = [0]

    def sb(shape, dtype, name=None):
        _n[0] += 1
        return nc.alloc_sbuf_tensor("sb_" + (name or ("t%d" % _n[0])), shape, dtype)

    cst = sb([128, NCONST], F32, "cst")
    pcol = sb([128, 16 + NL * LPC], F32, "pcol")
    pbc = sb([128, NL * LBC], F32, "pbc")
    epsc = sb([128, 4], F32, "epsc")
    identb = sb([128, 128], BF16, "identb")
    onesb = sb([128, 128], BF16, "onesb")
    ones128b = sb([128, 128], BF16, "ones128b")
    ones1kb = sb([128, 128], BF16, "ones1kb")
    onesrmsb = sb([128, 128], BF16, "onesrmsb")
    WsT = sb([128, 8, 128], BF16, "WsT")
    bsmat = sb([128, 8, 128], BF16, "bsmat")
    nexpA = sb([128, NL, 16], F32, "nexpA")
    wba = sb([128, NL, 64], BF16, "wba")
    wpool = [sb([128, 4096], BF16, "wp%d" % i) for i in range(NWP)]
    wpool_b = [Buf("wp%d" % i) for i in range(NWP)]
    xin = [sb([128, 1024], F32, "xin%d" % i) for i in range(2)]
    xin_b = [Buf("xin%d" % i) for i in range(2)]
    xT = sb([128, 8, 512], F32, "xT")
    xT_b = [Buf("xT%d" % i) for i in range(8)]
    xTb = sb([128, 8, 512], BF16, "xTb")
    xTb_b = [Buf("xTb%d" % i) for i in range(8)]
    qkv_pre = sb([128, 12, 515], BF16, "qkv_pre")
    qp_b = [Buf("qp%d" % i) for i in range(12)]
    tails = sb([128, NL, 12, 3], BF16, "tails")
    tails_b = [Buf("tails%d" % i) for i in range(NL)]
    diag = [sb([128, 4, 128], BF16, "diag%d" % i) for i in range(2)]
    diag_b = [Buf("diag%d" % i) for i in range(2)]
    hid = sb([128, 32, 512], BF16, "hid")
    hid_b = [Buf("hid%d" % i) for i in range(32)]
    vg = [sb([128, 512], F32, "vg%d" % i) for i in range(2)]
    vg_b = [Buf("vg%d" % i) for i in range(2)]
    sF = [sb([128, 512], F32, "sF%d" % i) for i in range(4)]
    sF_b = [Buf("sF%d" % i) for i in range(4)]
    sB = [sb([128, 512], BF16, "sB%d" % i) for i in range(2)]
    sB_b = [Buf("sB%d" % i) for i in range(2)]
    rstd_t = sb([128, 512], F32, "rstd")
    nmr_t = sb([128, 512], F32, "nmr")
    ln_b = Buf("lnstat")
    S32 = sb([128, NL, 4, 128], F32, "S32")
    Sbf = sb([128, NL, 4, 128], BF16, "Sbf")
    S32_b = [[Buf("S32_%d%d" % (l, h)) for h in range(4)] for l in range(NL)]
    Sbf_b = [[Buf("Sbf_%d%d" % (l, h)) for h in range(4)] for l in range(NL)]
    gnames = ["beta", "g", "sbn", "gc", "ngc", "gcb", "bEg", "Ed0", "Ed1", "eg", "tmpa", "tmpb"]
    gt = {n: sb([128, 16], F32, "g_" + n) for n in gnames}
    egl = sb([128, 32], F32, "g_egl")
    gates_b = Buf("gates")
    mv = sb([128, 16], F32, "mv")
    st6 = sb([128, 12], F32, "st6")
    mv_b = Buf("mv")
    hd = []
    for h in range(4):
        d_ = {}
        for n in ["attnT", "P", "kbg", "kd0", "kd1", "vb", "nwT", "qdT", "vnew", "EG"]:
            d_[n] = sb([128, 128], BF16, "h%d_%s" % (h, n))
            d_[n + "_b"] = Buf("h%d_%s" % (h, n))
        d_["YX"] = [sb([128, 256], BF16, "h%d_YX%d" % (h, i)) for i in range(2)]
        d_["YX_b"] = [Buf("h%d_YX%d" % (h, i)) for i in range(2)]
        hd.append(d_)
    dgp = [sb([128, 256], F32, "dg%d" % i) for i in range(2)]
    dgp_b = [Buf("dg%d" % i) for i in range(2)]
    tmpp = [sb([128, 256], F32, "tmpE%d" % i) for i in range(2)]
    tmpp_b = [Buf("tmpE%d" % i) for i in range(2)]
    Ep = [sb([128, 256], F32, "E%d" % i) for i in range(2)]
    Ep_b = [Buf("E%d" % i) for i in range(2)]

    big = [PSB(nc.alloc_psum_tensor("pbig%d" % i, [128, 512], F32), "pbig%d" % i) for i in range(5)]
    psO = PSB(nc.alloc_psum_tensor("psO", [128, 512], F32), "psO")
    pT = [nc.alloc_psum_tensor("pT%d" % i, [128, 512], F32) for i in range(2)]
    pT_b = [Buf("pT%d" % i) for i in range(2)]
    rot = {"big": 0, "small": 0, "pT": 0, "sF": 0, "sB": 0, "vg": 0, "dg": 0, "tmp": 0, "E": 0, "diag": 0}

    def nxt(key, n):
        i = rot[key]
        rot[key] = (i + 1) % n
        return i

    def bigps():
        return big[nxt("big", NBIG)]

    def smallq():
        p_ = bigps()
        return p_.t[:, 0:128], p_.b()

    def pTbank():
        i = nxt("pT", 2)
        return pT[i], [pT_b[i]]

    def sFn():
        i = nxt("sF", 4)
        return sF[i], [sF_b[i]]

    def sBn():
        i = nxt("sB", 2)
        return sB[i], [sB_b[i]]

    def mm(out, lhsT, rhs, start, stop, reads, writes, sig=True):
        tk.op("pe", lambda e: e.matmul(out, lhsT=lhsT, rhs=rhs, start=start, stop=stop), reads, writes, sig)

    def act(out, in_, func, reads, writes, bias=None, scale=None):
        kw = {}
        if bias is not None:
            kw["bias"] = bias
        if scale is not None:
            kw["scale"] = scale
        tk.op("act", lambda e: e.activation(out=out, in_=in_, func=func, **kw), reads, writes)

    def ts(eng, out, in0, s1, s2, op0, op1, reads, writes):
        tk.op(eng, lambda e: e.tensor_scalar(out=out, in0=in0, scalar1=s1, scalar2=s2, op0=op0, op1=op1), reads, writes)

    def rsq(out, in0, epscol, reads, writes):
        act(out, in0, AF.Sqrt, reads, writes, bias=epsc[:, epscol:epscol + 1])
        tk.op("dve", lambda e: e.reciprocal(out=out, in_=out), writes, writes)

    def ts1(eng, out, in0, s1, op0, reads, writes):
        tk.op(eng, lambda e: e.tensor_single_scalar(out=out, in_=in0, scalar=s1, op=op0), reads, writes)

    def stt(eng, out, in0, s, in1, op0, op1, reads, writes):
        tk.op(eng, lambda e: e.scalar_tensor_tensor(out=out, in0=in0, scalar=s, in1=in1, op0=op0, op1=op1), reads, writes)

    def tt(eng, out, in0, in1, op, reads, writes):
        tk.op(eng, lambda e: e.tensor_tensor(out=out, in0=in0, in1=in1, op=op), reads, writes)

    def cp(eng, out, in_, reads, writes):
        if eng == "act":
            act(out, in_, AF.Copy, reads, writes)
        else:
            tk.op(eng, lambda e: e.tensor_copy(out=out, in_=in_), reads, writes)

    def mset(eng, ap, v, writes):
        tk.op(eng, lambda e: e.memset(ap, v), (), writes)

    ident_f = cst[:, 0:128]
    ones_f = cst[:, 128:256]
    Mu_f = cst[:, 256:384]
    Bd_f = cst[:, 384:512]
    E0_f = cst[:, 512:640]
    E1_f = cst[:, 640:768]
    maskneg2 = cst[:, 768:1024]
    triU = cst[:, 1024:1152]

    sc_b = [[Buf("sc%d_%d" % (l, g)) for g in range(4)] for l in range(NL)]
    win_cols = [0, 512, 1024, 1536, 2056, 2568]
    for l in range(NL):
        for g in range(4):
            tk.newsem("cv%d_%d" % (l, g))
    tk.newsem("cvba")
    wba_sc_b = Buf("wbasc")
    for l in range(NL if not _NOCAST[0] else 0):
        for pi, c0 in enumerate(win_cols):
            tk.dma("pool", wsc[l, pi].rearrange("p (k c) -> p k c", k=8),
                   w_in[l][:, c0:c0 + 512].rearrange("(k p) c -> p k c", p=128), "cv%d_0" % l, (), [sc_b[l][0]])
        tk.dma("pool", wbasc[l].rearrange("p (k c) -> p k c", k=8),
               w_in[l][:, 2048:2056].rearrange("(k p) c -> p k c", p=128), "cvba", (), [wba_sc_b])
        for j in range(2):
            tk.dma("pool", wsc[l, 6 + j].rearrange("p (k c) -> p k c", k=8),
                   w_out[l][:, j * 512:(j + 1) * 512].rearrange("(k p) c -> p k c", p=128), "cv%d_1" % l, (), [sc_b[l][1]])
        for j in range(8):
            tk.dma("pool", wsc[l, 8 + j].rearrange("p (k c) -> p k c", k=8),
                   w_up[l][:, j * 512:(j + 1) * 512].rearrange("(k p) c -> p k c", p=128), "cv%d_2" % l, (), [sc_b[l][2]])
        for m in range(8):
            tk.dma("pool", wsc[l, 16 + m].rearrange("p (f c) -> p f c", f=32),
                   w_down[l][:, m * 128:(m + 1) * 128].rearrange("(f p) c -> p f c", p=128), "cv%d_3" % l, (), [sc_b[l][3]])

    tk.newsem("const")
    cbuf = Buf("const")
    tk.dma("sp", cst[:], consts_d, "const", (), [cbuf])
    tk.dma("sp", pcol[:], pcol_d, "const", (), [cbuf])
    tk.dma("sp", pbc[:], pbc_d, "const", (), [cbuf])
    tk.dma("sp", sF[0][0:1, :], bsrow_d[:, 0:512], "const", (), [cbuf])
    tk.dma("sp", sF[1][0:1, :], bsrow_d[:, 512:1024], "const", (), [cbuf])
    tk.dma("sp", xin[0][:].rearrange("p (a q) -> p a q", a=8), ws_d.rearrange("a p q -> p a q"), "const", (), [cbuf, xin_b[0]])
    tk.dma("sp", wba[:], wbasc.rearrange("l p c -> p l c"), "const", [wba_sc_b], [cbuf])
    for e in ("pe", "act", "dve", "pool"):
        tk.wait_all(e, "const")

    setup_b = Buf("setup")
    cp("dve", identb[:], ident_f, (), [setup_b])
    mset("dve", onesb[:], 1.0, [setup_b])
    mset("dve", epsc[:, 0:1], LN_EPS, [setup_b])
    mset("dve", epsc[:, 1:2], NEPS, [setup_b])
    mset("dve", epsc[:, 2:3], 128.0 * NEPS, [setup_b])
    mset("dve", epsc[:, 3:4], 1.0, [setup_b])
    mset("dve", ones128b[:], 128.0, [setup_b])
    mset("dve", ones1kb[:], 1.0 / 1024.0, [setup_b])
    mset("dve", onesrmsb[:], 1.0 / 128.0, [setup_b])
    mset("dve", bsmat[:], 0.0, [setup_b])
    for a in range(8):
        src = sF[a // 4][0:1, (a % 4) * 128:(a % 4 + 1) * 128]
        cp("dve", bsmat[0:1, a, :], src, [setup_b], [setup_b])
    for a in range(8):
        pq, pqb = smallq()
        mm(pq, xin[0][:, a * 128:(a + 1) * 128], ident_f, True, True, [setup_b, xin_b[0]], pqb)
        tt("dve", WsT[:, a, :], pq, triU, ALU.mult, pqb, [setup_b])
    for l in range(NL):
        act(nexpA[:, l, :], pbc[:, l * LBC + 1024:l * LBC + 1040], AF.Exp, [setup_b], [setup_b])
        ts1("dve", nexpA[:, l, :], nexpA[:, l, :], -1.0, ALU.mult, [setup_b], [setup_b])
    for e in ("pe", "act", "pool", "dve"):
        tk._wait(e, {"dve": tk.cnt["dve"], "act": tk.cnt["act"], "pe": tk.cnt["pe"]})

    if _DELAY[0]:
        for _i in range(_DELAY[0]):
            tk.op("act", lambda e: e.activation(out=vg[0][:, 0:512], in_=vg[1][:, 0:512], func=AF.Copy), (), [vg_b[0]])
    NTP = TP // 512
    tiles = []
    for s in range(NSP):
        for ti in range(NTP):
            tiles.append((s, ti, 512))
    tiles.append((NSP, 0, 128))
    pieces = []
    for (_s, _ti, _n) in tiles:
        for l in range(NL):
            for pi in range(24):
                g = 0 if pi < 6 else (1 if pi < 8 else (2 if pi < 16 else 3))
                pieces.append((l, pi, g))
    for i in range(NWP):
        tk.newsem("wp%d" % i)
    wst = {"i": 0, "loaded": 0}

    def wnext():
        while wst["loaded"] < min(len(pieces), wst["i"] + NWP):
            j = wst["loaded"]
            l, pi, g = pieces[j]
            slot = j % NWP
            tk.dma("sp", wpool[slot][:], wsc[l, pi], "wp%d" % slot, [sc_b[l][g]], [wpool_b[slot]])
            wst["loaded"] += 1
        j = wst["i"]
        wst["i"] += 1
        return wpool[j % NWP], [wpool_b[j % NWP]]

    tk.newsem("xin0")
    tk.newsem("xin1")
    tk.newsem("sty0")
    tk.newsem("sty1")
    tk.newsem("misc")

    def misc_store(out, in_, reads):
        tk.dma("sp", out, in_, "misc", reads, ())
        tk.wait_all("sp", "misc")

    def misc_load(out, in_, writes):
        tk.dma("sp", out, in_, "misc", (), writes)
        tk.wait_all("sp", "misc")

    def barrier():
        snap = {k: tk.cnt[k] for k in ("pe", "act", "dve", "pool")}
        for e in ("pe", "act", "dve", "pool"):
            tk._wait(e, {k: v for k, v in snap.items() if v > 0})

    def ln_fm(N, gcol, bcol):
        psM = bigps()
        psQ = bigps()
        for m in range(8):
            cp("act", xTb[:, m, 0:N], xT[:, m, 0:N], [xT_b[m]], [xTb_b[m]])
            sq, sqb = sBn()
            act(sq[:, 0:N], xT[:, m, 0:N], AF.Square, [xT_b[m]], sqb)
            mm(psM.t[:, 0:N], ones1kb[:], xTb[:, m, 0:N], m == 0, m == 7, [xTb_b[m]], psM.b(), sig=(m == 7))
            mm(psQ.t[:, 0:N], ones1kb[:], sq[:, 0:N], m == 0, m == 7, sqb, psQ.b(), sig=True)
        t1, t1b = sFn()
        act(t1[:, 0:N], psM.t[:, 0:N], AF.Square, psM.b(), t1b)
        stt("dve", t1[:, 0:N], t1[:, 0:N], -1.0, psQ.t[:, 0:N], ALU.mult, ALU.add, t1b + psQ.b(), t1b)
        rsq(rstd_t[:, 0:N], t1[:, 0:N], 0, t1b, [ln_b])
        for m in range(8):
            t, tb = sFn()
            stt("dve", t[:, 0:N], xT[:, m, 0:N], 1.0, psM.t[:, 0:N], ALU.mult, ALU.subtract, [xT_b[m]] + psM.b(), tb)
            tt("dve", t[:, 0:N], t[:, 0:N], rstd_t[:, 0:N], ALU.mult, tb + [ln_b], tb)
            act(xT[:, m, 0:N], t[:, 0:N], AF.Identity, tb, [xT_b[m]],
                bias=pcol[:, bcol + m:bcol + m + 1], scale=pcol[:, gcol + m:gcol + m + 1])
            act(xTb[:, m, 0:N], t[:, 0:N], AF.Identity, tb, [xTb_b[m]],
                bias=pcol[:, bcol + m:bcol + m + 1], scale=pcol[:, gcol + m:gcol + m + 1])

    cur_seq = -1
    tile_no = -1

    def chk(l, name):
        if stop is not None and stop == (tile_no, l, name):
            raise _Stop()

    try:
      if stop == "setup":
          raise _Stop()
      for (s, ti, N) in tiles:
          tile_no += 1
          NB = N // 128
          is_sample = (s == NSP)
          n_real = 64 if is_sample else N
          last_tile = is_sample or (ti == NTP - 1)
          if s != cur_seq:
              cur_seq = s
              if not is_sample:
                  for l in range(NL):
                      mset("pool", S32[:, l], 0.0, S32_b[l])
                      mset("pool", Sbf[:, l], 0.0, Sbf_b[l])
                      mset("pool", tails[:, l], 0.0, [tails_b[l]])
              else:
                  for l in range(NL):
                      misc_load(S32[:, l], sdelta[l].rearrange("h d e -> d h e"), S32_b[l])
                      cp("pool", Sbf[:, l], S32[:, l], S32_b[l], Sbf_b[l])
                      pq, pqb = smallq()
                      for cb in range(3):
                          t, tb = sFn()
                          misc_load(t[0:3, :], cconv[l][:, cb * 512:(cb + 1) * 512], tb)
                          for cc in range(4):
                              c = cb * 4 + cc
                              mm(pq[:, c * 3:c * 3 + 3], t[0:3, cc * 128:(cc + 1) * 128], ident_f[0:3, 0:3], True, True, tb, pqb, sig=(cc == 3))
                      cp("dve", tails[:, l].rearrange("p c r -> p (c r)"), pq[:, 0:36], pqb, [tails_b[l]])

          for blk in range(NB):
              slot = nxt("vg", 2)
              xi, xib = xin[slot], [xin_b[slot]]
              if is_sample:
                  mset("pool", xi[:], 0.0, xib)
                  tk.dma("sp", xi[0:64, :], xs, "xin%d" % slot, (), xib)
              else:
                  r0 = ti * 512 + blk * 128
                  tk.dma("sp", xi[:], xp[s, r0:r0 + 128, :], "xin%d" % slot, (), xib)
              for hf in range(2):
                  tk.op("dve", lambda e, hf=hf: e.bn_stats(out=st6[:, hf * 6:(hf + 1) * 6], in_=xi[:, hf * 512:(hf + 1) * 512]), xib, [mv_b])
              tk.op("dve", lambda e: e.bn_aggr(out=mv[:, 0:2], in_=st6[:, 0:12]), [mv_b], [mv_b])
              rsq(mv[:, 2:3], mv[:, 1:2], 0, [mv_b], [mv_b])
              stt("dve", mv[:, 3:4], mv[:, 0:1], -1.0, mv[:, 2:3], ALU.mult, ALU.mult, [mv_b], [mv_b])
              act(xi[:], xi[:], AF.Identity, xib + [mv_b], xib, bias=mv[:, 3:4], scale=mv[:, 2:3])
              for half in range(2):
                  ps = bigps()
                  for cc in range(4):
                      c = half * 4 + cc
                      mm(ps.t[:, cc * 128:(cc + 1) * 128], xi[:, c * 128:(c + 1) * 128], ident_f, True, True, xib, [ps.q[cc]], sig=(cc == 3))
                  for cc in range(4):
                      c = half * 4 + cc
                      act(xT[:, c, blk * 128:(blk + 1) * 128], ps.t[:, cc * 128:(cc + 1) * 128], AF.Identity, [ps.q[cc]], [xT_b[c]],
                          bias=pcol[:, 8 + c:9 + c], scale=pcol[:, c:c + 1])
          for c in range(8):
              cp("dve", xTb[:, c, 0:N], xT[:, c, 0:N], [xT_b[c]], [xTb_b[c]])

          chk(0, "p1")
          for l in range(NL):
              PC = 16 + l * LPC
              BC = l * LBC
              for pc in range(3):
                  wp, wpb = wnext()
                  for mi in range(4):
                      c = pc * 4 + mi
                      ps = bigps()
                      for k in range(8):
                          mm(ps.t[:, 0:N], wp[:, k * 512 + mi * 128:k * 512 + (mi + 1) * 128], xTb[:, k, 0:N],
                             k == 0, k == 7, wpb + [xTb_b[k]], ps.b(), sig=(k == 7))
                      cp("act", qkv_pre[:, c, 3:3 + N], ps.t[:, 0:N], ps.b(), [qp_b[c]])
                  if last_tile:
                      ps = bigps()
                      for k in range(8):
                          mm(ps.t[0:3, 0:512], xTb[:, k, n_real - 3:n_real], wp[:, k * 512:(k + 1) * 512],
                             k == 0, k == 7, wpb + [xTb_b[k]], ps.b(), sig=(k == 7))
                      t, tb = sFn()
                      cp("act", t[0:3, :], ps.t[0:3, 0:512], ps.b(), tb)
                      dst = ncs[l] if is_sample else ncp[l, s]
                      misc_store(dst[:, pc * 512:(pc + 1) * 512], t[0:3, :], tb)
              cp("pool", qkv_pre[:, :, 0:3], tails[:, l], [tails_b[l]], qp_b)
              cp("pool", tails[:, l], qkv_pre[:, :, N:N + 3], qp_b, [tails_b[l]])
              for c in range(12):
                  di = nxt("diag", 2)
                  for j in range(4):
                      act(diag[di][:, j, :], identb[:], AF.Identity, (), [diag_b[di]], scale=pcol[:, PC + 32 + c * 4 + j:PC + 33 + c * 4 + j])
                  ps = bigps()
                  for j in range(4):
                      mm(ps.t[:, 0:N], diag[di][:, j, :], qkv_pre[:, c, j:j + N], j == 0, j == 3, [diag_b[di], qp_b[c]], ps.b(), sig=(j == 3))
                  act(hid[:, c, 0:N], ps.t[:, 0:N], AF.Silu, ps.b(), [hid_b[c]])
              wp, wpb = wnext()
              for mi in range(4):
                  ps = bigps()
                  for k in range(8):
                      mm(ps.t[:, 0:N], wp[:, k * 512 + mi * 128:k * 512 + (mi + 1) * 128], xTb[:, k, 0:N],
                         k == 0, k == 7, wpb + [xTb_b[k]], ps.b(), sig=(k == 7))
                  act(hid[:, 12 + mi, 0:N], ps.t[:, 0:N], AF.Silu, ps.b(), [hid_b[12 + mi]])
              pq, pqb = smallq()
              for blk in range(NB):
                  for k in range(8):
                      mm(pq[:, blk * 8:(blk + 1) * 8], xTb[:, k, blk * 128:(blk + 1) * 128], wba[:, l, k * 8:(k + 1) * 8],
                         k == 0, k == 7, [xTb_b[k]], pqb, sig=(k == 7))
              NG = NB * 4
              pq3 = pq[:, 0:NB * 8].rearrange("p (b c) -> p b c", c=8)

              def g3(name):
                  return gt[name][:, 0:NG].rearrange("p (b c) -> p b c", c=4)
              gb = [gates_b]
              act(g3("tmpa"), pq3[:, :, 0:4], AF.Exp, pqb, gb, scale=-1.0)
              act(g3("sbn"), g3("tmpa"), AF.Ln, gb, gb, bias=epsc[:, 3:4])
              ts1("dve", gt["tmpa"][:, 0:NG], gt["tmpa"][:, 0:NG], 1.0, ALU.add, gb, gb)
              tk.op("dve", lambda e: e.reciprocal(out=gt["beta"][:, 0:NG], in_=gt["tmpa"][:, 0:NG]), gb, gb)
              tt("dve", g3("tmpb"), pq3[:, :, 4:8], pbc[:, BC + 1040:BC + 1040 + NG].rearrange("p (b c) -> p b c", c=4), ALU.add, pqb + gb, gb)
              act(gt["tmpb"][:, 0:NG], gt["tmpb"][:, 0:NG], AF.Exp, gb, gb)
              act(gt["tmpb"][:, 0:NG], gt["tmpb"][:, 0:NG], AF.Ln, gb, gb, bias=epsc[:, 3:4])
              tt("dve", gt["g"][:, 0:NG], gt["tmpb"][:, 0:NG], nexpA[:, l, 0:NG], ALU.mult, gb, gb)
              pg, pgb = smallq()
              mm(pg[:, 0:NG], Mu_f, gt["g"][:, 0:NG], True, True, gb, pgb, sig=False)
              mm(pg[:, 16:16 + NG], Bd_f, gt["g"][:, 0:NG], True, True, gb, pgb, sig=False)
              mm(pg[:, 32:32 + NG], E0_f, gt["g"][:, 0:NG], True, True, gb, pgb, sig=False)
              mm(pg[:, 48:48 + NG], E1_f, gt["g"][:, 0:NG], True, True, gb, pgb)
              cp("dve", gt["gc"][:, 0:NG], pg[:, 0:NG], pgb, gb)
              ts1("dve", gt["ngc"][:, 0:NG], gt["gc"][:, 0:NG], -1.0, ALU.mult, gb, gb)
              stt("dve", gt["gcb"][:, 0:NG], gt["sbn"][:, 0:NG], -1.0, gt["gc"][:, 0:NG], ALU.mult, ALU.add, gb, gb)
              act(gt["eg"][:, 0:NG], pg[:, 0:NG], AF.Exp, pgb, gb)
              tt("dve", gt["bEg"][:, 0:NG], gt["eg"][:, 0:NG], gt["beta"][:, 0:NG], ALU.mult, gb, gb)
              tt("dve", gt["tmpa"][:, 0:NG], pg[:, 16:16 + NG], gt["ngc"][:, 0:NG], ALU.add, pgb + gb, gb)
              act(gt["tmpa"][:, 0:NG], gt["tmpa"][:, 0:NG], AF.Exp, gb, gb)
              ts1("dve", gt["Ed0"][:, 0:NG], gt["tmpa"][:, 0:NG], E0_f[:, 0:1], ALU.mult, gb, gb)
              ts1("dve", gt["Ed1"][:, 0:NG], gt["tmpa"][:, 0:NG], E1_f[:, 0:1], ALU.mult, gb, gb)
              act(egl[:, 0:NG], pg[:, 32:32 + NG], AF.Exp, pgb, gb)
              act(egl[:, 16:16 + NG], pg[:, 48:48 + NG], AF.Exp, pgb, gb)
              wp, wpb = wnext()
              for mi in range(4):
                  ps = bigps()
                  for k in range(8):
                      mm(ps.t[:, 0:N], wp[:, k * 512 + mi * 128:k * 512 + (mi + 1) * 128], xTb[:, k, 0:N],
                         k == 0, k == 7, wpb + [xTb_b[k]], ps.b(), sig=(k == 7))
                  act(hid[:, 16 + mi, 0:N], ps.t[:, 0:N], AF.Gelu, ps.b(), [hid_b[16 + mi]])
              wp, wpb = wnext()
              for blk in range(NB):
                  ps = bigps()
                  for k in range(8):
                      mm(ps.t[:, 0:512], xTb[:, k, blk * 128:(blk + 1) * 128], wp[:, k * 512:(k + 1) * 512],
                         k == 0, k == 7, wpb + [xTb_b[k]], ps.b(), sig=(k == 7))
                  vi = nxt("vg", 2)
                  v_, vb_ = vg[vi], [vg_b[vi]]
                  act(v_[:], ps.t[:, 0:512], AF.Gelu, ps.b(), vb_)
                  tk.op("dve", lambda e: e.bn_stats(out=st6[:, 0:6], in_=v_[:]), vb_, [mv_b])
                  tk.op("dve", lambda e: e.bn_aggr(out=mv[:, 0:2], in_=st6[:, 0:6]), [mv_b], [mv_b])
                  rsq(mv[:, 2:3], mv[:, 1:2], 0, [mv_b], [mv_b])
                  stt("dve", mv[:, 3:4], mv[:, 0:1], -1.0, mv[:, 2:3], ALU.mult, ALU.mult, [mv_b], [mv_b])
                  act(v_[:], v_[:], AF.Identity, vb_ + [mv_b], vb_, bias=mv[:, 3:4], scale=mv[:, 2:3])
                  tt("dve", v_[:], v_[:], pbc[:, BC:BC + 512], ALU.mult, vb_, vb_)
                  if is_sample:
                      tt("dve", v_[:], v_[:], pbc[:, BC + 512:BC + 1024], ALU.add, vb_, vb_)
                      cp("pool", hid[:, 20 + blk, :], v_[:], vb_, [hid_b[20 + blk]])
                      misc_store(nvs[l], v_[0:64, :], vb_)
                  else:
                      tt("dve", hid[:, 20 + blk, :], v_[:], pbc[:, BC + 512:BC + 1024], ALU.add, vb_, [hid_b[20 + blk]])

              chk(l, "p2")
              for c in range(8):
                  sq, sqb = sBn()
                  act(sq[:, 0:N], hid[:, c, 0:N], AF.Square, [hid_b[c]], sqb)
                  ps = bigps()
                  if c < 4:
                      mm(ps.t[:, 0:N], ones128b[:], sq[:, 0:N], True, True, sqb, ps.b())
                      eps_ = 2
                  else:
                      mm(ps.t[:, 0:N], onesb[:], sq[:, 0:N], True, True, sqb, ps.b())
                      eps_ = 1
                  t, tb = sFn()
                  rsq(t[:, 0:N], ps.t[:, 0:N], eps_, ps.b(), tb)
                  tt("dve", hid[:, c, 0:N], hid[:, c, 0:N], t[:, 0:N], ALU.mult, [hid_b[c]] + tb, [hid_b[c]])

              chk(l, "g1")
              for blk in range(NB):
                  bs = slice(blk * 128, (blk + 1) * 128)
                  chk(l, "g5b%d" % blk)
                  for h in range(4):
                      H = hd[h]
                      idx = blk * 4 + h
                      qb_, kb_, vb2_ = [hid_b[h]], [hid_b[4 + h]], [hid_b[8 + h]]
                      di = nxt("dg", 2)
                      act(dgp[di][:, 0:128], identb[:], AF.Identity, gb, [dgp_b[di]], scale=gt["gc"][:, idx:idx + 1])
                      act(dgp[di][:, 128:256], identb[:], AF.Identity, gb, [dgp_b[di]], scale=gt["gcb"][:, idx:idx + 1])
                      ps = bigps()
                      mm(ps.t[:, 0:256], ones_f, dgp[di][:], True, True, [dgp_b[di]], ps.b(0, 256), sig=False)
                      mm(ps.t[:, 256:512], hid[:, 4 + h, bs], hid[:, 0:8, :].rearrange("p (a h) n -> p a h n", a=2)[:, :, h, bs], True, True, qb_ + kb_, ps.b(256, 512))
                      ti_ = nxt("tmp", 2)
                      stt("dve", tmpp[ti_][:], ps.t[:, 0:256], gt["ngc"][:, idx:idx + 1], maskneg2, ALU.add, ALU.add,
                          ps.b(0, 256) + gb, [tmpp_b[ti_]])
                      ei = nxt("E", 2)
                      act(Ep[ei][:], tmpp[ti_][:], AF.Exp, [tmpp_b[ti_]], [Ep_b[ei]])
                      act(H["EG"][:], ps.t[:, 0:128], AF.Exp, ps.b(0, 128), [H["EG_b"]])
                      yx0, yx0b = H["YX"][0], [H["YX_b"][0]]
                      tt("dve", H["attnT"][:], ps.t[:, 256:384], Ep[ei][:, 0:128], ALU.mult, ps.b(256, 384) + [Ep_b[ei]], [H["attnT_b"]])
                      tt("dve", yx0[:, 0:128], ps.t[:, 384:512], Ep[ei][:, 128:256], ALU.mult, ps.b(384, 512) + [Ep_b[ei]], yx0b)
                      stt("dve", H["P"][:], yx0[:, 0:128], -1.0, identb[:], ALU.mult, ALU.add, yx0b, [H["P_b"]])
                      tt("pool", H["qdT"][:], hid[:, h, bs], H["EG"][:], ALU.mult, qb_ + [H["EG_b"]], [H["qdT_b"]])
                  pt, ptb = pTbank()
                  for h in range(4):
                      H = hd[h]
                      mm(pt[:, h * 128:(h + 1) * 128], H["YX"][0][:, 0:128], identb[:], True, True, [H["YX_b"][0]], ptb, sig=(h == 3))
                  for h in range(4):
                      H = hd[h]
                      cp("act", H["YX"][0][:, 128:256], pt[:, h * 128:(h + 1) * 128], ptb, [H["YX_b"][0]])
                  pk, pkb = pTbank()
                  for h in range(4):
                      mm(pk[:, h * 128:(h + 1) * 128], hid[:, 4 + h, bs], identb[:], True, True, [hid_b[4 + h]], pkb, sig=(h == 3))
                  for h in range(4):
                      H = hd[h]
                      idx = blk * 4 + h
                      pks = pk[:, h * 128:(h + 1) * 128]
                      act(H["kbg"][:], pks, AF.Identity, pkb + gb, [H["kbg_b"]], scale=gt["bEg"][:, idx:idx + 1])
                      act(H["kd0"][:], pks, AF.Identity, pkb + gb, [H["kd0_b"]], scale=gt["Ed0"][:, idx:idx + 1])
                      act(H["kd1"][:], pks, AF.Identity, pkb + gb, [H["kd1_b"]], scale=gt["Ed1"][:, idx:idx + 1])
                  pv, pvb = pTbank()
                  for h in range(4):
                      mm(pv[:, h * 128:(h + 1) * 128], hid[:, 8 + h, bs], identb[:], True, True, [hid_b[8 + h]], pvb, sig=(h == 3))
                  for h in range(4):
                      H = hd[h]
                      idx = blk * 4 + h
                      ts1("dve", H["vb"][:], pv[:, h * 128:(h + 1) * 128], gt["beta"][:, idx:idx + 1], ALU.mult, pvb + gb, [H["vb_b"]])
                  chk(l, "g2b%d" % blk)
                  for k in range(6):
                      for h in range(4):
                          H = hd[h]
                          cur, curb = H["YX"][k % 2], [H["YX_b"][k % 2]]
                          nx_, nxb = H["YX"][(k + 1) % 2], [H["YX_b"][(k + 1) % 2]]
                          if k >= 1:
                              psA = bigps()
                              mm(psA.t[:, 0:128], cur[:, 128:256], H["P"][:], True, True, curb + [H["P_b"]], psA.b())
                          if k <= 4:
                              psB = bigps()
                              mm(psB.t[:, 0:128], cur[:, 128:256], cur[:, 0:128], True, True, curb, psB.b(), sig=False)
                              mm(psB.t[:, 128:256], cur[:, 0:128], cur[:, 128:256], True, True, curb, psB.b())
                              cp("act", nx_[:], psB.t[:, 0:256], psB.b(), nxb)
                          if k >= 1:
                              tt("dve", H["P"][:], H["P"][:], psA.t[:, 0:128], ALU.add, [H["P_b"]] + psA.b(), [H["P_b"]])
                      chk(l, "n%db%d" % (k, blk))
                  for h in range(4):
                      H = hd[h]
                      pq, pqb = smallq()
                      mm(pq, H["kbg"][:], H["P"][:], True, True, [H["kbg_b"], H["P_b"]], pqb)
                      act(H["nwT"][:], pq, AF.Identity, pqb, [H["nwT_b"]], scale=-1.0)
                  chk(l, "g3b%d" % blk)
                  for i in range(2):
                      cs = slice(i * 64, (i + 1) * 64)
                      pqs = []
                      for h in range(4):
                          H = hd[h]
                          pq, pqb = smallq()
                          mm(pq, H["P"][:], H["vb"][:], True, False, [H["P_b"], H["vb_b"]], pqb, sig=False)
                          mm(pq, H["nwT"][:], Sbf[:, l, h, :], False, True, [H["nwT_b"], Sbf_b[l][h]], pqb)
                          pqs.append((pq, pqb))
                      for h in range(4):
                          H = hd[h]
                          pq, pqb = pqs[h]
                          cp("act", H["vnew"][:], pq, pqb, [H["vnew_b"]])
                      pq2s = []
                      for h in range(4):
                          H = hd[h]
                          oc = slice(h * 128 + i * 64, h * 128 + (i + 1) * 64)
                          mm(psO.t[:, oc], Sbf[:, l, h, :], H["qdT"][:, cs], True, False, [Sbf_b[l][h], H["qdT_b"]], [psO.q[h]], sig=False)
                          mm(psO.t[:, oc], H["vnew"][:], H["attnT"][:, cs], False, True, [H["vnew_b"], H["attnT_b"]], [psO.q[h]], sig=False)
                          pq2, pq2b = smallq()
                          kd = H["kd0"] if i == 0 else H["kd1"]
                          kdb = H["kd0_b"] if i == 0 else H["kd1_b"]
                          mm(pq2, kd[:], H["vnew"][:], True, True, [kdb, H["vnew_b"]], pq2b)
                          pq2s.append((pq2, pq2b))
                      for h in range(4):
                          idx = blk * 4 + h
                          pq2, pq2b = pq2s[h]
                          stt("dve", S32[:, l, h, :], S32[:, l, h, :], egl[:, i * 16 + idx:i * 16 + idx + 1], pq2, ALU.mult, ALU.add,
                              [S32_b[l][h]] + pq2b + gb, [S32_b[l][h]])
                          cp("act", Sbf[:, l, h, :], S32[:, l, h, :], [S32_b[l][h]], [Sbf_b[l][h]])
                      if is_sample and i == 0:
                          misc_store(nds[l].rearrange("h d e -> d h e"), S32[:, l], S32_b[l])
                  chk(l, "g4b%d" % blk)
                  sq, sqb = sBn()
                  act(sq[:, 0:512], psO.t[:, 0:512], AF.Square, psO.b(), sqb)
                  ps = bigps()
                  mm(ps.t[:, 0:512], onesrmsb[:], sq[:, 0:512], True, True, sqb, ps.b())
                  t, tb = sFn()
                  rsq(t[:, 0:512], ps.t[:, 0:512], 1, ps.b(), tb)
                  tt("dve", t[:, 0:512], t[:, 0:512], psO.t[:, 0:512], ALU.mult, tb + psO.b(), tb)
                  stt("dve", qkv_pre[:, 0:4, bs], t[:, 0:512].rearrange("p (h c) -> p h c", h=4), pcol[:, PC + 80:PC + 81],
                      hid[:, 12:16, bs], ALU.mult, ALU.mult, tb + hid_b[12:16], qp_b[0:4])
              if last_tile and not is_sample:
                  misc_store(ndp[l, s].rearrange("h d e -> d h e"), S32[:, l], S32_b[l])

              chk(l, "gdn")
              for g in range(4):
                  ps = bigps()
                  for blk in range(NB):
                      bs = slice(blk * 128, (blk + 1) * 128)
                      mm(ps.t[:, bs], onesb[:], bsmat[:, l * 4 + g, :], True, False, (), ps.b(), sig=False)
                      mm(ps.t[:, bs], hid[:, 20 + blk, g * 128:(g + 1) * 128], WsT[:, l * 4 + g, :], False, True, [hid_b[20 + blk]], ps.b())
                  tt("dve", qkv_pre[:, 4 + g, 0:N], ps.t[:, 0:N], hid[:, 16 + g, 0:N], ALU.mult, ps.b() + [hid_b[16 + g]], [qp_b[4 + g]])

              chk(l, "mix")
              for m in range(8):
                  if m % 4 == 0:
                      wp, wpb = wnext()
                  mi = m % 4
                  ps = bigps()
                  for k in range(8):
                      mm(ps.t[:, 0:N], wp[:, k * 512 + mi * 128:k * 512 + (mi + 1) * 128], qkv_pre[:, k, 0:N],
                         k == 0, k == 7, wpb + [qp_b[k]], ps.b(), sig=(k == 7))
                  stt("dve", xT[:, m, 0:N], xT[:, m, 0:N], ALPHA, ps.t[:, 0:N], ALU.mult, ALU.add, [xT_b[m]] + ps.b(), [xT_b[m]])
              chk(l, "res1")
              ln_fm(N, PC + 0, PC + 8)
              chk(l, "ln1")

              for j in range(8):
                  wp, wpb = wnext()
                  for fi in range(4):
                      f = j * 4 + fi
                      ps = bigps()
                      for k in range(8):
                          mm(ps.t[:, 0:N], wp[:, k * 512 + fi * 128:k * 512 + (fi + 1) * 128], xTb[:, k, 0:N],
                             k == 0, k == 7, wpb + [xTb_b[k]], ps.b(), sig=(k == 7))
                      t, tb = sFn()
                      act(t[:, 0:N], ps.t[:, 0:N], AF.Relu, ps.b(), tb)
                      tt("dve", hid[:, f, 0:N], t[:, 0:N], t[:, 0:N], ALU.mult, tb, [hid_b[f]])
              for m in range(8):
                  wp, wpb = wnext()
                  ps = bigps()
                  for f in range(32):
                      mm(ps.t[:, 0:N], wp[:, f * 128:(f + 1) * 128], hid[:, f, 0:N], f == 0, f == 31, wpb + [hid_b[f]], ps.b(), sig=(f == 31))
                  stt("dve", xT[:, m, 0:N], xT[:, m, 0:N], ALPHA, ps.t[:, 0:N], ALU.mult, ALU.add, [xT_b[m]] + ps.b(), [xT_b[m]])
              chk(l, "res2")
              ln_fm(N, PC + 16, PC + 24)
              chk(l, "ln2")

          for blk in range(NB):
              slot = nxt("vg", 2)
              yo, yob = xin[slot], [xin_b[slot]]
              for half in range(2):
                  ps = bigps()
                  for cc in range(4):
                      c = half * 4 + cc
                      mm(ps.t[:, cc * 128:(cc + 1) * 128], xT[:, c, blk * 128:(blk + 1) * 128], ident_f, True, True, [xT_b[c]], [ps.q[cc]], sig=(cc == 3))
                  cp("act" if half else "dve", yo[:, half * 512:(half + 1) * 512], ps.t[:, 0:512], ps.b(), yob)
              if is_sample:
                  tk.dma("sp", ys, yo[0:64, :], "sty%d" % slot, yob, ())
              else:
                  r0 = ti * 512 + blk * 128
                  tk.dma("sp", yp[s, r0:r0 + 128, :], yo[:], "sty%d" % slot, yob, ())

    except _Stop:
        pass

    for key in ("sty0", "sty1", "misc"):
        tk.wait_all("sp", key)
    _LAST_CNT.clear()
    _LAST_CNT["sbuf_free"] = nc.sbuf_bytes_remaining
    _LAST_CNT.update(tk.cnt)
    return nc


def _consts():
    c = np.zeros((128, NCONST), np.float32)
    i = np.arange(128)
    P = i[:, None]
    Fr = i[None, :]
    same = (P // 64) == (Fr // 64)
    c[:, 0:128] = np.eye(128)
    c[:, 128:256] = 1.0
    c[:, 256:384] = (same & (P <= Fr))
    c[:, 384:512] = same
    c[:, 512:640] = (P < 64) * np.ones((1, 128))
    c[:, 640:768] = (P >= 64) * np.ones((1, 128))
    c[:, 768:896] = np.where(same & (P <= Fr), 0.0, NEG)
    c[:, 896:1024] = np.where(same & (P < Fr), 0.0, NEG)
    c[:, 1024:1152] = (P <= Fr)
    return c


def _pack_small(inp):
    f = lambda a: np.asarray(a, np.float32)
    col = lambda v: f(v).reshape(-1, 128).T
    pcol = np.zeros((128, 16 + NL * LPC), np.float32)
    pcol[:, 0:8] = col(inp["ln0_g"])
    pcol[:, 8:16] = col(inp["ln0_b"])
    pbc = np.zeros((128, NL * LBC), np.float32)
    for l in range(NL):
        b = 16 + l * LPC
        pcol[:, b:b + 8] = col(inp["ln1_g"][l])
        pcol[:, b + 8:b + 16] = col(inp["ln1_b"][l])
        pcol[:, b + 16:b + 24] = col(inp["ln2_g"][l])
        pcol[:, b + 24:b + 32] = col(inp["ln2_b"][l])
        cw = f(inp["conv_w"][l])
        pcol[:, b + 32:b + 80] = cw.reshape(4, 12, 128).transpose(2, 1, 0).reshape(128, 48)
        pcol[:, b + 80] = f(inp["gdn_norm_g"][l])
        o = l * LBC
        pbc[:, o:o + 512] = f(inp["smlp_ln_g"][l])[None, :]
        pbc[:, o + 512:o + 1024] = f(inp["smlp_ln_b"][l])[None, :]
        pbc[:, o + 1024:o + 1040] = np.tile(f(inp["a_log"][l]), 4)[None, :]
        pbc[:, o + 1040:o + 1056] = np.tile(f(inp["dt_bias"][l]), 4)[None, :]
    bsrow = f(inp["b_s"]).reshape(1, NL * 4 * 128)
    ws = f(inp["w_s"]).reshape(NL * 4, 128, 128)
    return pcol, pbc, bsrow, ws


SEQUENTIAL_LAUNCH = False
_NC_CACHE = {}
_NOCAST = [0]
_DELAY = [0]
_LAST_CNT = {}
_STOP = [None]


def run_cores(inp, TP, NSP, ncores):
    key = (TP, NSP, _STOP[0])
    if key not in _NC_CACHE:
        _NC_CACHE[key] = build(TP, NSP, _STOP[0])
    nc = _NC_CACHE[key]
    f = lambda a: np.ascontiguousarray(np.asarray(a, np.float32))
    pcol, pbc, bsrow, ws = _pack_small(inp)
    consts = _consts()
    shared = {"w_in": f(inp["w_in"]), "w_out": f(inp["w_out"]), "w_up": f(inp["w_up"]), "w_down": f(inp["w_down"]),
              "ws": ws, "consts": consts, "pcol": pcol, "pbc": pbc, "bsrow": bsrow}
    xp = f(inp["x_prompt"])
    xs = f(inp["x_sample"])
    cc = f(inp["cache_conv"])
    sd = f(inp["state_delta"])
    in_maps = []
    for i in range(ncores):
        m = dict(shared)
        m["xp"] = np.ascontiguousarray(xp[i * NSP:(i + 1) * NSP])
        m["xs"] = np.ascontiguousarray(xs[i])
        m["cconv"] = np.ascontiguousarray(cc[:, i])
        m["sdelta"] = np.ascontiguousarray(sd[:, i])
        in_maps.append(m)
    if SEQUENTIAL_LAUNCH and ncores > 1:
        R = []
        for i in range(ncores):
            R.append(run_bass_kernel_spmd(nc, [in_maps[i]], core_ids=[0]).results[0])
    else:
        R = run_bass_kernel_spmd(nc, in_maps, core_ids=list(range(ncores))).results
    y_p = np.concatenate([r["yp"] for r in R], axis=0)
    y_s = np.stack([r["ys"] for r in R], axis=0)
    ncp = np.concatenate([r["ncp"] for r in R], axis=1)
    ndp = np.concatenate([r["ndp"] for r in R], axis=1)
    ncs = np.stack([r["ncs"] for r in R], axis=1)
    nds = np.stack([r["nds"] for r in R], axis=1)
    nvs = np.stack([r["nvs"] for r in R], axis=1)
    return tuple(np.ascontiguousarray(a, dtype=np.float32) for a in (y_p, y_s, ncp, ndp, ncs, nds, nvs))


def kernel(**inputs):
    return run_cores(inputs, 4096, 2, 8)
```
